# Optimizing a Trainium2 kernel written in Bass

```python
import jax, jax.numpy as jnp
from jax import lax
import numpy as np

D_MODEL = 1024
BATCH = 8
SEQ = 2048
DEPTH = 4
DEC_BATCH = 128
DEC_SEQ = 1
PAST_LEN = 16384
PAGE_SIZE = 128

BRANCH_W = D_MODEL // 2
N_BRANCH = 3
CONV_W = 3
CONV_GROUPS = 8
CHUNK = 128
SGU_HEADS = 8
SGU_HEAD_DIM = BRANCH_W // SGU_HEADS
POOL_WINDOWS = (2, 4, 8, 16)
POOL_GROUPS = len(POOL_WINDOWS)
POOL_GROUP_DIM = BRANCH_W // POOL_GROUPS
POOL_BUF = max(POOL_WINDOWS) - 1
EPS = 1e-6
IN_SIZES = (BRANCH_W, BRANCH_W, BRANCH_W, BRANCH_W,
            BRANCH_W, BRANCH_W, BRANCH_W,
            BRANCH_W, BRANCH_W,
            D_MODEL, D_MODEL, D_MODEL)
IN_COLS = sum(IN_SIZES)
IN_SPLITS = [int(s) for s in np.cumsum(IN_SIZES)[:-1]]

kernel_name = 'hybrid_conv_sgu_pool_decoder_step'


def _rmsnorm(x, g):
    xf = x.astype(jnp.float32)
    y = xf * lax.rsqrt(jnp.mean(xf * xf, axis=-1, keepdims=True) + EPS)
    return (y * g.astype(jnp.float32)).astype(x.dtype)


def _layernorm(x, g, b):
    xf = x.astype(jnp.float32)
    mu = jnp.mean(xf, axis=-1, keepdims=True)
    var = jnp.mean(jnp.square(xf - mu), axis=-1, keepdims=True)
    y = (xf - mu) * lax.rsqrt(var + EPS)
    return (y * g.astype(jnp.float32) + b.astype(jnp.float32)).astype(x.dtype)


def _short_conv(z, prev, w, b):
    T = z.shape[1]
    full = jnp.concatenate([prev.astype(z.dtype), z], axis=1)
    out = b + sum(full[:, k:k + T] * w[k] for k in range(CONV_W))
    return out, full[:, -(CONV_W - 1):]


def _spatial_gate(u, v, ws, bs):
    N, T, Wd = v.shape
    L = CHUNK if T >= CHUNK else T
    nc = -(-T // L)
    vp = jnp.pad(v, ((0, 0), (0, nc * L - T), (0, 0)))
    vr = vp.reshape(N, nc, L, SGU_HEADS, SGU_HEAD_DIM)
    mask = jnp.tril(jnp.ones((L, L), dtype=bool))
    ws_m = jnp.where(mask, ws[:, :L, :L], jnp.zeros((), ws.dtype))
    mixed = jnp.einsum('hts,ncshd->ncthd', ws_m, vr) + jnp.transpose(bs[:, :L])[:, :, None]
    mixed = mixed.reshape(N, nc * L, Wd)[:, :T]
    return u * mixed


def _multiscale_pool(xc, prev, start):
    N, T, Wd = xc.shape
    full = jnp.concatenate([prev.astype(xc.dtype), xc], axis=1)
    fullf = full.astype(jnp.float32)
    cs = jnp.concatenate([jnp.zeros((N, 1, Wd), jnp.float32), jnp.cumsum(fullf, axis=1)], axis=1)
    pos = start + jnp.arange(T)
    outs = []
    for g, w in enumerate(POOL_WINDOWS):
        sl = slice(g * POOL_GROUP_DIM, (g + 1) * POOL_GROUP_DIM)
        s = cs[:, POOL_BUF + 1:, sl] - cs[:, POOL_BUF + 1 - w:POOL_BUF + 1 - w + T, sl]
        cnt = jnp.minimum(w, pos + 1).astype(jnp.float32)[None, :, None]
        outs.append(s / cnt)
    pooled = jnp.concatenate(outs, axis=-1)
    mixed = (pooled - xc.astype(jnp.float32)).astype(xc.dtype)
    return mixed, full[:, -POOL_BUF:]


def _layer(x, c, start, conv_prev, pool_prev, w_ada, b_ada, norm_g, w_in, conv_w, conv_b,
           lnv_g, lnv_b, sgu_w, sgu_b, pool_w, pool_b, pool_scale, w_branch, w_out):
    N, T, _ = x.shape
    mod = jax.nn.silu(c) @ w_ada + b_ada
    shift, scale, gate = jnp.split(mod[:, None, :], 3, axis=-1)
    h = _rmsnorm(x, norm_g) * (1 + scale) + shift
    p = h @ w_in
    a_b, a_c, a_h, a_g, b_u, b_v, b_g, c_x, c_g, m_a, m_b, m_c = jnp.split(p, IN_SPLITS, axis=-1)
    conv_out, conv_new = _short_conv(a_c * a_h, conv_prev, conv_w, conv_b)
    y_a = a_b * conv_out * jax.nn.silu(a_g)
    u = jax.nn.gelu(b_u)
    v = _layernorm(jax.nn.gelu(b_v), lnv_g, lnv_b)
    y_b = _spatial_gate(u, v, sgu_w, sgu_b) * jax.nn.silu(b_g)
    v_new = v[:, ((T - 1) // CHUNK) * CHUNK:]
    pm, pool_new = _multiscale_pool(c_x, pool_prev, start)
    y_c = jnp.einsum('ntgi,gio->ntgo', pm.reshape(N, T, POOL_GROUPS, POOL_GROUP_DIM), pool_w)
    y_c = (y_c.reshape(N, T, BRANCH_W) + pool_b) * pool_scale * jax.nn.silu(c_g)
    merged = (jax.nn.sigmoid(m_a) * (y_a @ w_branch[0])
              + jax.nn.sigmoid(m_b) * (y_b @ w_branch[1])
              + jax.nn.sigmoid(m_c) * (y_c @ w_branch[2]))
    x = x + gate * (merged @ w_out)
    return x, conv_new, pool_new, v_new


def setup_inputs(seed: int = 0) -> dict:
    key = jax.random.key(seed)
    ks = jax.random.split(key, 24)
    nrm = lambda k, s, sc: jax.random.normal(k, s, jnp.float32) * sc
    D, W = D_MODEL, BRANCH_W
    return {
        'x_prompt': nrm(ks[0], (BATCH, SEQ, D), 1.0),
        'x_sample': nrm(ks[1], (DEC_BATCH, DEC_SEQ, D), 1.0),
        'c_prompt': nrm(ks[2], (BATCH, D), 1.0),
        'c_sample': nrm(ks[3], (DEC_BATCH, D), 1.0),
        'state_conv': nrm(ks[4], (DEPTH, DEC_BATCH, CONV_W - 1, W), 1.0),
        'state_pool': nrm(ks[5], (DEPTH, DEC_BATCH, POOL_BUF, W), 1.0),
        'w_ada': nrm(ks[6], (DEPTH, D, 3 * D), 0.5 * D ** -0.5),
        'b_ada': nrm(ks[7], (DEPTH, 3 * D), 0.02),
        'norm_g': 1.0 + nrm(ks[8], (DEPTH, D), 0.02),
        'w_in': nrm(ks[9], (DEPTH, D, IN_COLS), D ** -0.5),
        'conv_w': nrm(ks[10], (DEPTH, CONV_W, W), CONV_W ** -0.5),
        'conv_b': nrm(ks[11], (DEPTH, W), 0.02),
        'lnv_g': 1.0 + nrm(ks[12], (DEPTH, W), 0.02),
        'lnv_b': nrm(ks[13], (DEPTH, W), 0.02),
        'sgu_w': nrm(ks[14], (DEPTH, SGU_HEADS, CHUNK, CHUNK), CHUNK ** -0.5),
        'sgu_b': 1.0 + nrm(ks[15], (DEPTH, SGU_HEADS, CHUNK), 0.02),
        'pool_w': nrm(ks[16], (DEPTH, POOL_GROUPS, POOL_GROUP_DIM, POOL_GROUP_DIM), POOL_GROUP_DIM ** -0.5),
        'pool_b': nrm(ks[17], (DEPTH, W), 0.02),
        'pool_scale': 1.0 + nrm(ks[18], (DEPTH, W), 0.1),
        'w_branch': nrm(ks[19], (DEPTH, N_BRANCH, W, D), W ** -0.5),
        'w_out': nrm(ks[20], (DEPTH, D, D), D ** -0.5),
        'final_g': 1.0 + nrm(ks[21], (D,), 0.02),
    }


def reference(x_prompt, x_sample, c_prompt, c_sample, state_conv, state_pool, w_ada, b_ada, norm_g,
              w_in, conv_w, conv_b, lnv_g, lnv_b, sgu_w, sgu_b, pool_w, pool_b, pool_scale,
              w_branch, w_out, final_g):
    xp, xs = x_prompt, x_sample
    conv_p, conv_s, pool_p, pool_s, v_p, v_s = [], [], [], [], [], []
    zero_conv = jnp.zeros((xp.shape[0], CONV_W - 1, BRANCH_W), xp.dtype)
    zero_pool = jnp.zeros((xp.shape[0], POOL_BUF, BRANCH_W), xp.dtype)
    for l in range(DEPTH):
        params = (w_ada[l], b_ada[l], norm_g[l], w_in[l], conv_w[l], conv_b[l], lnv_g[l], lnv_b[l],
                  sgu_w[l], sgu_b[l], pool_w[l], pool_b[l], pool_scale[l], w_branch[l], w_out[l])
        xp, cp, pp, vp = _layer(xp, c_prompt, 0, zero_conv, zero_pool, *params)
        xs, cs_, ps, vs = _layer(xs, c_sample, PAST_LEN, state_conv[l], state_pool[l], *params)
        conv_p.append(cp); conv_s.append(cs_); pool_p.append(pp); pool_s.append(ps)
        v_p.append(vp); v_s.append(vs)
    y_prompt = _rmsnorm(xp, final_g)
    y_sample = _rmsnorm(xs, final_g)
    return (y_prompt, y_sample, jnp.stack(conv_p), jnp.stack(conv_s), jnp.stack(pool_p),
            jnp.stack(pool_s), jnp.stack(v_p), jnp.stack(v_s))
```

```python
import os
import numpy as np
from contextlib import ExitStack
import concourse.bass as bass
import concourse.mybir as mybir
from concourse.bass_utils import run_bass_kernel_spmd

F32 = mybir.dt.float32
BF16 = mybir.dt.bfloat16
AF = mybir.ActivationFunctionType
ALU = mybir.AluOpType
AX = mybir.AxisListType

D = 1024
W = 512
L = 4
KC = 8
NCORE = 8
SEQ = 2048
NS = 16
SUPTOK = 1024
NTS = SUPTOK + NS
EPS = 1e-6
NSLOT = 3
NBANK = 8
NTMP = 6
POOL_WIN = (2, 4, 8, 16)
NEWTON = int(os.environ.get("MK_NEWTON", 2))
RMS_NEWTON = int(os.environ.get("MK_RMS_NEWTON", 2))
OFFLOAD = os.environ.get("MK_POOL", "").split(",")

O_BA = 0
O_NG = 96
O_FG = 128
O_CW = 136
O_CB = 184
O_PB = 200
O_PS = 216
O_SA = 232
O_SB = 248
O_EPS = 264
NCOLS = 272

T_A = 0
T_BV = 4
T_BUG = 5
T_CX = 7
T_CG = 8
T_MRG = 9
T_WO = 18
NT_MAIN = 20
NT_ADA = 6


class Buf:
    __slots__ = ("name", "w", "r", "sem", "total")

    def __init__(self, name):
        self.name = name
        self.w = {}
        self.r = {}
        self.sem = None
        self.total = 0


class Eng:
    def __init__(self, name, sem):
        self.name = name
        self.sem = sem
        self.count = 0
        self.waited = {}
        self.prog = []


class Tracker:
    def __init__(self, nc, es):
        self.nc = nc
        self.es = es
        self.sems = {}
        self.eng = {}
        for n in ("pe", "act", "dve", "pool", "sp"):
            key = "s_" + n
            self.sems[key] = es.enter_context(nc.semaphore(key))
            self.eng[n] = Eng(n, key)
        self.store_bufs = []

    def _need(self, reads, writes):
        need = {}
        for b in reads:
            for k, v in b.w.items():
                if need.get(k, 0) < v:
                    need[k] = v
        for b in writes:
            for k, v in b.w.items():
                if need.get(k, 0) < v:
                    need[k] = v
            for k, v in b.r.items():
                if need.get(k, 0) < v:
                    need[k] = v
        return need

    def _waits(self, E, need, skip=None):
        for key, val in need.items():
            if key == skip:
                continue
            if E.name == "pe" and key == E.sem:
                continue
            if E.waited.get(key, 0) < val:
                E.waited[key] = val
                h = self.sems[key]
                E.prog.append(("wait_ge", dict(sem=h, val=val), None))

    def op(self, eng, name, kw=None, reads=(), writes=(), inc=True):
        E = self.eng[eng]
        self._waits(E, self._need(reads, writes))
        if inc:
            E.count += 1
            val = E.count
            h = self.sems[E.sem]
        else:
            val = E.count + 1
            h = None
        E.prog.append((name, kw, h))
        for b in reads:
            if b.r.get(E.sem, 0) < val:
                b.r[E.sem] = val
        for b in writes:
            b.w = {E.sem: val}
            b.r = {}

    def dma(self, q, out_ap, in_ap, buf, reads=(), writes=(), store=False, multi=False):
        E = self.eng[q]
        if buf.sem is None:
            key = "d_" + buf.name
            self.sems[key] = self.es.enter_context(self.nc.semaphore(key))
            buf.sem = key
        key = buf.sem
        self._waits(E, self._need(reads, writes), skip=key if multi else None)
        buf.total += 16
        val = buf.total
        h = self.sems[key]
        E.prog.append(("dma_start", dict(out=out_ap, in_=in_ap), (h, 16)))
        for b in reads:
            if b.r.get(key, 0) < val:
                b.r[key] = val
        for b in writes:
            b.w = {key: val}
            b.r = {}
        if store and buf not in self.store_bufs:
            self.store_bufs.append(buf)

    def final_wait(self, q):
        E = self.eng[q]
        for b in self.store_bufs:
            h = self.sems[b.sem]
            E.prog.append(("wait_ge", dict(sem=h, val=b.total), None))


def replay(e, prog):
    for name, kw, h in prog:
        if name == "wait_ge":
            e.wait_ge(kw["sem"], kw["val"])
            continue
        if callable(name):
            ins = name(e)
        else:
            ins = getattr(e, name)(**kw)
        if h is not None:
            if isinstance(h, tuple):
                ins.then_inc(h[0], h[1])
            else:
                ins.then_inc(h, 1)


class Ring:
    def __init__(self, items):
        self.items = items
        self.i = 0

    def next(self):
        it = self.items[self.i % len(self.items)]
        self.i += 1
        return it


class Prog:
    def __init__(self, nlayers=L):
        self.nl = nlayers
        self.nc = bass.Bass("TRN2", target_bir_lowering=False)

    def dram_in(self, name, shape):
        return self.nc.dram_tensor(name, list(shape), F32, kind="ExternalInput").ap()

    def dram_out(self, name, shape):
        return self.nc.dram_tensor(name, list(shape), F32, kind="ExternalOutput").ap()

    def sb(self, name, shape, dt=F32):
        return self.es.enter_context(self.nc.sbuf_tensor("sb_" + name, list(shape), dt))

    def build(self):
        nc = self.nc
        d = {}
        d["xp"] = self.dram_in("xp", [SEQ, D])
        d["xs"] = self.dram_in("xs", [NS, D])
        d["cc"] = self.dram_in("cc", [NS + 1, D])
        d["sconv"] = self.dram_in("sconv", [L, NS, 2, W])
        d["spool"] = self.dram_in("spool", [L, NS, 15, W])
        d["wa"] = self.dram_in("wa", [L, NT_ADA, 128, 4096])
        d["wm"] = self.dram_in("wm", [L, NT_MAIN, 128, 4096])
        d["cols"] = self.dram_in("cols", [128, NCOLS])
        d["lnv"] = self.dram_in("lnv", [L, 2, W])
        d["sguw"] = self.dram_in("sguw", [L, 8, 128, 128])
        d["biasT"] = self.dram_in("biasT", [128, L, 4, 128])
        d["poolw"] = self.dram_in("poolw", [128, L * 4, 128])
        d["ident"] = self.dram_in("ident", [128, 128])
        d["mask"] = self.dram_in("mask", [128, 128])
        d["pt"] = self.dram_in("pt", [128, 16, 128])
        d["yp"] = self.dram_out("yp", [SEQ, D])
        d["ys"] = self.dram_out("ys", [NS, D])
        d["convp"] = self.dram_out("convp", [L, 2, W])
        d["convs"] = self.dram_out("convs", [L, NS, 2, W])
        d["poolp"] = self.dram_out("poolp", [L, 15, W])
        d["pools"] = self.dram_out("pools", [L, NS, 15, W])
        d["vp"] = self.dram_out("vp", [L, 128, W])
        d["vs"] = self.dram_out("vs", [L, NS, W])
        self.d = d
        with ExitStack() as es:
            self.es = es
            self.T = Tracker(nc, es)
            self.alloc()
            self.emit()
            block = es.enter_context(nc.Block())

            @block.tensor
            def _(e):
                replay(e, self.T.eng["pe"].prog)

            @block.scalar
            def _(e):
                replay(e, self.T.eng["act"].prog)

            @block.vector
            def _(e):
                replay(e, self.T.eng["dve"].prog)

            @block.gpsimd
            def _(e):
                replay(e, self.T.eng["pool"].prog)

            @block.sync
            def _(e):
                replay(e, self.T.eng["sp"].prog)
        return nc

    def alloc(self):
        sb = self.sb
        self.colsT = sb("colsT", [128, NCOLS]); self.b_cols = Buf("cols")
        self.ident = sb("ident", [128, 128]); self.b_ident = Buf("ident")
        self.mask = sb("mask", [128, 128]); self.b_mask = Buf("mask")
        self.ones = sb("ones", [128, 128], BF16); self.b_ones = Buf("ones")
        self.ptb = sb("ptb", [128, 16, 128], BF16); self.b_ptb = Buf("ptb")
        self.pwb = sb("pwb", [128, L * 4, 128], BF16); self.b_pwb = Buf("pwb")
        self.wsT = sb("wsT", [128, 2, 8, 128], BF16); self.b_wsT = [Buf("wsT0"), Buf("wsT1")]
        self.biasT = sb("biasT", [128, 2, 4, 128]); self.b_biasT = [Buf("biasT0"), Buf("biasT1")]
        self.lnv = sb("lnv", [128, 2, 2, W]); self.b_lnv = [Buf("lnv0"), Buf("lnv1")]
        self.modT = sb("modT", [128, L, 24, NS + 1]); self.b_modT = [Buf("modT%d" % l) for l in range(L)]
        self.gsP = sb("gsP", [128, L, 8]); self.b_gsP = [Buf("gsP%d" % l) for l in range(L)]
        self.gsS = sb("gsS", [128, L, 8, NS]); self.b_gsS = [Buf("gsS%d" % l) for l in range(L)]
        self.scb = sb("scb", [128, 8, NS + 1], BF16); self.b_scb = Buf("scb")
        self.zcarry = sb("zcarry", [128, L, 4, 2]); self.b_zcarry = [[Buf("zc%d_%d" % (l, m)) for m in range(4)] for l in range(L)]
        self.cxcarry = sb("cxcarry", [128, L, W], BF16); self.b_cxcarry = [Buf("cxc%d" % l) for l in range(L)]
        self.CP = sb("CP", [128, L, 4, NS]); self.b_CP = [Buf("CP%d" % l) for l in range(L)]
        self.SpT = sb("SpT", [128, L, 4, NS]); self.b_SpT = [Buf("SpT%d" % l) for l in range(L)]
        self.ztail = sb("ztail", [128, 4, 2]); self.b_ztail = Buf("ztail")
        self.zsall = sb("zsall", [128, 4, NS]); self.b_zsall = Buf("zsall")
        small = sb("small", [128, 4, 16]); self.small_ring = Ring([(small[:, i, :], Buf("small%d" % i)) for i in range(4)])
        rstd = [sb("rstd%d" % i, [128, W]) for i in range(2)]
        self.rstd = Ring([(rstd[i], Buf("rstd%d" % i)) for i in range(2)])
        self.xT = sb("xT", [128, KC, NTS]); self.b_xT = [[Buf("xT%d_%d" % (k, t)) for t in range(3)] for k in range(KC)]
        self.hT = sb("hT", [128, KC, NTS], BF16); self.b_hT = [[Buf("hT%d_%d" % (k, t)) for t in range(3)] for k in range(KC)]
        self.yb = [sb("y%s" % n, [128, 4, NTS], BF16) for n in "ABC"]
        self.b_yb = [[[Buf("y%s%d_%d" % (n, m, t)) for t in range(3)] for m in range(4)] for n in "ABC"]
        self.mg = sb("mg", [128, KC, NTS], BF16); self.b_mg = [[Buf("mg%d_%d" % (k, t)) for t in range(3)] for k in range(KC)]
        self.slots = [sb("slot%d" % i, [128, 4096], BF16) for i in range(NSLOT)]
        self.b_slots = [Buf("slot%d" % i) for i in range(NSLOT)]
        self.zbuf = sb("zbuf", [128, 2 + SUPTOK]); self.b_zbuf = [Buf("zb_c"), Buf("zb_0"), Buf("zb_1")]
        self.vb = sb("vb", [128, 9, W], BF16); self.b_vb = [Buf("vb%d" % i) for i in range(9)]
        self.cxb = sb("cxb", [128, 8, W], BF16); self.b_cxb = [Buf("cxb%d" % i) for i in range(8)]
        tmps = [sb("tmp%d" % i, [128, W]) for i in range(NTMP)]
        self.tmp = Ring([(tmps[i], Buf("tmp%d" % i)) for i in range(NTMP)])
        tbs = [sb("tbf%d" % i, [128, W], BF16) for i in range(3)]
        self.tbf = Ring([(tbs[i], Buf("tbf%d" % i)) for i in range(3)])
        self.xstage = sb("xstage", [128, D]); self.b_xstage = Buf("xstage")
        self.st32 = sb("st32", [32, 2048]); self.b_st32 = Buf("st32")
        self.pmS = sb("pmS", [128, 4, NS], BF16); self.b_pmS = Buf("pmS")
        self.mixS = sb("mixS", [128, 4, NS]); self.b_mixS = Buf("mixS")
        self.b_d2d = Buf("d2d")
        banks = [self.es.enter_context(self.nc.psum_tensor("bk%d" % i, [128, 512], F32)) for i in range(NBANK)]
        self.banks = Ring([(banks[i], Buf("bk%d" % i)) for i in range(NBANK)])
        if os.environ.get("MK_VERBOSE"):
            print("SBUF bytes remaining/partition:", self.nc.sbuf_bytes_remaining)

    def col(self, off):
        return self.colsT[:, off:off + 1]

    def rsqrt_dve(self, x, xb, y, yb, t, tb, iters=None):
        T = self.T
        I32 = mybir.dt.int32
        if iters is None:
            iters = NEWTON
        T.op("dve", "tensor_single_scalar", dict(out=t.bitcast(I32), in_=x.bitcast(I32), scalar=1, op=ALU.arith_shift_right), reads=[xb], writes=[tb])
        T.op("dve", "tensor_scalar", dict(out=y.bitcast(I32), in0=t.bitcast(I32), scalar1=-1, scalar2=0x5f3759df, op0=ALU.mult, op1=ALU.add),
             reads=[tb], writes=[yb])
        for _ in range(iters):
            T.op("dve", "tensor_tensor", dict(out=t, in0=y, in1=y, op=ALU.mult), reads=[yb], writes=[tb])
            T.op("dve", "scalar_tensor_tensor", dict(out=t, in0=t, scalar=-0.5, in1=x, op0=ALU.mult, op1=ALU.mult), reads=[tb, xb], writes=[tb])
            T.op("dve", "scalar_tensor_tensor", dict(out=y, in0=t, scalar=1.5, in1=y, op0=ALU.add, op1=ALU.mult), reads=[tb, yb], writes=[yb])

    def silu2(self, bank, bank_buf, wd):
        T = self.T
        sg, sgb = self.tmp.next()
        T.op("act", "activation", dict(out=sg[:, 0:wd], in_=bank[:, 0:wd], func=AF.Tanh, scale=0.5), reads=[bank_buf], writes=[sgb])
        T.op("dve", "scalar_tensor_tensor", dict(out=sg[:, 0:wd], in0=sg[:, 0:wd], scalar=1.0, in1=bank[:, 0:wd], op0=ALU.add, op1=ALU.mult),
             reads=[sgb, bank_buf], writes=[sgb])
        return sg, sgb

    def mm_group(self, bank_buf, mms, reads, inc=True):
        def fn(e, mms=mms):
            ins = None
            for (o, lt, r, st, sp, tp) in mms:
                if tp is None:
                    ins = e.matmul(o, lt, r, start=st, stop=sp)
                else:
                    ins = e.matmul(o, lt, r, start=st, stop=sp, tile_position=tp)
            return ins
        self.T.op("pe", fn, None, reads=reads, writes=[bank_buf], inc=inc)

    def tr_group(self, bank_buf, trs, reads):
        def fn(e, trs=trs):
            ins = None
            for (o, i, idn) in trs:
                ins = e.transpose(o, i, idn)
            return ins
        self.T.op("pe", fn, None, reads=list(reads) + [self.b_ident], writes=[bank_buf])

    ADA_AFTER = (0, 1, 2, 3, 4, 6)

    def plan_stream(self):
        order = []
        for sup in range(2):
            for l in range(self.nl):
                if sup == 0 and l == 0:
                    for t in range(NT_ADA):
                        order.append((sup, l, "a", t))
                ta = 0
                for t in range(NT_MAIN):
                    order.append((sup, l, "m", t))
                    if sup == 0 and l + 1 < self.nl and t in self.ADA_AFTER:
                        order.append((0, l + 1, "a", ta))
                        ta += 1
        self.fills = []
        self.fill_pos = {}
        for k, (sup, l, kind, t) in enumerate(order):
            self.fill_pos[(sup, l, kind, t)] = k
            self.fills.append(self.d["wa"][l, t, :, :] if kind == "a" else self.d["wm"][l, t, :, :])
        self.fill_emitted = 0
        self.pending_mod = []

    def use_tile(self, sup, l, kind, t):
        k = self.fill_pos[(sup, l, kind, t)]
        while self.fill_emitted < min(len(self.fills), k + NSLOT - 1):
            j = self.fill_emitted
            s = j % NSLOT
            self.T.dma("pool", self.slots[s][:, :], self.fills[j], buf=self.b_slots[s], writes=[self.b_slots[s]])
            self.fill_emitted += 1
        s = k % NSLOT
        return self.slots[s], self.b_slots[s]

    def emit(self):
        T = self.T
        d = self.d
        self.plan_stream()
        stop = int(os.environ.get("MK_STOP", 10 ** 9))

        class _Stop(Exception):
            pass

        def stage(n):
            if n + 100 * self.cur_sup >= stop:
                raise _Stop()
        self.cur_sup = 0
        try:
            self.startup()
            stage(1)
            for sup in range(2):
                self.cur_sup = sup
                self.tiles = [(0, 512, False), (512, 512, False)] + ([(1024, NS, True)] if sup == 1 else [])
                self.load_x(sup)
                stage(2)
                for l in range(self.nl):
                    self.layer_params(sup, l)
                    stage(3)
                    if sup == 0 and l == 0:
                        self.mod(l)
                    if sup == 0 and l + 1 < self.nl:
                        self.pending_mod = [(l + 1, t) for t in range(NT_ADA)]
                    stage(4)
                    self.rmsnorm(sup, l)
                    stage(5)
                    self.phase_a(sup, l)
                    stage(6)
                    self.phase_b(sup, l)
                    self.phase_b_end(sup)
                    stage(7)
                    self.phase_c(sup, l)
                    stage(8)
                    self.merge(sup, l)
                    stage(9)
                    self.wout(sup, l)
                    stage(10)
                self.final(sup)
                stage(11)
        except _Stop:
            for sb_ in self.b_slots:
                if sb_.sem is not None:
                    T.eng["sp"].prog.append(("wait_ge", dict(sem=T.sems[sb_.sem], val=sb_.total), None))
        T.final_wait("sp")

    def startup(self):
        T = self.T
        d = self.d
        T.dma("sp", self.colsT[:, :], d["cols"][:, :], buf=self.b_cols, writes=[self.b_cols])
        T.dma("sp", self.ident[:, :], d["ident"][:, :], buf=self.b_ident, writes=[self.b_ident])
        T.dma("sp", self.mask[:, :], d["mask"][:, :], buf=self.b_mask, writes=[self.b_mask])
        T.dma("pool", self.ptb[:, :, :], d["pt"][:, :, :], buf=self.b_ptb, writes=[self.b_ptb])
        T.dma("pool", self.pwb[:, :, :], d["poolw"][:, :, :], buf=self.b_pwb, writes=[self.b_pwb])
        T.op("dve", "memset", dict(ap=self.ones[:, :], constant=1.0), writes=[self.b_ones])
        st = self.st32
        T.dma("sp", st[0:NS + 1, 0:D], d["cc"][:, :], buf=self.b_st32, writes=[self.b_st32])
        T.op("act", "activation", dict(out=st[0:NS + 1, 0:D], in_=st[0:NS + 1, 0:D], func=AF.Silu),
             reads=[self.b_st32], writes=[self.b_st32])
        bk, bb = self.banks.next()
        n1 = NS + 1
        self.tr_group(bb, [(bk[:, kc * n1:(kc + 1) * n1], st[0:n1, kc * 128:(kc + 1) * 128], self.ident[0:n1, 0:n1])
                           for kc in range(KC)], [self.b_st32])
        T.op("dve", "tensor_copy", dict(out=self.scb[:, :, :], in_=bk[:, 0:KC * n1].rearrange("p (a b) -> p a b", a=KC)),
             reads=[bb], writes=[self.b_scb])
        for l in range(self.nl):
            T.dma("sp", d["convs"][l, :, 0, :], d["sconv"][l, :, 1, :], buf=self.b_d2d, store=True)
            T.dma("sp", d["pools"][l, :, 0:14, :], d["spool"][l, :, 1:15, :], buf=self.b_d2d, store=True)
        for l in range(self.nl):
            T.dma("sp", st[0:NS, 0:2 * W], d["sconv"][l].rearrange("n r c -> n (r c)"), buf=self.b_st32, writes=[self.b_st32])
            bk, bb = self.banks.next()
            self.tr_group(bb, [(bk[:, (r * 4 + m) * NS:(r * 4 + m + 1) * NS], st[0:NS, r * W + m * 128:r * W + (m + 1) * 128],
                                self.ident[0:NS, 0:NS]) for r in range(2) for m in range(4)], [self.b_st32])
            for m in range(4):
                w0 = self.col(O_CW + (l * 3 + 0) * 4 + m)
                w1 = self.col(O_CW + (l * 3 + 1) * 4 + m)
                cb = self.col(O_CB + l * 4 + m)
                T.op("dve", "tensor_scalar", dict(
                    out=self.CP[:, l, m, :], in0=bk[:, m * NS:(m + 1) * NS], scalar1=w0, scalar2=cb, op0=ALU.mult, op1=ALU.add),
                    reads=[bb, self.b_cols], writes=[self.b_CP[l]])
                T.op("dve", "scalar_tensor_tensor", dict(
                    out=self.CP[:, l, m, :], in0=bk[:, (4 + m) * NS:(5 + m) * NS], scalar=w1, in1=self.CP[:, l, m, :],
                    op0=ALU.mult, op1=ALU.add), reads=[bb, self.b_cols, self.b_CP[l]], writes=[self.b_CP[l]])
        for l in range(self.nl):
            sp_t, sp_b = self.tmp.next()
            for g, w in enumerate(POOL_WIN):
                nr = w - 1
                T.dma("sp", st[0:NS, 0:nr * 128].rearrange("p (r c) -> p r c", r=nr),
                      d["spool"][l, :, 15 - nr:15, g * 128:(g + 1) * 128], buf=self.b_st32, writes=[self.b_st32])
                T.op("dve", "tensor_reduce", dict(
                    out=sp_t[0:NS, g * 128:(g + 1) * 128], in_=st[0:NS, 0:nr * 128].rearrange("p (r c) -> p c r", r=nr),
                    axis=AX.X, op=ALU.add), reads=[self.b_st32], writes=[sp_b])
            bk, bb = self.banks.next()
            self.tr_group(bb, [(bk[:, g * NS:(g + 1) * NS], sp_t[0:NS, g * 128:(g + 1) * 128], self.ident[0:NS, 0:NS])
                               for g in range(4)], [sp_b])
            for g, w in enumerate(POOL_WIN):
                T.op("dve", "tensor_scalar", dict(
                    out=self.SpT[:, l, g, :], in0=bk[:, g * NS:(g + 1) * NS], scalar1=1.0 / w, scalar2=None, op0=ALU.mult),
                    reads=[bb], writes=[self.b_SpT[l]])
        T.op("dve", "memset", dict(ap=self.zcarry[:, :, :, :], constant=0.0), writes=[b for r in self.b_zcarry for b in r])

    def load_x(self, sup):
        T = self.T
        d = self.d
        for blk in range(8):
            r0 = (sup * 8 + blk) * 128
            ti = blk // 4
            for half in range(2):
                xs_, xsb_ = self.tmp.next()
                T.dma("sp", xs_[:, :], d["xp"][r0:r0 + 128, half * 512:(half + 1) * 512], buf=xsb_, writes=[xsb_])
                bk, bb = self.banks.next()
                self.tr_group(bb, [(bk[:, i * 128:(i + 1) * 128], xs_[:, i * 128:(i + 1) * 128], self.ident[:, :]) for i in range(4)], [xsb_])
                out = self.xT[:, half * 4:(half + 1) * 4, blk * 128:(blk + 1) * 128]
                src = bk[:, :].rearrange("p (a b) -> p a b", a=4)
                wr = [self.b_xT[half * 4 + i][ti] for i in range(4)]
                if half == 0:
                    T.op("act", "activation", dict(out=out, in_=src, func=AF.Copy), reads=[bb], writes=wr)
                else:
                    T.op("dve", "tensor_copy", dict(out=out, in_=src), reads=[bb], writes=wr)
        if sup == 1:
            st = self.st32
            T.dma("sp", st[0:NS, 0:D], d["xs"][:, :], buf=self.b_st32, writes=[self.b_st32])
            bk, bb = self.banks.next()
            self.tr_group(bb, [(bk[:, kc * NS:(kc + 1) * NS], st[0:NS, kc * 128:(kc + 1) * 128], self.ident[0:NS, 0:NS])
                               for kc in range(KC)], [self.b_st32])
            T.op("dve", "tensor_copy", dict(out=self.xT[:, :, SUPTOK:NTS], in_=bk[:, 0:KC * NS].rearrange("p (a b) -> p a b", a=KC)),
                 reads=[bb], writes=[self.b_xT[k][2] for k in range(KC)])

    def layer_params(self, sup, l):
        T = self.T
        d = self.d
        r = (sup * self.nl + l) % 2
        self.pr = r
        for i in range(2):
            T.dma("sp", self.lnv[:, r, i, :], d["lnv"][l, i:i + 1, :].partition_broadcast(128), buf=self.b_lnv[r], writes=[self.b_lnv[r]], multi=(i == 1))
        T.dma("sp", self.biasT[:, r, :, :], d["biasT"][:, l, :, :], buf=self.b_biasT[r], writes=[self.b_biasT[r]])
        T.dma("sp", self.xstage[:, :].rearrange("p (h s) -> p h s", h=8), d["sguw"][l].rearrange("h t s -> t h s"),
              buf=self.b_xstage, writes=[self.b_xstage])
        for half in range(2):
            bk, bb = self.banks.next()
            self.tr_group(bb, [(bk[:, i * 128:(i + 1) * 128], self.xstage[:, (half * 4 + i) * 128:(half * 4 + i + 1) * 128],
                                self.ident[:, :]) for i in range(4)], [self.b_xstage])
            T.op("dve", "tensor_tensor", dict(
                out=self.wsT[:, r, half * 4:(half + 1) * 4, :], in0=bk[:, :].rearrange("p (a b) -> p a b", a=4),
                in1=self.mask[:, :].unsqueeze(1).broadcast_to([128, 4, 128]), op=ALU.mult),
                reads=[bb, self.b_mask], writes=[self.b_wsT[r]])

    def mod_tile(self, l, t):
        T = self.T
        n1 = NS + 1
        bk, bb = self.banks.next()
        slot, sbuf = self.use_tile(0, l, "a", t)
        for c in range(4):
            mms = [(bk[:, c * n1:(c + 1) * n1], slot[:, (kc * 4 + c) * 128:(kc * 4 + c + 1) * 128], self.scb[:, kc, :],
                    kc == 0, kc == KC - 1, None) for kc in range(KC)]
            self.mm_group(bb, mms, [sbuf, self.b_scb])
        T.op("dve", "tensor_tensor", dict(
            out=self.modT[:, l, 4 * t:4 * t + 4, :], in0=bk[:, 0:4 * n1].rearrange("p (a b) -> p a b", a=4),
            in1=self.colsT[:, O_BA + l * 24 + 4 * t:O_BA + l * 24 + 4 * t + 4].unsqueeze(2).broadcast_to([128, 4, n1]), op=ALU.add),
            reads=[bb, self.b_cols], writes=[self.b_modT[l]])
        if t == NT_ADA - 1:
            self.mod_finish(l)

    def mod_step(self):
        if self.pending_mod:
            l, t = self.pending_mod.pop(0)
            self.mod_tile(l, t)

    def mod(self, l):
        for t in range(NT_ADA):
            self.mod_tile(l, t)

    def mod_finish(self, l):
        T = self.T
        n1 = NS + 1
        T.op("dve", "tensor_scalar", dict(out=self.modT[:, l, 16:24, :], in0=self.modT[:, l, 16:24, :], scalar1=0.5, scalar2=None, op0=ALU.mult),
             reads=[self.b_modT[l]], writes=[self.b_modT[l]])
        T.op("dve", "scalar_tensor_tensor", dict(
            out=self.gsP[:, l, :], in0=self.modT[:, l, 8:16, 0], scalar=1.0, in1=self.colsT[:, O_NG + l * 8:O_NG + (l + 1) * 8],
            op0=ALU.add, op1=ALU.mult), reads=[self.b_modT[l], self.b_cols], writes=[self.b_gsP[l]])
        T.op("dve", "scalar_tensor_tensor", dict(
            out=self.gsS[:, l, :, :], in0=self.modT[:, l, 8:16, 1:n1], scalar=1.0,
            in1=self.colsT[:, O_NG + l * 8:O_NG + (l + 1) * 8].unsqueeze(2).broadcast_to([128, 8, NS]),
            op0=ALU.add, op1=ALU.mult), reads=[self.b_modT[l], self.b_cols], writes=[self.b_gsS[l]])

    def rms_rstd(self, ti, c0, wd):
        T = self.T
        bk, bb = self.banks.next()
        for kc in range(KC):
            sq, sqb = self.tbf.next()
            T.op("act", "activation", dict(out=sq[:, 0:wd], in_=self.xT[:, kc, c0:c0 + wd], func=AF.Square),
                 reads=[self.b_xT[kc][ti]], writes=[sqb])
            self.mm_group(bb, [(bk[:, 0:wd], self.ones[:, :], sq[:, 0:wd], kc == 0, kc == KC - 1, None)], [sqb, self.b_ones])
        rt, rtb = self.rstd.next()
        xs, xsb = self.tmp.next()
        ts, tsb = self.tmp.next()
        T.op("dve", "tensor_scalar", dict(out=xs[:, 0:wd], in0=bk[:, 0:wd], scalar1=1.0 / D, scalar2=EPS, op0=ALU.mult, op1=ALU.add),
             reads=[bb], writes=[xsb])
        self.rsqrt_dve(xs[:, 0:wd], xsb, rt[:, 0:wd], rtb, ts[:, 0:wd], tsb, iters=RMS_NEWTON)
        return rt, rtb

    def rmsnorm(self, sup, l):
        T = self.T
        for ti, (c0, wd, samp) in enumerate(self.tiles):
            rt, rtb = self.rms_rstd(ti, c0, wd)
            if not samp:
                for kc in range(KC):
                    t, tb = self.tmp.next()
                    T.op("dve", "scalar_tensor_tensor", dict(
                        out=t[:, 0:wd], in0=self.xT[:, kc, c0:c0 + wd], scalar=self.gsP[:, l, kc:kc + 1], in1=rt[:, 0:wd],
                        op0=ALU.mult, op1=ALU.mult), reads=[self.b_xT[kc][ti], self.b_gsP[l], rtb], writes=[tb])
                    T.op("act", "activation", dict(
                        out=self.hT[:, kc, c0:c0 + wd], in_=t[:, 0:wd], func=AF.Identity, bias=self.modT[:, l, kc, 0:1], scale=1.0),
                        reads=[tb, self.b_modT[l]], writes=[self.b_hT[kc][ti]])
            else:
                t, tb = self.tmp.next()
                tv = t[:, 0:KC * NS].rearrange("p (a b) -> p a b", a=KC)
                T.op("dve", "tensor_tensor", dict(
                    out=tv, in0=self.xT[:, :, c0:c0 + wd], in1=rt[:, 0:wd].unsqueeze(1).broadcast_to([128, KC, NS]), op=ALU.mult),
                    reads=[self.b_xT[k][ti] for k in range(KC)] + [rtb], writes=[tb])
                T.op("dve", "tensor_tensor", dict(out=tv, in0=tv, in1=self.gsS[:, l, :, :], op=ALU.mult),
                     reads=[tb, self.b_gsS[l]], writes=[tb])
                T.op("dve", "tensor_tensor", dict(out=self.hT[:, :, c0:c0 + wd], in0=tv, in1=self.modT[:, l, 0:8, 1:NS + 1], op=ALU.add),
                     reads=[tb, self.b_modT[l]], writes=[self.b_hT[k][ti] for k in range(KC)])

    def proj_group(self, slot, sbuf, blk_of_kc, ti, c0, wd, bank=None):
        if bank is None:
            bank = self.banks.next()
        bk, bb = bank
        mms = [(bk[:, 0:wd], slot[:, blk_of_kc(kc) * 128:(blk_of_kc(kc) + 1) * 128], self.hT[:, kc, c0:c0 + wd],
                kc == 0, kc == KC - 1, None) for kc in range(KC)]
        self.mm_group(bb, mms, [sbuf] + [self.b_hT[kc][ti] for kc in range(KC)])
        return bk, bb

    def phase_a(self, sup, l):
        T = self.T
        d = self.d
        zb = self.zbuf
        for m in range(4):
            slot, sbuf = self.use_tile(sup, l, "m", T_A + m)
            w0 = self.col(O_CW + (l * 3 + 0) * 4 + m)
            w1 = self.col(O_CW + (l * 3 + 1) * 4 + m)
            w2 = self.col(O_CW + (l * 3 + 2) * 4 + m)
            cb = self.col(O_CB + l * 4 + m)
            T.op("dve", "tensor_copy", dict(out=zb[:, 0:2], in_=self.zcarry[:, l, m, :]),
                 reads=[self.b_zcarry[l][m]], writes=[self.b_zbuf[0]])
            for ti, (c0, wd, samp) in enumerate(self.tiles):
                if samp:
                    bk, bb = self.banks.next()
                    for c in range(4):
                        mms = [(bk[:, c * NS:(c + 1) * NS], slot[:, (kc * 4 + c) * 128:(kc * 4 + c + 1) * 128], self.hT[:, kc, c0:c0 + wd],
                                kc == 0, kc == KC - 1, None) for kc in range(KC)]
                        self.mm_group(bb, mms, [sbuf] + [self.b_hT[kc][ti] for kc in range(KC)])
                    s_, sb_ = self.tmp.next()
                    T.op("act", "activation", dict(out=s_[:, 0:4 * NS], in_=bk[:, 0:4 * NS], func=AF.Copy), reads=[bb], writes=[sb_])
                    T.op("act", "activation", dict(out=s_[:, 4 * NS:5 * NS], in_=s_[:, 3 * NS:4 * NS], func=AF.Tanh, scale=0.5), reads=[sb_], writes=[sb_])
                    T.op("dve", "scalar_tensor_tensor", dict(out=s_[:, 4 * NS:5 * NS], in0=s_[:, 4 * NS:5 * NS], scalar=1.0, in1=s_[:, 3 * NS:4 * NS],
                                                             op0=ALU.add, op1=ALU.mult), reads=[sb_], writes=[sb_])
                    T.op("dve", "tensor_tensor", dict(out=self.zsall[:, m, :], in0=s_[:, NS:2 * NS], in1=s_[:, 2 * NS:3 * NS], op=ALU.mult),
                         reads=[sb_], writes=[self.b_zsall])
                    T.op("dve", "scalar_tensor_tensor", dict(out=s_[:, 5 * NS:6 * NS], in0=self.zsall[:, m, :], scalar=w2, in1=self.CP[:, l, m, :],
                                                             op0=ALU.mult, op1=ALU.add), reads=[self.b_zsall, self.b_CP[l], self.b_cols], writes=[sb_])
                    T.op("dve", "tensor_tensor", dict(out=s_[:, 5 * NS:6 * NS], in0=s_[:, 0:NS], in1=s_[:, 5 * NS:6 * NS], op=ALU.mult), reads=[sb_], writes=[sb_])
                    T.op("dve", "scalar_tensor_tensor", dict(out=self.yb[0][:, m, c0:c0 + wd], in0=s_[:, 5 * NS:6 * NS], scalar=0.5, in1=s_[:, 4 * NS:5 * NS],
                                                             op0=ALU.mult, op1=ALU.mult), reads=[sb_], writes=[self.b_yb[0][m][ti]])
                    continue
                pb = [self.proj_group(slot, sbuf, (lambda kc, c=c: kc * 4 + c), ti, c0, wd) for c in range(4)]
                (b_ab, bb_ab), (b_ac, bb_ac), (b_ah, bb_ah), (b_ag, bb_ag) = pb
                sg, sgb = self.silu2(b_ag, bb_ag, wd)
                ah, ahb = self.tmp.next()
                T.op("act", "activation", dict(out=ah[:, 0:wd], in_=b_ah[:, 0:wd], func=AF.Copy),
                     reads=[bb_ah], writes=[ahb])
                acc, accb = self.tmp.next()
                if not samp:
                    zcur = self.b_zbuf[1 + ti]
                    zprev = self.b_zbuf[ti]
                    T.op("dve", "tensor_tensor", dict(out=zb[:, 2 + c0:2 + c0 + wd], in0=b_ac[:, 0:wd], in1=ah[:, 0:wd], op=ALU.mult),
                         reads=[bb_ac, ahb], writes=[zcur])
                    T.op("act", "activation", dict(out=acc[:, 0:wd], in_=zb[:, 2 + c0:2 + c0 + wd], func=AF.Identity, bias=cb, scale=w2),
                         reads=[zcur, self.b_cols], writes=[accb])
                    T.op("pool" if "conv" in OFFLOAD else "dve", "scalar_tensor_tensor", dict(out=acc[:, 0:wd], in0=zb[:, 1 + c0:1 + c0 + wd], scalar=w1, in1=acc[:, 0:wd],
                                                                          op0=ALU.mult, op1=ALU.add),
                         reads=[zcur, zprev, accb, self.b_cols], writes=[accb])
                    T.op("pool" if "conv" in OFFLOAD else "dve", "scalar_tensor_tensor", dict(out=acc[:, 0:wd], in0=zb[:, c0:c0 + wd], scalar=w0, in1=acc[:, 0:wd],
                                                                          op0=ALU.mult, op1=ALU.add),
                         reads=[zcur, zprev, accb, self.b_cols], writes=[accb])
                else:
                    T.op("dve", "tensor_tensor", dict(out=self.zsall[:, m, :], in0=b_ac[:, 0:wd], in1=ah[:, 0:wd], op=ALU.mult),
                         reads=[bb_ac, ahb], writes=[self.b_zsall])
                    T.op("dve", "scalar_tensor_tensor", dict(out=acc[:, 0:wd], in0=self.zsall[:, m, :], scalar=w2, in1=self.CP[:, l, m, :],
                                                                               op0=ALU.mult, op1=ALU.add),
                         reads=[self.b_zsall, self.b_CP[l], self.b_cols], writes=[accb])
                T.op("dve", "tensor_tensor", dict(out=acc[:, 0:wd], in0=b_ab[:, 0:wd], in1=acc[:, 0:wd], op=ALU.mult),
                     reads=[bb_ab, accb], writes=[accb])
                T.op("dve", "scalar_tensor_tensor", dict(out=self.yb[0][:, m, c0:c0 + wd], in0=acc[:, 0:wd], scalar=0.5, in1=sg[:, 0:wd], op0=ALU.mult, op1=ALU.mult),
                     reads=[accb, sgb], writes=[self.b_yb[0][m][ti]])
            if sup == 0:
                self.mod_step()
                T.op("dve", "tensor_copy", dict(out=self.zcarry[:, l, m, :], in_=zb[:, SUPTOK:SUPTOK + 2]),
                     reads=[self.b_zbuf[2]], writes=[self.b_zcarry[l][m]])
            else:
                T.op("dve", "tensor_copy", dict(out=self.ztail[:, m, :], in_=zb[:, SUPTOK:SUPTOK + 2]),
                     reads=[self.b_zbuf[2]], writes=[self.b_ztail])
        if sup == 1 and "a_out" not in os.environ.get("MK_SKIP", ""):
            bk, bb = self.banks.next()
            self.tr_group(bb, [(bk[0:2, m * 128:(m + 1) * 128], self.ztail[:, m, :], self.ident[:, :]) for m in range(4)], [self.b_ztail])
            o, ob = self.tmp.next()
            T.op("act", "activation", dict(out=o[0:2, :], in_=bk[0:2, :], func=AF.Copy), reads=[bb], writes=[ob])
            T.dma("sp", d["convp"][l, :, :], o[0:2, :], buf=ob, reads=[ob], store=True)
            bk, bb = self.banks.next()
            self.tr_group(bb, [(bk[0:NS, m * 128:(m + 1) * 128], self.zsall[:, m, :], self.ident[:, :]) for m in range(4)], [self.b_zsall])
            o, ob = self.tmp.next()
            T.op("act", "activation", dict(out=o[0:NS, :], in_=bk[0:NS, :], func=AF.Copy), reads=[bb], writes=[ob])
            T.dma("sp", d["convs"][l, :, 1, :], o[0:NS, :], buf=ob, reads=[ob], store=True)

    def phase_b(self, sup, l):
        T = self.T
        d = self.d
        r = self.pr
        slot, sbuf = self.use_tile(sup, l, "m", T_BV)
        SKIP = os.environ.get("MK_SKIP", "").split(",")
        nblk = 9 if (sup == 1 and "b_samp" not in SKIP) else 8
        st_ = {}

        def part1(blk):
            samp = blk == 8
            np_ = NS if samp else 128
            c0 = SUPTOK if samp else blk * 128
            ti = 2 if samp else blk // 4
            bk, bb = self.banks.next()
            mms = [(bk[0:np_, :], self.hT[:, kc, c0:c0 + np_], slot[:, kc * 512:(kc + 1) * 512], kc == 0, kc == KC - 1, None) for kc in range(KC)]
            self.mm_group(bb, mms, [sbuf] + [self.b_hT[kc][ti] for kc in range(KC)])
            g, gb = self.tmp.next()
            T.op("act", "activation", dict(out=g[0:np_, :], in_=bk[0:np_, :], func=AF.Gelu_apprx_tanh), reads=[bb], writes=[gb])
            sm, smb = self.small_ring.next()
            T.op("dve", "bn_stats", dict(out=sm[0:np_, 0:6], in_=g[0:np_, :]), reads=[gb], writes=[smb])
            T.op("dve", "bn_aggr", dict(out=sm[0:np_, 8:10], in_=sm[0:np_, 0:6]), reads=[smb], writes=[smb])
            T.op("dve", "tensor_scalar", dict(out=sm[0:np_, 10:11], in0=sm[0:np_, 9:10], scalar1=EPS, scalar2=None, op0=ALU.add), reads=[smb], writes=[smb])
            self.rsqrt_dve(sm[0:np_, 10:11], smb, sm[0:np_, 11:12], smb, sm[0:np_, 12:13], smb)
            T.op("dve", "scalar_tensor_tensor", dict(out=sm[0:np_, 13:14], in0=sm[0:np_, 8:9], scalar=-1.0, in1=sm[0:np_, 11:12], op0=ALU.mult, op1=ALU.mult),
                 reads=[smb], writes=[smb])
            T.op("act", "activation", dict(out=g[0:np_, :], in_=g[0:np_, :], func=AF.Identity, bias=sm[0:np_, 13:14], scale=sm[0:np_, 11:12]),
                 reads=[gb, smb], writes=[gb])
            st_[blk] = (samp, np_, g, gb)

        def part2(blk):
            samp, np_, g, gb = st_.pop(blk)
            T.op("pool" if "ln" in OFFLOAD else "dve", "tensor_tensor", dict(out=g[0:np_, :], in0=g[0:np_, :], in1=self.lnv[0:np_, r, 0, :], op=ALU.mult),
                 reads=[gb, self.b_lnv[r]], writes=[gb])
            T.op("pool" if "ln" in OFFLOAD else "dve", "tensor_tensor", dict(out=g[0:np_, :], in0=g[0:np_, :], in1=self.lnv[0:np_, r, 1, :], op=ALU.add),
                 reads=[gb, self.b_lnv[r]], writes=[gb])
            T.op("act", "activation", dict(out=self.vb[0:np_, blk, :], in_=g[0:np_, :], func=AF.Copy), reads=[gb], writes=[self.b_vb[blk]])
            if sup == 1 and blk == 7 and "b_vp" not in SKIP:
                T.dma("sp", d["vp"][l, :, :], g[:, :], buf=gb, reads=[gb], store=True)
            if samp and "b_vs" not in SKIP:
                T.dma("sp", d["vs"][l, :, :], g[0:NS, :], buf=gb, reads=[gb], store=True)
            if samp and "b_tr" not in SKIP:
                bk2, bb2 = self.banks.next()
                self.tr_group(bb2, [(bk2[:, m * NS:(m + 1) * NS], g[0:NS, m * 128:(m + 1) * 128], self.ident[0:NS, 0:NS]) for m in range(4)], [gb])
                for m in range(4):
                    T.op("dve", "tensor_scalar", dict(
                        out=self.mixS[:, m, :], in0=bk2[:, m * NS:(m + 1) * NS], scalar1=self.col(O_SA + l * 4 + m), scalar2=self.col(O_SB + l * 4 + m),
                        op0=ALU.mult, op1=ALU.add), reads=[bb2, self.b_cols], writes=[self.b_mixS])

        for blk in range(nblk + 1):
            if blk < nblk:
                part1(blk)
            if blk >= 1:
                part2(blk - 1)
        if sup == 0:
            self.mod_step()
        for m in range(4):
            slot, sbuf = self.use_tile(sup, l, "m", T_BUG + m // 2)
            mm_ = m % 2
            for ti, (c0, wd, samp) in enumerate(self.tiles):
                if samp:
                    bk, bb = self.banks.next()
                    for c in range(2):
                        mms = [(bk[:, c * NS:(c + 1) * NS], slot[:, ((mm_ * 2 + c) * 8 + kc) * 128:((mm_ * 2 + c) * 8 + kc + 1) * 128],
                                self.hT[:, kc, c0:c0 + wd], kc == 0, kc == KC - 1, None) for kc in range(KC)]
                        self.mm_group(bb, mms, [sbuf] + [self.b_hT[kc][ti] for kc in range(KC)])
                    s_, sb_ = self.tmp.next()
                    T.op("act", "activation", dict(out=s_[:, 0:NS], in_=bk[:, 0:NS], func=AF.Gelu_apprx_tanh), reads=[bb], writes=[sb_])
                    T.op("act", "activation", dict(out=s_[:, NS:2 * NS], in_=bk[:, NS:2 * NS], func=AF.Copy), reads=[bb], writes=[sb_])
                    T.op("act", "activation", dict(out=s_[:, 2 * NS:3 * NS], in_=s_[:, NS:2 * NS], func=AF.Tanh, scale=0.5), reads=[sb_], writes=[sb_])
                    T.op("dve", "scalar_tensor_tensor", dict(out=s_[:, 2 * NS:3 * NS], in0=s_[:, 2 * NS:3 * NS], scalar=1.0, in1=s_[:, NS:2 * NS],
                                                             op0=ALU.add, op1=ALU.mult), reads=[sb_], writes=[sb_])
                    T.op("dve", "tensor_tensor", dict(out=s_[:, 3 * NS:4 * NS], in0=self.mixS[:, m, :], in1=s_[:, 0:NS], op=ALU.mult),
                         reads=[self.b_mixS, sb_], writes=[sb_])
                    T.op("dve", "scalar_tensor_tensor", dict(out=self.yb[1][:, m, c0:c0 + wd], in0=s_[:, 3 * NS:4 * NS], scalar=0.5, in1=s_[:, 2 * NS:3 * NS],
                                                             op0=ALU.mult, op1=ALU.mult), reads=[sb_], writes=[self.b_yb[1][m][ti]])
                    continue
                b_u, bb_u = self.proj_group(slot, sbuf, (lambda kc: (mm_ * 2 + 0) * 8 + kc), ti, c0, wd)
                b_g, bb_g = self.proj_group(slot, sbuf, (lambda kc: (mm_ * 2 + 1) * 8 + kc), ti, c0, wd)
                u, ub = self.tmp.next()
                T.op("act", "activation", dict(out=u[:, 0:wd], in_=b_u[:, 0:wd], func=AF.Gelu_apprx_tanh), reads=[bb_u], writes=[ub])
                sg, sgb = self.silu2(b_g, bb_g, wd)
                if not samp:
                    bk, bb = self.banks.next()
                    mms = []
                    for bi in range(4):
                        blk = ti * 4 + bi
                        for hh in range(2):
                            h = 2 * m + hh
                            mms.append((bk[hh * 64:(hh + 1) * 64, bi * 128:(bi + 1) * 128], self.vb[:, blk, h * 64:(h + 1) * 64],
                                        self.wsT[:, r, h, :], True, True, (0, hh * 64)))
                    self.mm_group(bb, mms, [self.b_vb[ti * 4 + bi] for bi in range(4)] + [self.b_wsT[r]])
                    t, tb = self.tmp.next()
                    T.op("dve", "tensor_tensor", dict(
                        out=t[:, :].rearrange("p (a b) -> p a b", a=4), in0=bk[:, :].rearrange("p (a b) -> p a b", a=4),
                        in1=self.biasT[:, r, m, :].unsqueeze(1).broadcast_to([128, 4, 128]), op=ALU.add),
                        reads=[bb, self.b_biasT[r]], writes=[tb])
                    T.op("pool" if "bug" in OFFLOAD else "dve", "tensor_tensor", dict(out=t[:, 0:wd], in0=t[:, 0:wd], in1=u[:, 0:wd], op=ALU.mult), reads=[tb, ub], writes=[tb])
                else:
                    t, tb = self.tmp.next()
                    T.op("dve", "tensor_tensor", dict(out=t[:, 0:wd], in0=self.mixS[:, m, :], in1=u[:, 0:wd], op=ALU.mult),
                         reads=[self.b_mixS, ub], writes=[tb])
                T.op("dve", "scalar_tensor_tensor", dict(out=self.yb[1][:, m, c0:c0 + wd], in0=t[:, 0:wd], scalar=0.5, in1=sg[:, 0:wd], op0=ALU.mult, op1=ALU.mult),
                     reads=[tb, sgb], writes=[self.b_yb[1][m][ti]])

    def phase_b_end(self, sup):
        if sup == 0:
            self.mod_step()

    def phase_c(self, sup, l):
        T = self.T
        d = self.d
        slot, sbuf = self.use_tile(sup, l, "m", T_CX)
        for blk in range(8):
            ti = blk // 4
            c0 = blk * 128
            bk, bb = self.banks.next()
            mms = [(bk[:, :], self.hT[:, kc, c0:c0 + 128], slot[:, kc * 512:(kc + 1) * 512], kc == 0, kc == KC - 1, None) for kc in range(KC)]
            self.mm_group(bb, mms, [sbuf] + [self.b_hT[kc][ti] for kc in range(KC)])
            T.op("act", "activation", dict(out=self.cxb[:, blk, :], in_=bk[:, :], func=AF.Copy), reads=[bb], writes=[self.b_cxb[blk]])
            if sup == 1 and blk == 7:
                bk3, bb3 = self.banks.next()
                mms = [(bk3[0:15, :], self.hT[:, kc, SUPTOK - 15:SUPTOK], slot[:, kc * 512:(kc + 1) * 512], kc == 0, kc == KC - 1, None) for kc in range(KC)]
                self.mm_group(bb3, mms, [sbuf] + [self.b_hT[kc][1] for kc in range(KC)])
                o, ob = self.tmp.next()
                T.op("dve", "tensor_copy", dict(out=o[0:15, :], in_=bk3[0:15, :]), reads=[bb3], writes=[ob])
                T.dma("sp", d["poolp"][l, :, :], o[0:15, :], buf=ob, reads=[ob], store=True)
        if sup == 1:
            c0 = SUPTOK
            bk, bb = self.banks.next()
            for g in range(4):
                mms = [(bk[:, g * NS:(g + 1) * NS], slot[:, (kc * 4 + g) * 128:(kc * 4 + g + 1) * 128], self.hT[:, kc, c0:c0 + NS],
                        kc == 0, kc == KC - 1, None) for kc in range(KC)]
                self.mm_group(bb, mms, [sbuf] + [self.b_hT[kc][2] for kc in range(KC)])
            xc, xcb = self.tmp.next()
            T.op("act", "activation", dict(out=xc[:, 0:4 * NS], in_=bk[:, 0:4 * NS], func=AF.Copy), reads=[bb], writes=[xcb])
            for g, w in enumerate(POOL_WIN):
                T.op("dve", "scalar_tensor_tensor", dict(
                    out=self.pmS[:, g, :], in0=xc[:, g * NS:(g + 1) * NS], scalar=1.0 / w - 1.0, in1=self.SpT[:, l, g, :], op0=ALU.mult, op1=ALU.add),
                    reads=[xcb, self.b_SpT[l]], writes=[self.b_pmS])
            bk2, bb2 = self.banks.next()
            self.tr_group(bb2, [(bk2[0:NS, g * 128:(g + 1) * 128], xc[:, g * NS:(g + 1) * NS], self.ident[:, :]) for g in range(4)], [xcb])
            o, ob = self.tmp.next()
            T.op("act", "activation", dict(out=o[0:NS, :], in_=bk2[0:NS, :], func=AF.Copy), reads=[bb2], writes=[ob])
            T.dma("sp", d["pools"][l, :, 14, :], o[0:NS, :], buf=ob, reads=[ob], store=True)
        slot, sbuf = self.use_tile(sup, l, "m", T_CG)
        for g in range(4):
            for ti, (c0, wd, samp) in enumerate(self.tiles):
                pm, pmb = self.tbf.next()
                if not samp:
                    bk, bb = self.banks.next()
                    mms = []
                    rd = [self.b_ptb]
                    for bi in range(4):
                        blk = ti * 4 + bi
                        o = bk[:, bi * 128:(bi + 1) * 128]
                        cur = self.cxb[:, blk, g * 128:(g + 1) * 128]
                        rd.append(self.b_cxb[blk])
                        if sup == 0 and blk == 0:
                            mms.append((o, cur, self.ptb[:, g * 4 + 2, :], True, False, None))
                            mms.append((o, cur, self.ptb[:, g * 4 + 3, :], False, True, None))
                        else:
                            if blk == 0:
                                prev = self.cxcarry[64:128, l, g * 128:(g + 1) * 128]
                                rd.append(self.b_cxcarry[l])
                            else:
                                prev = self.cxb[64:128, blk - 1, g * 128:(g + 1) * 128]
                                rd.append(self.b_cxb[blk - 1])
                            mms.append((o, cur, self.ptb[:, g * 4 + 0, :], True, False, None))
                            mms.append((o, prev, self.ptb[64:128, g * 4 + 1, :], False, True, None))
                    self.mm_group(bb, mms, rd)
                    T.op("act", "activation", dict(out=pm[:, 0:wd], in_=bk[:, 0:wd], func=AF.Copy), reads=[bb], writes=[pmb])
                    rhs = pm[:, 0:wd]
                    rhs_b = pmb
                else:
                    rhs = self.pmS[:, g, :]
                    rhs_b = self.b_pmS
                byc, bbyc = self.banks.next()
                self.mm_group(bbyc, [(byc[:, 0:wd], self.pwb[:, l * 4 + g, :], rhs, True, True, None)], [self.b_pwb, rhs_b])
                b_g, bb_g = self.proj_group(slot, sbuf, (lambda kc, g=g: g * 8 + kc), ti, c0, wd)
                sg, sgb = self.silu2(b_g, bb_g, wd)
                t, tb = self.tmp.next()
                T.op("dve", "tensor_scalar", dict(
                    out=t[:, 0:wd], in0=byc[:, 0:wd], scalar1=self.col(O_PB + l * 4 + g), scalar2=self.col(O_PS + l * 4 + g), op0=ALU.add, op1=ALU.mult),
                    reads=[bbyc, self.b_cols], writes=[tb])
                T.op("dve", "scalar_tensor_tensor", dict(out=self.yb[2][:, g, c0:c0 + wd], in0=t[:, 0:wd], scalar=0.5, in1=sg[:, 0:wd], op0=ALU.mult, op1=ALU.mult),
                     reads=[tb, sgb], writes=[self.b_yb[2][g][ti]])
        if sup == 0:
            T.op("dve", "tensor_copy", dict(out=self.cxcarry[:, l, :], in_=self.cxb[:, 7, :]), reads=[self.b_cxb[7]], writes=[self.b_cxcarry[l]])

    def merge(self, sup, l):
        T = self.T
        for j in range(KC):
            for ti, (c0, wd, samp) in enumerate(self.tiles):
                acc, accb = self.tmp.next()
                for br in range(3):
                    base = j * 36 + br * 12
                    bk, bb = self.banks.next()
                    mms = []
                    rd = []
                    for kc in range(KC):
                        idx = base + kc
                        slot, sbuf = self.use_tile(sup, l, "m", T_MRG + idx // 32)
                        p = idx % 32
                        mms.append((bk[:, 0:wd], slot[:, p * 128:(p + 1) * 128], self.hT[:, kc, c0:c0 + wd], kc == 0, kc == KC - 1, None))
                        if sbuf not in rd:
                            rd.append(sbuf)
                    self.mm_group(bb, mms, rd + [self.b_hT[kc][ti] for kc in range(KC)])
                    sgt, sgtb = self.tmp.next()
                    T.op("act", "activation", dict(out=sgt[:, 0:wd], in_=bk[:, 0:wd], func=AF.Tanh, scale=0.5), reads=[bb], writes=[sgtb])
                    bp, bbp = self.banks.next()
                    mms = []
                    rd = []
                    for k4 in range(4):
                        idx = base + 8 + k4
                        slot, sbuf = self.use_tile(sup, l, "m", T_MRG + idx // 32)
                        p = idx % 32
                        mms.append((bp[:, 0:wd], slot[:, p * 128:(p + 1) * 128], self.yb[br][:, k4, c0:c0 + wd], k4 == 0, k4 == 3, None))
                        if sbuf not in rd:
                            rd.append(sbuf)
                    self.mm_group(bbp, mms, rd + [self.b_yb[br][k4][ti] for k4 in range(4)])
                    if br == 0:
                        T.op("dve", "scalar_tensor_tensor", dict(out=acc[:, 0:wd], in0=sgt[:, 0:wd], scalar=1.0, in1=bp[:, 0:wd], op0=ALU.add, op1=ALU.mult),
                             reads=[bbp, sgtb], writes=[accb])
                    else:
                        T.op("dve", "scalar_tensor_tensor", dict(out=sgt[:, 0:wd], in0=sgt[:, 0:wd], scalar=1.0, in1=bp[:, 0:wd], op0=ALU.add, op1=ALU.mult),
                             reads=[bbp, sgtb], writes=[sgtb])
                        if br == 1:
                            T.op("pool" if "merge" in OFFLOAD else "dve", "tensor_tensor", dict(out=acc[:, 0:wd], in0=acc[:, 0:wd], in1=sgt[:, 0:wd], op=ALU.add),
                                 reads=[accb, sgtb], writes=[accb])
                        else:
                            T.op("pool" if "merge" in OFFLOAD else "dve", "tensor_tensor", dict(out=self.mg[:, j, c0:c0 + wd], in0=acc[:, 0:wd], in1=sgt[:, 0:wd], op=ALU.add),
                                 reads=[accb, sgtb], writes=[self.b_mg[j][ti]])

    def wout(self, sup, l):
        T = self.T
        for j in range(KC):
            slot, sbuf = self.use_tile(sup, l, "m", T_WO + j // 4)
            jj = j % 4
            for ti, (c0, wd, samp) in enumerate(self.tiles):
                bk, bb = self.banks.next()
                mms = [(bk[:, 0:wd], slot[:, (jj * 8 + kc) * 128:(jj * 8 + kc + 1) * 128], self.mg[:, kc, c0:c0 + wd], kc == 0, kc == KC - 1, None)
                       for kc in range(KC)]
                self.mm_group(bb, mms, [sbuf] + [self.b_mg[kc][ti] for kc in range(KC)])
                if not samp:
                    T.op("dve", "scalar_tensor_tensor", dict(
                        out=self.xT[:, j, c0:c0 + wd], in0=bk[:, 0:wd], scalar=self.modT[:, l, 16 + j, 0:1], in1=self.xT[:, j, c0:c0 + wd],
                        op0=ALU.mult, op1=ALU.add), reads=[bb, self.b_modT[l], self.b_xT[j][ti]], writes=[self.b_xT[j][ti]])
                else:
                    t, tb = self.tmp.next()
                    T.op("dve", "tensor_tensor", dict(out=t[:, 0:wd], in0=bk[:, 0:wd], in1=self.modT[:, l, 16 + j, 1:NS + 1], op=ALU.mult),
                         reads=[bb, self.b_modT[l]], writes=[tb])
                    T.op("dve", "tensor_tensor", dict(out=self.xT[:, j, c0:c0 + wd], in0=self.xT[:, j, c0:c0 + wd], in1=t[:, 0:wd], op=ALU.add),
                         reads=[tb, self.b_xT[j][ti]], writes=[self.b_xT[j][ti]])

    def final(self, sup):
        T = self.T
        d = self.d
        for ti, (c0, wd, samp) in enumerate(self.tiles):
            rt, rtb = self.rms_rstd(ti, c0, wd)
            for kc in range(KC):
                T.op("dve", "scalar_tensor_tensor", dict(
                    out=self.xT[:, kc, c0:c0 + wd], in0=self.xT[:, kc, c0:c0 + wd], scalar=self.col(O_FG + kc), in1=rt[:, 0:wd],
                    op0=ALU.mult, op1=ALU.mult), reads=[self.b_xT[kc][ti], self.b_cols, rtb], writes=[self.b_xT[kc][ti]])
            if not samp:
                for bi in range(4):
                    blk = ti * 4 + bi
                    r0 = (sup * 8 + blk) * 128
                    for half in range(2):
                        bk, bb = self.banks.next()
                        self.tr_group(bb, [(bk[:, i * 128:(i + 1) * 128], self.xT[:, half * 4 + i, blk * 128:(blk + 1) * 128], self.ident[:, :])
                                           for i in range(4)], [self.b_xT[half * 4 + i][ti] for i in range(4)])
                        o, ob = self.tmp.next()
                        if half == 0:
                            T.op("act", "activation", dict(out=o[:, :], in_=bk[:, :], func=AF.Copy), reads=[bb], writes=[ob])
                        else:
                            T.op("dve", "tensor_copy", dict(out=o[:, :], in_=bk[:, :]), reads=[bb], writes=[ob])
                        T.dma("sp", d["yp"][r0:r0 + 128, half * 512:(half + 1) * 512], o[:, :], buf=ob, reads=[ob], store=True)
            else:
                st = self.st32
                for half in range(2):
                    bk, bb = self.banks.next()
                    self.tr_group(bb, [(bk[0:NS, i * 128:(i + 1) * 128], self.xT[:, half * 4 + i, c0:c0 + NS], self.ident[:, :]) for i in range(4)],
                                  [self.b_xT[half * 4 + i][ti] for i in range(4)])
                    T.op("dve", "tensor_copy", dict(out=st[0:NS, half * 512:(half + 1) * 512], in_=bk[0:NS, :]), reads=[bb], writes=[self.b_st32])
                T.dma("sp", d["ys"][:, :], st[0:NS, 0:D], buf=self.b_st32, reads=[self.b_st32], store=True)


def _tiles_from_blocks(blocks):
    n = blocks.shape[0] // 32
    return np.ascontiguousarray(blocks.reshape(n, 32, 128, 128).transpose(0, 2, 1, 3).reshape(n, 128, 4096))


def _build_streams(w_ada, w_in, w_branch, w_out):
    wa = np.empty((L, NT_ADA, 128, 4096), np.float32)
    wm = np.empty((L, NT_MAIN, 128, 4096), np.float32)
    CO = dict(ab=0, ac=4, ah=8, ag=12, bu=16, bv=20, bg=24, cx=28, cg=32, ma=36, mb=44, mc=52)
    for l in range(L):
        a4 = w_ada[l].reshape(8, 128, 24, 128).transpose(0, 2, 1, 3)
        bl = [a4[kc, 4 * t + c] for t in range(NT_ADA) for kc in range(8) for c in range(4)]
        wa[l] = _tiles_from_blocks(np.stack(bl))
        i4 = w_in[l].reshape(8, 128, 60, 128).transpose(0, 2, 1, 3)
        b4 = w_branch[l].reshape(3, 4, 128, 8, 128).transpose(0, 1, 3, 2, 4)
        o4 = w_out[l].reshape(8, 128, 8, 128).transpose(0, 2, 1, 3)
        bl = []
        for m in range(4):
            bl += [i4[kc, CO[c] + m] for kc in range(8) for c in ("ab", "ac", "ah", "ag")]
        bl += [i4[kc, CO["bv"] + c] for kc in range(8) for c in range(4)]
        for q in range(2):
            for mm_ in range(2):
                for c in ("bu", "bg"):
                    bl += [i4[kc, CO[c] + 2 * q + mm_] for kc in range(8)]
        bl += [i4[kc, CO["cx"] + c] for kc in range(8) for c in range(4)]
        bl += [i4[kc, CO["cg"] + g] for g in range(4) for kc in range(8)]
        for j in range(8):
            for br, c in enumerate(("ma", "mb", "mc")):
                bl += [i4[kc, CO[c] + j] for kc in range(8)]
                bl += [b4[br, k4, j] for k4 in range(4)]
        for j in range(8):
            bl += [o4[kc, j] for kc in range(8)]
        assert len(bl) == NT_MAIN * 32
        wm[l] = _tiles_from_blocks(np.stack(bl))
    return wa, wm


def _chunk_cols(v, n):
    return np.ascontiguousarray(np.asarray(v, np.float32).reshape(n, 128).T)


def _pool_tables():
    import ml_dtypes
    pt = np.zeros((128, 16, 128), np.float64)
    s = np.arange(128)[:, None]
    t = np.arange(128)[None, :]
    for g, w in enumerate(POOL_WIN):
        diag = ((s <= t) & (s > t - w)) / float(w) - (s == t)
        off = ((s - 128) > (t - w)) / float(w)
        cnt = np.minimum(w, t + 1).astype(np.float64)
        first = ((s <= t) & (s > t - w)) / cnt - (s == t)
        hi = first.astype(np.float32).astype(ml_dtypes.bfloat16).astype(np.float64)
        lo = (first - hi).astype(np.float32).astype(ml_dtypes.bfloat16).astype(np.float64)
        pt[:, g * 4 + 0] = diag
        pt[:, g * 4 + 1] = off
        pt[:, g * 4 + 2] = hi
        pt[:, g * 4 + 3] = lo
    return pt.astype(np.float32)


_NC_CACHE = {}


def _get_nc(nl):
    if nl not in _NC_CACHE:
        _NC_CACHE[nl] = Prog(nl).build()
    return _NC_CACHE[nl]


def kernel(x_prompt, x_sample, c_prompt, c_sample, state_conv, state_pool, w_ada, b_ada, norm_g,
           w_in, conv_w, conv_b, lnv_g, lnv_b, sgu_w, sgu_b, pool_w, pool_b, pool_scale,
           w_branch, w_out, final_g):
    f = lambda a: np.ascontiguousarray(np.asarray(a, dtype=np.float32))
    x_prompt, x_sample, c_prompt, c_sample = f(x_prompt), f(x_sample), f(c_prompt), f(c_sample)
    state_conv, state_pool = f(state_conv), f(state_pool)
    w_ada, w_in, w_branch, w_out = f(w_ada), f(w_in), f(w_branch), f(w_out)
    sgu_w, sgu_b, pool_w = f(sgu_w), f(sgu_b), f(pool_w)
    nl = int(os.environ.get("MK_NL", L))
    wa, wm = _build_streams(w_ada, w_in, w_branch, w_out)
    cols = np.zeros((128, NCOLS), np.float32)
    for l in range(L):
        cols[:, O_BA + l * 24:O_BA + (l + 1) * 24] = _chunk_cols(b_ada[l], 24)
        cols[:, O_NG + l * 8:O_NG + (l + 1) * 8] = _chunk_cols(norm_g[l], 8)
        for k in range(3):
            cols[:, O_CW + (l * 3 + k) * 4:O_CW + (l * 3 + k + 1) * 4] = _chunk_cols(conv_w[l, k], 4)
        cols[:, O_CB + l * 4:O_CB + (l + 1) * 4] = _chunk_cols(conv_b[l], 4)
        cols[:, O_PB + l * 4:O_PB + (l + 1) * 4] = _chunk_cols(pool_b[l], 4)
        cols[:, O_PS + l * 4:O_PS + (l + 1) * 4] = _chunk_cols(pool_scale[l], 4)
        cols[:, O_SA + l * 4:O_SA + (l + 1) * 4] = _chunk_cols(np.repeat(sgu_w[l, :, 0, 0], 64), 4)
        cols[:, O_SB + l * 4:O_SB + (l + 1) * 4] = _chunk_cols(np.repeat(sgu_b[l, :, 0], 64), 4)
    cols[:, O_FG:O_FG + 8] = _chunk_cols(final_g, 8)
    cols[:, O_EPS] = EPS
    lnv = np.ascontiguousarray(np.stack([f(lnv_g), f(lnv_b)], axis=1))
    biasT = np.ascontiguousarray(np.repeat(sgu_b.reshape(L, 4, 2, 1, 128), 64, axis=3).reshape(L, 4, 128, 128).transpose(2, 0, 1, 3))
    poolw = np.ascontiguousarray(pool_w.reshape(L * 4, 128, 128).transpose(1, 0, 2))
    ident = np.eye(128, dtype=np.float32)
    mask = np.triu(np.ones((128, 128), np.float32))
    pt = _pool_tables()
    shared = dict(wa=wa, wm=wm, cols=cols, lnv=lnv, sguw=sgu_w, biasT=biasT, poolw=poolw, ident=ident, mask=mask, pt=pt)
    in_maps = []
    for c in range(NCORE):
        s0, s1 = c * NS, (c + 1) * NS
        m = dict(shared)
        m["xp"] = x_prompt[c]
        m["xs"] = np.ascontiguousarray(x_sample[s0:s1, 0, :])
        m["cc"] = np.ascontiguousarray(np.concatenate([c_prompt[c:c + 1], c_sample[s0:s1]], axis=0))
        m["sconv"] = np.ascontiguousarray(state_conv[:, s0:s1])
        m["spool"] = np.ascontiguousarray(state_pool[:, s0:s1])
        in_maps.append(m)
    nc = _get_nc(nl)
    res = run_bass_kernel_spmd(nc, in_maps, core_ids=list(range(NCORE)))
    R = res.results
    y_prompt = np.stack([R[c]["yp"] for c in range(NCORE)], axis=0)
    y_sample = np.concatenate([R[c]["ys"] for c in range(NCORE)], axis=0)[:, None, :]
    conv_p = np.stack([R[c]["convp"] for c in range(NCORE)], axis=1)
    conv_s = np.concatenate([R[c]["convs"] for c in range(NCORE)], axis=1)
    pool_p = np.stack([R[c]["poolp"] for c in range(NCORE)], axis=1)
    pool_s = np.concatenate([R[c]["pools"] for c in range(NCORE)], axis=1)
    v_p = np.stack([R[c]["vp"] for c in range(NCORE)], axis=1)
    v_s = np.concatenate([R[c]["vs"] for c in range(NCORE)], axis=1)[:, :, None, :]
    outs = (y_prompt, y_sample, conv_p, conv_s, pool_p, pool_s, v_p, v_s)
    return tuple(np.ascontiguousarray(o.astype(np.float32)) for o in outs)
```

```python
import os
import numpy as np
from contextlib import ExitStack
import concourse.bass as bass
import concourse.mybir as mybir
from concourse.bass_utils import run_bass_kernel_spmd

F32 = mybir.dt.float32
BF16 = mybir.dt.bfloat16
AF = mybir.ActivationFunctionType
ALU = mybir.AluOpType
AX = mybir.AxisListType

D = 1024
W = 512
L = 4
KC = 8
NCORE = 8
SEQ = 2048
NS = 16
SUPTOK = 1024
NTS = SUPTOK + NS
EPS = 1e-6
NSLOT = 3
NBANK = 8
NTMP = 6
POOL_WIN = (2, 4, 8, 16)
NEWTON = int(os.environ.get("MK_NEWTON", 2))
RMS_NEWTON = int(os.environ.get("MK_RMS_NEWTON", 2))
OFFLOAD = os.environ.get("MK_POOL", "").split(",")

O_BA = 0
O_NG = 96
O_FG = 128
O_CW = 136
O_CB = 184
O_PB = 200
O_PS = 216
O_SA = 232
O_SB = 248
O_EPS = 264
NCOLS = 272

T_A = 0
T_BV = 4
T_BUG = 6
T_CX = 5
T_CG = 8
T_MRG = 9
T_WO = 18
NT_MAIN = 20
NT_ADA = 6


class Buf:
    __slots__ = ("name", "w", "r", "sem", "total")

    def __init__(self, name):
        self.name = name
        self.w = {}
        self.r = {}
        self.sem = None
        self.total = 0


class Eng:
    def __init__(self, name, sem):
        self.name = name
        self.sem = sem
        self.count = 0
        self.waited = {}
        self.prog = []


class Tracker:
    def __init__(self, nc, es):
        self.nc = nc
        self.es = es
        self.sems = {}
        self.eng = {}
        for n in ("pe", "act", "dve", "pool", "sp"):
            key = "s_" + n
            self.sems[key] = es.enter_context(nc.semaphore(key))
            self.eng[n] = Eng(n, key)
        self.store_bufs = []

    def _need(self, reads, writes):
        need = {}
        for b in reads:
            for k, v in b.w.items():
                if need.get(k, 0) < v:
                    need[k] = v
        for b in writes:
            for k, v in b.w.items():
                if need.get(k, 0) < v:
                    need[k] = v
            for k, v in b.r.items():
                if need.get(k, 0) < v:
                    need[k] = v
        return need

    def _waits(self, E, need, skip=None):
        for key, val in need.items():
            if key == skip:
                continue
            if E.name == "pe" and key == E.sem:
                continue
            if E.waited.get(key, 0) < val:
                E.waited[key] = val
                h = self.sems[key]
                E.prog.append(("wait_ge", dict(sem=h, val=val), None))

    def op(self, eng, name, kw=None, reads=(), writes=(), inc=True):
        E = self.eng[eng]
        self._waits(E, self._need(reads, writes))
        if inc:
            E.count += 1
            val = E.count
            h = self.sems[E.sem]
        else:
            val = E.count + 1
            h = None
        E.prog.append((name, kw, h))
        for b in reads:
            if b.r.get(E.sem, 0) < val:
                b.r[E.sem] = val
        for b in writes:
            b.w = {E.sem: val}
            b.r = {}

    def dma(self, q, out_ap, in_ap, buf, reads=(), writes=(), store=False, multi=False):
        E = self.eng[q]
        if buf.sem is None:
            key = "d_" + buf.name
            self.sems[key] = self.es.enter_context(self.nc.semaphore(key))
            buf.sem = key
        key = buf.sem
        self._waits(E, self._need(reads, writes), skip=key if multi else None)
        buf.total += 16
        val = buf.total
        h = self.sems[key]
        E.prog.append(("dma_start", dict(out=out_ap, in_=in_ap), (h, 16)))
        for b in reads:
            if b.r.get(key, 0) < val:
                b.r[key] = val
        for b in writes:
            b.w = {key: val}
            b.r = {}
        if store and buf not in self.store_bufs:
            self.store_bufs.append(buf)

    def final_wait(self, q):
        E = self.eng[q]
        for b in self.store_bufs:
            h = self.sems[b.sem]
            E.prog.append(("wait_ge", dict(sem=h, val=b.total), None))


def replay(e, prog):
    for name, kw, h in prog:
        if name == "wait_ge":
            e.wait_ge(kw["sem"], kw["val"])
            continue
        if callable(name):
            ins = name(e)
        else:
            ins = getattr(e, name)(**kw)
        if h is not None:
            if isinstance(h, tuple):
                ins.then_inc(h[0], h[1])
            else:
                ins.then_inc(h, 1)


class Ring:
    def __init__(self, items):
        self.items = items
        self.i = 0

    def next(self):
        it = self.items[self.i % len(self.items)]
        self.i += 1
        return it


class Prog:
    def __init__(self, nlayers=L):
        self.nl = nlayers
        self.nc = bass.Bass("TRN2", target_bir_lowering=False)

    def dram_in(self, name, shape):
        return self.nc.dram_tensor(name, list(shape), F32, kind="ExternalInput").ap()

    def dram_out(self, name, shape):
        return self.nc.dram_tensor(name, list(shape), F32, kind="ExternalOutput").ap()

    def sb(self, name, shape, dt=F32):
        return self.es.enter_context(self.nc.sbuf_tensor("sb_" + name, list(shape), dt))

    def build(self):
        nc = self.nc
        d = {}
        d["xp"] = self.dram_in("xp", [SEQ, D])
        d["xs"] = self.dram_in("xs", [NS, D])
        d["cc"] = self.dram_in("cc", [NS + 1, D])
        d["sconv"] = self.dram_in("sconv", [L, NS, 2, W])
        d["spool"] = self.dram_in("spool", [L, NS, 15, W])
        d["wa"] = self.dram_in("wa", [L, NT_ADA, 128, 4096])
        d["wm"] = self.dram_in("wm", [L, NT_MAIN, 128, 4096])
        d["cols"] = self.dram_in("cols", [128, NCOLS])
        d["lnv"] = self.dram_in("lnv", [L, 2, W])
        d["sguw"] = self.dram_in("sguw", [L, 8, 128, 128])
        d["biasT"] = self.dram_in("biasT", [128, L, 4, 128])
        d["poolw"] = self.dram_in("poolw", [128, L * 4, 128])
        d["ident"] = self.dram_in("ident", [128, 128])
        d["mask"] = self.dram_in("mask", [128, 128])
        d["pt"] = self.dram_in("pt", [128, 16, 128])
        d["yp"] = self.dram_out("yp", [SEQ, D])
        d["ys"] = self.dram_out("ys", [NS, D])
        d["convp"] = self.dram_out("convp", [L, 2, W])
        d["convs"] = self.dram_out("convs", [L, NS, 2, W])
        d["poolp"] = self.dram_out("poolp", [L, 15, W])
        d["pools"] = self.dram_out("pools", [L, NS, 15, W])
        d["vp"] = self.dram_out("vp", [L, 128, W])
        d["vs"] = self.dram_out("vs", [L, NS, W])
        self.d = d
        with ExitStack() as es:
            self.es = es
            self.T = Tracker(nc, es)
            self.alloc()
            self.emit()
            block = es.enter_context(nc.Block())

            @block.tensor
            def _(e):
                replay(e, self.T.eng["pe"].prog)

            @block.scalar
            def _(e):
                replay(e, self.T.eng["act"].prog)

            @block.vector
            def _(e):
                replay(e, self.T.eng["dve"].prog)

            @block.gpsimd
            def _(e):
                replay(e, self.T.eng["pool"].prog)

            @block.sync
            def _(e):
                replay(e, self.T.eng["sp"].prog)
        return nc

    def alloc(self):
        sb = self.sb
        self.colsT = sb("colsT", [128, NCOLS]); self.b_cols = Buf("cols")
        self.ident = sb("ident", [128, 128]); self.b_ident = Buf("ident")
        self.mask = sb("mask", [128, 128]); self.b_mask = Buf("mask")
        self.ones = sb("ones", [128, 128], BF16); self.b_ones = Buf("ones")
        self.ptb = sb("ptb", [128, 16, 128], BF16); self.b_ptb = Buf("ptb")
        self.pwb = sb("pwb", [128, L * 4, 128], BF16); self.b_pwb = Buf("pwb")
        self.wsT = sb("wsT", [128, 2, 8, 128], BF16); self.b_wsT = [Buf("wsT0"), Buf("wsT1")]
        self.biasT = sb("biasT", [128, 2, 4, 128]); self.b_biasT = [Buf("biasT0"), Buf("biasT1")]
        self.lnv = sb("lnv", [128, 2, 2, W]); self.b_lnv = [Buf("lnv0"), Buf("lnv1")]
        self.modT = sb("modT", [128, L, 24, NS + 1]); self.b_modT = [Buf("modT%d" % l) for l in range(L)]
        self.gsP = sb("gsP", [128, L, 8]); self.b_gsP = [Buf("gsP%d" % l) for l in range(L)]
        self.gsS = sb("gsS", [128, L, 8, NS]); self.b_gsS = [Buf("gsS%d" % l) for l in range(L)]
        self.scb = sb("scb", [128, 8, NS + 1], BF16); self.b_scb = Buf("scb")
        self.zcarry = sb("zcarry", [128, L, 4, 2]); self.b_zcarry = [[Buf("zc%d_%d" % (l, m)) for m in range(4)] for l in range(L)]
        self.cxcarry = sb("cxcarry", [128, L, W], BF16); self.b_cxcarry = [Buf("cxc%d" % l) for l in range(L)]
        self.CP = sb("CP", [128, L, 4, NS]); self.b_CP = [Buf("CP%d" % l) for l in range(L)]
        self.SpT = sb("SpT", [128, L, 4, NS]); self.b_SpT = [Buf("SpT%d" % l) for l in range(L)]
        self.ztail = sb("ztail", [128, 4, 2]); self.b_ztail = Buf("ztail")
        self.zsall = sb("zsall", [128, 4, NS]); self.b_zsall = Buf("zsall")
        small = sb("small", [128, 4, 16]); self.small_ring = Ring([(small[:, i, :], Buf("small%d" % i)) for i in range(4)])
        rstd = [sb("rstd%d" % i, [128, W]) for i in range(2)]
        self.rstd = Ring([(rstd[i], Buf("rstd%d" % i)) for i in range(2)])
        self.xT = sb("xT", [128, KC, NTS]); self.b_xT = [[Buf("xT%d_%d" % (k, t)) for t in range(3)] for k in range(KC)]
        self.hT = sb("hT", [128, KC, NTS], BF16); self.b_hT = [[Buf("hT%d_%d" % (k, t)) for t in range(3)] for k in range(KC)]
        self.yb = [sb("y%s" % n, [128, 4, NTS], BF16) for n in "ABC"]
        self.b_yb = [[[Buf("y%s%d_%d" % (n, m, t)) for t in range(3)] for m in range(4)] for n in "ABC"]
        self.mg = sb("mg", [128, KC, NTS], BF16); self.b_mg = [[Buf("mg%d_%d" % (k, t)) for t in range(3)] for k in range(KC)]
        self.slots = [sb("slot%d" % i, [128, 4096], BF16) for i in range(NSLOT)]
        self.b_slots = [Buf("slot%d" % i) for i in range(NSLOT)]
        self.zbuf = sb("zbuf", [128, 2 + SUPTOK]); self.b_zbuf = [Buf("zb_c"), Buf("zb_0"), Buf("zb_1")]
        self.vb = sb("vb", [128, 9, W], BF16); self.b_vb = [Buf("vb%d" % i) for i in range(9)]
        self.cxb = sb("cxb", [128, 8, W], BF16); self.b_cxb = [Buf("cxb%d" % i) for i in range(8)]
        tmps = [sb("tmp%d" % i, [128, W]) for i in range(NTMP)]
        self.tmp = Ring([(tmps[i], Buf("tmp%d" % i)) for i in range(NTMP)])
        tbs = [sb("tbf%d" % i, [128, W], BF16) for i in range(3)]
        self.tbf = Ring([(tbs[i], Buf("tbf%d" % i)) for i in range(3)])
        self.xstage = sb("xstage", [128, D]); self.b_xstage = Buf("xstage")
        self.st32 = sb("st32", [32, 2048]); self.b_st32 = Buf("st32")
        self.pmS = sb("pmS", [128, 4, NS], BF16); self.b_pmS = Buf("pmS")
        self.mixS = sb("mixS", [128, 4, NS]); self.b_mixS = Buf("mixS")
        self.b_d2d = Buf("d2d")
        banks = [self.es.enter_context(self.nc.psum_tensor("bk%d" % i, [128, 512], F32)) for i in range(NBANK)]
        self.banks = Ring([(banks[i], Buf("bk%d" % i)) for i in range(NBANK)])
        if os.environ.get("MK_VERBOSE"):
            print("SBUF bytes remaining/partition:", self.nc.sbuf_bytes_remaining)

    def col(self, off):
        return self.colsT[:, off:off + 1]

    def rsqrt_dve(self, x, xb, y, yb, t, tb, iters=None):
        T = self.T
        I32 = mybir.dt.int32
        if iters is None:
            iters = NEWTON
        T.op("dve", "tensor_single_scalar", dict(out=t.bitcast(I32), in_=x.bitcast(I32), scalar=1, op=ALU.arith_shift_right), reads=[xb], writes=[tb])
        T.op("dve", "tensor_scalar", dict(out=y.bitcast(I32), in0=t.bitcast(I32), scalar1=-1, scalar2=0x5f3759df, op0=ALU.mult, op1=ALU.add),
             reads=[tb], writes=[yb])
        for _ in range(iters):
            T.op("dve", "tensor_tensor", dict(out=t, in0=y, in1=y, op=ALU.mult), reads=[yb], writes=[tb])
            T.op("dve", "scalar_tensor_tensor", dict(out=t, in0=t, scalar=-0.5, in1=x, op0=ALU.mult, op1=ALU.mult), reads=[tb, xb], writes=[tb])
            T.op("dve", "scalar_tensor_tensor", dict(out=y, in0=t, scalar=1.5, in1=y, op0=ALU.add, op1=ALU.mult), reads=[tb, yb], writes=[yb])

    def silu2(self, bank, bank_buf, wd):
        T = self.T
        sg, sgb = self.tmp.next()
        T.op("act", "activation", dict(out=sg[:, 0:wd], in_=bank[:, 0:wd], func=AF.Tanh, scale=0.5), reads=[bank_buf], writes=[sgb])
        T.op("dve", "scalar_tensor_tensor", dict(out=sg[:, 0:wd], in0=sg[:, 0:wd], scalar=1.0, in1=bank[:, 0:wd], op0=ALU.add, op1=ALU.mult),
             reads=[sgb, bank_buf], writes=[sgb])
        return sg, sgb

    def mm_group(self, bank_buf, mms, reads, inc=True):
        def fn(e, mms=mms):
            ins = None
            for (o, lt, r, st, sp, tp) in mms:
                if tp is None:
                    ins = e.matmul(o, lt, r, start=st, stop=sp)
                else:
                    ins = e.matmul(o, lt, r, start=st, stop=sp, tile_position=tp)
            return ins
        self.T.op("pe", fn, None, reads=reads, writes=[bank_buf], inc=inc)

    def tr_group(self, bank_buf, trs, reads):
        def fn(e, trs=trs):
            ins = None
            for (o, i, idn) in trs:
                ins = e.transpose(o, i, idn)
            return ins
        self.T.op("pe", fn, None, reads=list(reads) + [self.b_ident], writes=[bank_buf])

    ADA_AFTER = (0, 1, 2, 3, 4, 7)

    def plan_stream(self):
        order = []
        for sup in range(2):
            for l in range(self.nl):
                if sup == 0 and l == 0:
                    for t in range(NT_ADA):
                        order.append((sup, l, "a", t))
                ta = 0
                for t in range(NT_MAIN):
                    order.append((sup, l, "m", t))
                    if sup == 0 and l + 1 < self.nl and t in self.ADA_AFTER:
                        order.append((0, l + 1, "a", ta))
                        ta += 1
        self.fills = []
        self.fill_pos = {}
        for k, (sup, l, kind, t) in enumerate(order):
            self.fill_pos[(sup, l, kind, t)] = k
            self.fills.append(self.d["wa"][l, t, :, :] if kind == "a" else self.d["wm"][l, t, :, :])
        self.fill_emitted = 0
        self.pending_mod = []

    def use_tile(self, sup, l, kind, t):
        k = self.fill_pos[(sup, l, kind, t)]
        while self.fill_emitted < min(len(self.fills), k + NSLOT - 1):
            j = self.fill_emitted
            s = j % NSLOT
            self.T.dma("pool", self.slots[s][:, :], self.fills[j], buf=self.b_slots[s], writes=[self.b_slots[s]])
            self.fill_emitted += 1
        s = k % NSLOT
        return self.slots[s], self.b_slots[s]

    def emit(self):
        T = self.T
        d = self.d
        self.plan_stream()
        stop = int(os.environ.get("MK_STOP", 10 ** 9))

        class _Stop(Exception):
            pass

        def stage(n):
            if n + 100 * self.cur_sup >= stop:
                raise _Stop()
        self.cur_sup = 0
        try:
            self.startup()
            stage(1)
            for sup in range(2):
                self.cur_sup = sup
                self.tiles = [(0, 512, False), (512, 512, False)] + ([(1024, NS, True)] if sup == 1 else [])
                self.load_x(sup)
                stage(2)
                for l in range(self.nl):
                    self.layer_params(sup, l)
                    stage(3)
                    if sup == 0 and l == 0:
                        self.mod(l)
                    if sup == 0 and l + 1 < self.nl:
                        self.pending_mod = [(l + 1, t) for t in range(NT_ADA)]
                    stage(4)
                    self.rmsnorm(sup, l)
                    stage(5)
                    self.phase_a(sup, l)
                    stage(6)
                    self.phase_b(sup, l)
                    self.phase_c(sup, l)
                    self.phase_b_bug(sup, l)
                    self.phase_b_end(sup)
                    stage(7)
                    self.phase_c_pool(sup, l)
                    stage(8)
                    self.merge(sup, l)
                    stage(9)
                    self.wout(sup, l)
                    stage(10)
                self.final(sup)
                stage(11)
        except _Stop:
            for sb_ in self.b_slots:
                if sb_.sem is not None:
                    T.eng["sp"].prog.append(("wait_ge", dict(sem=T.sems[sb_.sem], val=sb_.total), None))
        T.final_wait("sp")

    def startup(self):
        T = self.T
        d = self.d
        T.dma("sp", self.colsT[:, :], d["cols"][:, :], buf=self.b_cols, writes=[self.b_cols])
        T.dma("sp", self.ident[:, :], d["ident"][:, :], buf=self.b_ident, writes=[self.b_ident])
        T.dma("sp", self.mask[:, :], d["mask"][:, :], buf=self.b_mask, writes=[self.b_mask])
        T.dma("pool", self.ptb[:, :, :], d["pt"][:, :, :], buf=self.b_ptb, writes=[self.b_ptb])
        T.dma("pool", self.pwb[:, :, :], d["poolw"][:, :, :], buf=self.b_pwb, writes=[self.b_pwb])
        T.op("dve", "memset", dict(ap=self.ones[:, :], constant=1.0), writes=[self.b_ones])
        st = self.st32
        T.dma("sp", st[0:NS + 1, 0:D], d["cc"][:, :], buf=self.b_st32, writes=[self.b_st32])
        T.op("act", "activation", dict(out=st[0:NS + 1, 0:D], in_=st[0:NS + 1, 0:D], func=AF.Silu),
             reads=[self.b_st32], writes=[self.b_st32])
        bk, bb = self.banks.next()
        n1 = NS + 1
        self.tr_group(bb, [(bk[:, kc * n1:(kc + 1) * n1], st[0:n1, kc * 128:(kc + 1) * 128], self.ident[0:n1, 0:n1])
                           for kc in range(KC)], [self.b_st32])
        T.op("dve", "tensor_copy", dict(out=self.scb[:, :, :], in_=bk[:, 0:KC * n1].rearrange("p (a b) -> p a b", a=KC)),
             reads=[bb], writes=[self.b_scb])
        for l in range(self.nl):
            T.dma("sp", d["convs"][l, :, 0, :], d["sconv"][l, :, 1, :], buf=self.b_d2d, store=True)
            T.dma("sp", d["pools"][l, :, 0:14, :], d["spool"][l, :, 1:15, :], buf=self.b_d2d, store=True)
        for l in range(self.nl):
            T.dma("sp", st[0:NS, 0:2 * W], d["sconv"][l].rearrange("n r c -> n (r c)"), buf=self.b_st32, writes=[self.b_st32])
            bk, bb = self.banks.next()
            self.tr_group(bb, [(bk[:, (r * 4 + m) * NS:(r * 4 + m + 1) * NS], st[0:NS, r * W + m * 128:r * W + (m + 1) * 128],
                                self.ident[0:NS, 0:NS]) for r in range(2) for m in range(4)], [self.b_st32])
            for m in range(4):
                w0 = self.col(O_CW + (l * 3 + 0) * 4 + m)
                w1 = self.col(O_CW + (l * 3 + 1) * 4 + m)
                cb = self.col(O_CB + l * 4 + m)
                T.op("dve", "tensor_scalar", dict(
                    out=self.CP[:, l, m, :], in0=bk[:, m * NS:(m + 1) * NS], scalar1=w0, scalar2=cb, op0=ALU.mult, op1=ALU.add),
                    reads=[bb, self.b_cols], writes=[self.b_CP[l]])
                T.op("dve", "scalar_tensor_tensor", dict(
                    out=self.CP[:, l, m, :], in0=bk[:, (4 + m) * NS:(5 + m) * NS], scalar=w1, in1=self.CP[:, l, m, :],
                    op0=ALU.mult, op1=ALU.add), reads=[bb, self.b_cols, self.b_CP[l]], writes=[self.b_CP[l]])
        for l in range(self.nl):
            sp_t, sp_b = self.tmp.next()
            for g, w in enumerate(POOL_WIN):
                nr = w - 1
                T.dma("sp", st[0:NS, 0:nr * 128].rearrange("p (r c) -> p r c", r=nr),
                      d["spool"][l, :, 15 - nr:15, g * 128:(g + 1) * 128], buf=self.b_st32, writes=[self.b_st32])
                T.op("dve", "tensor_reduce", dict(
                    out=sp_t[0:NS, g * 128:(g + 1) * 128], in_=st[0:NS, 0:nr * 128].rearrange("p (r c) -> p c r", r=nr),
                    axis=AX.X, op=ALU.add), reads=[self.b_st32], writes=[sp_b])
            bk, bb = self.banks.next()
            self.tr_group(bb, [(bk[:, g * NS:(g + 1) * NS], sp_t[0:NS, g * 128:(g + 1) * 128], self.ident[0:NS, 0:NS])
                               for g in range(4)], [sp_b])
            for g, w in enumerate(POOL_WIN):
                T.op("dve", "tensor_scalar", dict(
                    out=self.SpT[:, l, g, :], in0=bk[:, g * NS:(g + 1) * NS], scalar1=1.0 / w, scalar2=None, op0=ALU.mult),
                    reads=[bb], writes=[self.b_SpT[l]])
        T.op("dve", "memset", dict(ap=self.zcarry[:, :, :, :], constant=0.0), writes=[b for r in self.b_zcarry for b in r])

    def load_x(self, sup):
        T = self.T
        d = self.d
        for blk in range(8):
            r0 = (sup * 8 + blk) * 128
            ti = blk // 4
            for half in range(2):
                xs_, xsb_ = self.tmp.next()
                T.dma("sp", xs_[:, :], d["xp"][r0:r0 + 128, half * 512:(half + 1) * 512], buf=xsb_, writes=[xsb_])
                bk, bb = self.banks.next()
                self.tr_group(bb, [(bk[:, i * 128:(i + 1) * 128], xs_[:, i * 128:(i + 1) * 128], self.ident[:, :]) for i in range(4)], [xsb_])
                out = self.xT[:, half * 4:(half + 1) * 4, blk * 128:(blk + 1) * 128]
                src = bk[:, :].rearrange("p (a b) -> p a b", a=4)
                wr = [self.b_xT[half * 4 + i][ti] for i in range(4)]
                if half == 0:
                    T.op("act", "activation", dict(out=out, in_=src, func=AF.Copy), reads=[bb], writes=wr)
                else:
                    T.op("dve", "tensor_copy", dict(out=out, in_=src), reads=[bb], writes=wr)
        if sup == 1:
            st = self.st32
            T.dma("sp", st[0:NS, 0:D], d["xs"][:, :], buf=self.b_st32, writes=[self.b_st32])
            bk, bb = self.banks.next()
            self.tr_group(bb, [(bk[:, kc * NS:(kc + 1) * NS], st[0:NS, kc * 128:(kc + 1) * 128], self.ident[0:NS, 0:NS])
                               for kc in range(KC)], [self.b_st32])
            T.op("dve", "tensor_copy", dict(out=self.xT[:, :, SUPTOK:NTS], in_=bk[:, 0:KC * NS].rearrange("p (a b) -> p a b", a=KC)),
                 reads=[bb], writes=[self.b_xT[k][2] for k in range(KC)])

    def layer_params(self, sup, l):
        T = self.T
        d = self.d
        r = (sup * self.nl + l) % 2
        self.pr = r
        for i in range(2):
            T.dma("sp", self.lnv[:, r, i, :], d["lnv"][l, i:i + 1, :].partition_broadcast(128), buf=self.b_lnv[r], writes=[self.b_lnv[r]], multi=(i == 1))
        T.dma("sp", self.biasT[:, r, :, :], d["biasT"][:, l, :, :], buf=self.b_biasT[r], writes=[self.b_biasT[r]])
        T.dma("sp", self.xstage[:, :].rearrange("p (h s) -> p h s", h=8), d["sguw"][l].rearrange("h t s -> t h s"),
              buf=self.b_xstage, writes=[self.b_xstage])
        for half in range(2):
            bk, bb = self.banks.next()
            self.tr_group(bb, [(bk[:, i * 128:(i + 1) * 128], self.xstage[:, (half * 4 + i) * 128:(half * 4 + i + 1) * 128],
                                self.ident[:, :]) for i in range(4)], [self.b_xstage])
            T.op("dve", "tensor_tensor", dict(
                out=self.wsT[:, r, half * 4:(half + 1) * 4, :], in0=bk[:, :].rearrange("p (a b) -> p a b", a=4),
                in1=self.mask[:, :].unsqueeze(1).broadcast_to([128, 4, 128]), op=ALU.mult),
                reads=[bb, self.b_mask], writes=[self.b_wsT[r]])

    def mod_tile(self, l, t):
        T = self.T
        n1 = NS + 1
        bk, bb = self.banks.next()
        slot, sbuf = self.use_tile(0, l, "a", t)
        for c in range(4):
            mms = [(bk[:, c * n1:(c + 1) * n1], slot[:, (kc * 4 + c) * 128:(kc * 4 + c + 1) * 128], self.scb[:, kc, :],
                    kc == 0, kc == KC - 1, None) for kc in range(KC)]
            self.mm_group(bb, mms, [sbuf, self.b_scb])
        T.op("dve", "tensor_tensor", dict(
            out=self.modT[:, l, 4 * t:4 * t + 4, :], in0=bk[:, 0:4 * n1].rearrange("p (a b) -> p a b", a=4),
            in1=self.colsT[:, O_BA + l * 24 + 4 * t:O_BA + l * 24 + 4 * t + 4].unsqueeze(2).broadcast_to([128, 4, n1]), op=ALU.add),
            reads=[bb, self.b_cols], writes=[self.b_modT[l]])
        if t == NT_ADA - 1:
            self.mod_finish(l)

    def mod_step(self):
        if self.pending_mod:
            l, t = self.pending_mod.pop(0)
            self.mod_tile(l, t)

    def mod(self, l):
        for t in range(NT_ADA):
            self.mod_tile(l, t)

    def mod_finish(self, l):
        T = self.T
        n1 = NS + 1
        T.op("dve", "tensor_scalar", dict(out=self.modT[:, l, 16:24, :], in0=self.modT[:, l, 16:24, :], scalar1=0.5, scalar2=None, op0=ALU.mult),
             reads=[self.b_modT[l]], writes=[self.b_modT[l]])
        T.op("dve", "scalar_tensor_tensor", dict(
            out=self.gsP[:, l, :], in0=self.modT[:, l, 8:16, 0], scalar=1.0, in1=self.colsT[:, O_NG + l * 8:O_NG + (l + 1) * 8],
            op0=ALU.add, op1=ALU.mult), reads=[self.b_modT[l], self.b_cols], writes=[self.b_gsP[l]])
        T.op("dve", "scalar_tensor_tensor", dict(
            out=self.gsS[:, l, :, :], in0=self.modT[:, l, 8:16, 1:n1], scalar=1.0,
            in1=self.colsT[:, O_NG + l * 8:O_NG + (l + 1) * 8].unsqueeze(2).broadcast_to([128, 8, NS]),
            op0=ALU.add, op1=ALU.mult), reads=[self.b_modT[l], self.b_cols], writes=[self.b_gsS[l]])

    def rms_rstd(self, ti, c0, wd):
        T = self.T
        bk, bb = self.banks.next()
        for kc in range(KC):
            sq, sqb = self.tbf.next()
            T.op("act", "activation", dict(out=sq[:, 0:wd], in_=self.xT[:, kc, c0:c0 + wd], func=AF.Square),
                 reads=[self.b_xT[kc][ti]], writes=[sqb])
            self.mm_group(bb, [(bk[:, 0:wd], self.ones[:, :], sq[:, 0:wd], kc == 0, kc == KC - 1, None)], [sqb, self.b_ones])
        rt, rtb = self.rstd.next()
        xs, xsb = self.tmp.next()
        ts, tsb = self.tmp.next()
        T.op("dve", "tensor_scalar", dict(out=xs[:, 0:wd], in0=bk[:, 0:wd], scalar1=1.0 / D, scalar2=EPS, op0=ALU.mult, op1=ALU.add),
             reads=[bb], writes=[xsb])
        self.rsqrt_dve(xs[:, 0:wd], xsb, rt[:, 0:wd], rtb, ts[:, 0:wd], tsb, iters=RMS_NEWTON)
        return rt, rtb

    def rmsnorm(self, sup, l):
        T = self.T
        for ti, (c0, wd, samp) in enumerate(self.tiles):
            rt, rtb = self.rms_rstd(ti, c0, wd)
            if not samp:
                for kc in range(KC):
                    t, tb = self.tmp.next()
                    T.op("dve", "scalar_tensor_tensor", dict(
                        out=t[:, 0:wd], in0=self.xT[:, kc, c0:c0 + wd], scalar=self.gsP[:, l, kc:kc + 1], in1=rt[:, 0:wd],
                        op0=ALU.mult, op1=ALU.mult), reads=[self.b_xT[kc][ti], self.b_gsP[l], rtb], writes=[tb])
                    T.op("act", "activation", dict(
                        out=self.hT[:, kc, c0:c0 + wd], in_=t[:, 0:wd], func=AF.Identity, bias=self.modT[:, l, kc, 0:1], scale=1.0),
                        reads=[tb, self.b_modT[l]], writes=[self.b_hT[kc][ti]])
            else:
                t, tb = self.tmp.next()
                tv = t[:, 0:KC * NS].rearrange("p (a b) -> p a b", a=KC)
                T.op("dve", "tensor_tensor", dict(
                    out=tv, in0=self.xT[:, :, c0:c0 + wd], in1=rt[:, 0:wd].unsqueeze(1).broadcast_to([128, KC, NS]), op=ALU.mult),
                    reads=[self.b_xT[k][ti] for k in range(KC)] + [rtb], writes=[tb])
                T.op("dve", "tensor_tensor", dict(out=tv, in0=tv, in1=self.gsS[:, l, :, :], op=ALU.mult),
                     reads=[tb, self.b_gsS[l]], writes=[tb])
                T.op("dve", "tensor_tensor", dict(out=self.hT[:, :, c0:c0 + wd], in0=tv, in1=self.modT[:, l, 0:8, 1:NS + 1], op=ALU.add),
                     reads=[tb, self.b_modT[l]], writes=[self.b_hT[k][ti] for k in range(KC)])

    def proj_group(self, slot, sbuf, blk_of_kc, ti, c0, wd, bank=None):
        if bank is None:
            bank = self.banks.next()
        bk, bb = bank
        mms = [(bk[:, 0:wd], slot[:, blk_of_kc(kc) * 128:(blk_of_kc(kc) + 1) * 128], self.hT[:, kc, c0:c0 + wd],
                kc == 0, kc == KC - 1, None) for kc in range(KC)]
        self.mm_group(bb, mms, [sbuf] + [self.b_hT[kc][ti] for kc in range(KC)])
        return bk, bb

    def phase_a(self, sup, l):
        T = self.T
        d = self.d
        zb = self.zbuf
        for m in range(4):
            slot, sbuf = self.use_tile(sup, l, "m", T_A + m)
            w0 = self.col(O_CW + (l * 3 + 0) * 4 + m)
            w1 = self.col(O_CW + (l * 3 + 1) * 4 + m)
            w2 = self.col(O_CW + (l * 3 + 2) * 4 + m)
            cb = self.col(O_CB + l * 4 + m)
            T.op("dve", "tensor_copy", dict(out=zb[:, 0:2], in_=self.zcarry[:, l, m, :]),
                 reads=[self.b_zcarry[l][m]], writes=[self.b_zbuf[0]])
            for ti, (c0, wd, samp) in enumerate(self.tiles):
                if samp:
                    bk, bb = self.banks.next()
                    for c in range(4):
                        mms = [(bk[:, c * NS:(c + 1) * NS], slot[:, (kc * 4 + c) * 128:(kc * 4 + c + 1) * 128], self.hT[:, kc, c0:c0 + wd],
                                kc == 0, kc == KC - 1, None) for kc in range(KC)]
                        self.mm_group(bb, mms, [sbuf] + [self.b_hT[kc][ti] for kc in range(KC)])
                    s_, sb_ = self.tmp.next()
                    T.op("act", "activation", dict(out=s_[:, 0:4 * NS], in_=bk[:, 0:4 * NS], func=AF.Copy), reads=[bb], writes=[sb_])
                    T.op("act", "activation", dict(out=s_[:, 4 * NS:5 * NS], in_=s_[:, 3 * NS:4 * NS], func=AF.Tanh, scale=0.5), reads=[sb_], writes=[sb_])
                    T.op("dve", "scalar_tensor_tensor", dict(out=s_[:, 4 * NS:5 * NS], in0=s_[:, 4 * NS:5 * NS], scalar=1.0, in1=s_[:, 3 * NS:4 * NS],
                                                             op0=ALU.add, op1=ALU.mult), reads=[sb_], writes=[sb_])
                    T.op("dve", "tensor_tensor", dict(out=self.zsall[:, m, :], in0=s_[:, NS:2 * NS], in1=s_[:, 2 * NS:3 * NS], op=ALU.mult),
                         reads=[sb_], writes=[self.b_zsall])
                    T.op("dve", "scalar_tensor_tensor", dict(out=s_[:, 5 * NS:6 * NS], in0=self.zsall[:, m, :], scalar=w2, in1=self.CP[:, l, m, :],
                                                             op0=ALU.mult, op1=ALU.add), reads=[self.b_zsall, self.b_CP[l], self.b_cols], writes=[sb_])
                    T.op("dve", "tensor_tensor", dict(out=s_[:, 5 * NS:6 * NS], in0=s_[:, 0:NS], in1=s_[:, 5 * NS:6 * NS], op=ALU.mult), reads=[sb_], writes=[sb_])
                    T.op("dve", "scalar_tensor_tensor", dict(out=self.yb[0][:, m, c0:c0 + wd], in0=s_[:, 5 * NS:6 * NS], scalar=0.5, in1=s_[:, 4 * NS:5 * NS],
                                                             op0=ALU.mult, op1=ALU.mult), reads=[sb_], writes=[self.b_yb[0][m][ti]])
                    continue
                pb = [self.proj_group(slot, sbuf, (lambda kc, c=c: kc * 4 + c), ti, c0, wd) for c in range(4)]
                (b_ab, bb_ab), (b_ac, bb_ac), (b_ah, bb_ah), (b_ag, bb_ag) = pb
                sg, sgb = self.silu2(b_ag, bb_ag, wd)
                ah, ahb = self.tmp.next()
                T.op("act", "activation", dict(out=ah[:, 0:wd], in_=b_ah[:, 0:wd], func=AF.Copy),
                     reads=[bb_ah], writes=[ahb])
                acc, accb = self.tmp.next()
                if not samp:
                    zcur = self.b_zbuf[1 + ti]
                    zprev = self.b_zbuf[ti]
                    T.op("dve", "tensor_tensor", dict(out=zb[:, 2 + c0:2 + c0 + wd], in0=b_ac[:, 0:wd], in1=ah[:, 0:wd], op=ALU.mult),
                         reads=[bb_ac, ahb], writes=[zcur])
                    T.op("act", "activation", dict(out=acc[:, 0:wd], in_=zb[:, 2 + c0:2 + c0 + wd], func=AF.Identity, bias=cb, scale=w2),
                         reads=[zcur, self.b_cols], writes=[accb])
                    T.op("pool" if "conv" in OFFLOAD else "dve", "scalar_tensor_tensor", dict(out=acc[:, 0:wd], in0=zb[:, 1 + c0:1 + c0 + wd], scalar=w1, in1=acc[:, 0:wd],
                                                                          op0=ALU.mult, op1=ALU.add),
                         reads=[zcur, zprev, accb, self.b_cols], writes=[accb])
                    T.op("pool" if "conv" in OFFLOAD else "dve", "scalar_tensor_tensor", dict(out=acc[:, 0:wd], in0=zb[:, c0:c0 + wd], scalar=w0, in1=acc[:, 0:wd],
                                                                          op0=ALU.mult, op1=ALU.add),
                         reads=[zcur, zprev, accb, self.b_cols], writes=[accb])
                else:
                    T.op("dve", "tensor_tensor", dict(out=self.zsall[:, m, :], in0=b_ac[:, 0:wd], in1=ah[:, 0:wd], op=ALU.mult),
                         reads=[bb_ac, ahb], writes=[self.b_zsall])
                    T.op("dve", "scalar_tensor_tensor", dict(out=acc[:, 0:wd], in0=self.zsall[:, m, :], scalar=w2, in1=self.CP[:, l, m, :],
                                                                               op0=ALU.mult, op1=ALU.add),
                         reads=[self.b_zsall, self.b_CP[l], self.b_cols], writes=[accb])
                T.op("dve", "tensor_tensor", dict(out=acc[:, 0:wd], in0=b_ab[:, 0:wd], in1=acc[:, 0:wd], op=ALU.mult),
                     reads=[bb_ab, accb], writes=[accb])
                T.op("dve", "scalar_tensor_tensor", dict(out=self.yb[0][:, m, c0:c0 + wd], in0=acc[:, 0:wd], scalar=0.5, in1=sg[:, 0:wd], op0=ALU.mult, op1=ALU.mult),
                     reads=[accb, sgb], writes=[self.b_yb[0][m][ti]])
            if sup == 0:
                self.mod_step()
                T.op("dve", "tensor_copy", dict(out=self.zcarry[:, l, m, :], in_=zb[:, SUPTOK:SUPTOK + 2]),
                     reads=[self.b_zbuf[2]], writes=[self.b_zcarry[l][m]])
            else:
                T.op("dve", "tensor_copy", dict(out=self.ztail[:, m, :], in_=zb[:, SUPTOK:SUPTOK + 2]),
                     reads=[self.b_zbuf[2]], writes=[self.b_ztail])
        if sup == 1 and "a_out" not in os.environ.get("MK_SKIP", ""):
            bk, bb = self.banks.next()
            self.tr_group(bb, [(bk[0:2, m * 128:(m + 1) * 128], self.ztail[:, m, :], self.ident[:, :]) for m in range(4)], [self.b_ztail])
            o, ob = self.tmp.next()
            T.op("act", "activation", dict(out=o[0:2, :], in_=bk[0:2, :], func=AF.Copy), reads=[bb], writes=[ob])
            T.dma("sp", d["convp"][l, :, :], o[0:2, :], buf=ob, reads=[ob], store=True)
            bk, bb = self.banks.next()
            self.tr_group(bb, [(bk[0:NS, m * 128:(m + 1) * 128], self.zsall[:, m, :], self.ident[:, :]) for m in range(4)], [self.b_zsall])
            o, ob = self.tmp.next()
            T.op("act", "activation", dict(out=o[0:NS, :], in_=bk[0:NS, :], func=AF.Copy), reads=[bb], writes=[ob])
            T.dma("sp", d["convs"][l, :, 1, :], o[0:NS, :], buf=ob, reads=[ob], store=True)

    def phase_b(self, sup, l):
        T = self.T
        d = self.d
        r = self.pr
        slot, sbuf = self.use_tile(sup, l, "m", T_BV)
        SKIP = os.environ.get("MK_SKIP", "").split(",")
        nblk = 9 if (sup == 1 and "b_samp" not in SKIP) else 8
        st_ = {}

        def part1(blk):
            samp = blk == 8
            np_ = NS if samp else 128
            c0 = SUPTOK if samp else blk * 128
            ti = 2 if samp else blk // 4
            bk, bb = self.banks.next()
            mms = [(bk[0:np_, :], self.hT[:, kc, c0:c0 + np_], slot[:, kc * 512:(kc + 1) * 512], kc == 0, kc == KC - 1, None) for kc in range(KC)]
            self.mm_group(bb, mms, [sbuf] + [self.b_hT[kc][ti] for kc in range(KC)])
            g, gb = self.tmp.next()
            T.op("act", "activation", dict(out=g[0:np_, :], in_=bk[0:np_, :], func=AF.Gelu_apprx_tanh), reads=[bb], writes=[gb])
            sm, smb = self.small_ring.next()
            T.op("dve", "bn_stats", dict(out=sm[0:np_, 0:6], in_=g[0:np_, :]), reads=[gb], writes=[smb])
            T.op("dve", "bn_aggr", dict(out=sm[0:np_, 8:10], in_=sm[0:np_, 0:6]), reads=[smb], writes=[smb])
            T.op("dve", "tensor_scalar", dict(out=sm[0:np_, 10:11], in0=sm[0:np_, 9:10], scalar1=EPS, scalar2=None, op0=ALU.add), reads=[smb], writes=[smb])
            self.rsqrt_dve(sm[0:np_, 10:11], smb, sm[0:np_, 11:12], smb, sm[0:np_, 12:13], smb)
            T.op("dve", "scalar_tensor_tensor", dict(out=sm[0:np_, 13:14], in0=sm[0:np_, 8:9], scalar=-1.0, in1=sm[0:np_, 11:12], op0=ALU.mult, op1=ALU.mult),
                 reads=[smb], writes=[smb])
            T.op("act", "activation", dict(out=g[0:np_, :], in_=g[0:np_, :], func=AF.Identity, bias=sm[0:np_, 13:14], scale=sm[0:np_, 11:12]),
                 reads=[gb, smb], writes=[gb])
            st_[blk] = (samp, np_, g, gb)

        def part2(blk):
            samp, np_, g, gb = st_.pop(blk)
            T.op("pool" if "ln" in OFFLOAD else "dve", "tensor_tensor", dict(out=g[0:np_, :], in0=g[0:np_, :], in1=self.lnv[0:np_, r, 0, :], op=ALU.mult),
                 reads=[gb, self.b_lnv[r]], writes=[gb])
            T.op("pool" if "ln" in OFFLOAD else "dve", "tensor_tensor", dict(out=g[0:np_, :], in0=g[0:np_, :], in1=self.lnv[0:np_, r, 1, :], op=ALU.add),
                 reads=[gb, self.b_lnv[r]], writes=[gb])
            T.op("act", "activation", dict(out=self.vb[0:np_, blk, :], in_=g[0:np_, :], func=AF.Copy), reads=[gb], writes=[self.b_vb[blk]])
            if sup == 1 and blk == 7 and "b_vp" not in SKIP:
                T.dma("sp", d["vp"][l, :, :], g[:, :], buf=gb, reads=[gb], store=True)
            if samp and "b_vs" not in SKIP:
                T.dma("sp", d["vs"][l, :, :], g[0:NS, :], buf=gb, reads=[gb], store=True)
            if samp and "b_tr" not in SKIP:
                bk2, bb2 = self.banks.next()
                self.tr_group(bb2, [(bk2[:, m * NS:(m + 1) * NS], g[0:NS, m * 128:(m + 1) * 128], self.ident[0:NS, 0:NS]) for m in range(4)], [gb])
                for m in range(4):
                    T.op("dve", "tensor_scalar", dict(
                        out=self.mixS[:, m, :], in0=bk2[:, m * NS:(m + 1) * NS], scalar1=self.col(O_SA + l * 4 + m), scalar2=self.col(O_SB + l * 4 + m),
                        op0=ALU.mult, op1=ALU.add), reads=[bb2, self.b_cols], writes=[self.b_mixS])

        for blk in range(nblk + 1):
            if blk < nblk:
                part1(blk)
            if blk >= 1:
                part2(blk - 1)
        if sup == 0:
            self.mod_step()

    def phase_b_bug(self, sup, l):
        T = self.T
        d = self.d
        r = self.pr
        for m in range(4):
            slot, sbuf = self.use_tile(sup, l, "m", T_BUG + m // 2)
            mm_ = m % 2
            for ti, (c0, wd, samp) in enumerate(self.tiles):
                if samp:
                    bk, bb = self.banks.next()
                    for c in range(2):
                        mms = [(bk[:, c * NS:(c + 1) * NS], slot[:, ((mm_ * 2 + c) * 8 + kc) * 128:((mm_ * 2 + c) * 8 + kc + 1) * 128],
                                self.hT[:, kc, c0:c0 + wd], kc == 0, kc == KC - 1, None) for kc in range(KC)]
                        self.mm_group(bb, mms, [sbuf] + [self.b_hT[kc][ti] for kc in range(KC)])
                    s_, sb_ = self.tmp.next()
                    T.op("act", "activation", dict(out=s_[:, 0:NS], in_=bk[:, 0:NS], func=AF.Gelu_apprx_tanh), reads=[bb], writes=[sb_])
                    T.op("act", "activation", dict(out=s_[:, NS:2 * NS], in_=bk[:, NS:2 * NS], func=AF.Copy), reads=[bb], writes=[sb_])
                    T.op("act", "activation", dict(out=s_[:, 2 * NS:3 * NS], in_=s_[:, NS:2 * NS], func=AF.Tanh, scale=0.5), reads=[sb_], writes=[sb_])
                    T.op("dve", "scalar_tensor_tensor", dict(out=s_[:, 2 * NS:3 * NS], in0=s_[:, 2 * NS:3 * NS], scalar=1.0, in1=s_[:, NS:2 * NS],
                                                             op0=ALU.add, op1=ALU.mult), reads=[sb_], writes=[sb_])
                    T.op("dve", "tensor_tensor", dict(out=s_[:, 3 * NS:4 * NS], in0=self.mixS[:, m, :], in1=s_[:, 0:NS], op=ALU.mult),
                         reads=[self.b_mixS, sb_], writes=[sb_])
                    T.op("dve", "scalar_tensor_tensor", dict(out=self.yb[1][:, m, c0:c0 + wd], in0=s_[:, 3 * NS:4 * NS], scalar=0.5, in1=s_[:, 2 * NS:3 * NS],
                                                             op0=ALU.mult, op1=ALU.mult), reads=[sb_], writes=[self.b_yb[1][m][ti]])
                    continue
                b_u, bb_u = self.proj_group(slot, sbuf, (lambda kc: (mm_ * 2 + 0) * 8 + kc), ti, c0, wd)
                b_g, bb_g = self.proj_group(slot, sbuf, (lambda kc: (mm_ * 2 + 1) * 8 + kc), ti, c0, wd)
                u, ub = self.tmp.next()
                T.op("act", "activation", dict(out=u[:, 0:wd], in_=b_u[:, 0:wd], func=AF.Gelu_apprx_tanh), reads=[bb_u], writes=[ub])
                sg, sgb = self.silu2(b_g, bb_g, wd)
                if not samp:
                    bk, bb = self.banks.next()
                    mms = []
                    for bi in range(4):
                        blk = ti * 4 + bi
                        for hh in range(2):
                            h = 2 * m + hh
                            mms.append((bk[hh * 64:(hh + 1) * 64, bi * 128:(bi + 1) * 128], self.vb[:, blk, h * 64:(h + 1) * 64],
                                        self.wsT[:, r, h, :], True, True, (0, hh * 64)))
                    self.mm_group(bb, mms, [self.b_vb[ti * 4 + bi] for bi in range(4)] + [self.b_wsT[r]])
                    t, tb = self.tmp.next()
                    T.op("dve", "tensor_tensor", dict(
                        out=t[:, :].rearrange("p (a b) -> p a b", a=4), in0=bk[:, :].rearrange("p (a b) -> p a b", a=4),
                        in1=self.biasT[:, r, m, :].unsqueeze(1).broadcast_to([128, 4, 128]), op=ALU.add),
                        reads=[bb, self.b_biasT[r]], writes=[tb])
                    T.op("pool" if "bug" in OFFLOAD else "dve", "tensor_tensor", dict(out=t[:, 0:wd], in0=t[:, 0:wd], in1=u[:, 0:wd], op=ALU.mult), reads=[tb, ub], writes=[tb])
                else:
                    t, tb = self.tmp.next()
                    T.op("dve", "tensor_tensor", dict(out=t[:, 0:wd], in0=self.mixS[:, m, :], in1=u[:, 0:wd], op=ALU.mult),
                         reads=[self.b_mixS, ub], writes=[tb])
                T.op("dve", "scalar_tensor_tensor", dict(out=self.yb[1][:, m, c0:c0 + wd], in0=t[:, 0:wd], scalar=0.5, in1=sg[:, 0:wd], op0=ALU.mult, op1=ALU.mult),
                     reads=[tb, sgb], writes=[self.b_yb[1][m][ti]])

    def phase_b_end(self, sup):
        if sup == 0:
            self.mod_step()

    def phase_c(self, sup, l):
        T = self.T
        d = self.d
        slot, sbuf = self.use_tile(sup, l, "m", T_CX)
        for blk in range(8):
            ti = blk // 4
            c0 = blk * 128
            bk, bb = self.banks.next()
            mms = [(bk[:, :], self.hT[:, kc, c0:c0 + 128], slot[:, kc * 512:(kc + 1) * 512], kc == 0, kc == KC - 1, None) for kc in range(KC)]
            self.mm_group(bb, mms, [sbuf] + [self.b_hT[kc][ti] for kc in range(KC)])
            T.op("act", "activation", dict(out=self.cxb[:, blk, :], in_=bk[:, :], func=AF.Copy), reads=[bb], writes=[self.b_cxb[blk]])
            if sup == 1 and blk == 7:
                bk3, bb3 = self.banks.next()
                mms = [(bk3[0:15, :], self.hT[:, kc, SUPTOK - 15:SUPTOK], slot[:, kc * 512:(kc + 1) * 512], kc == 0, kc == KC - 1, None) for kc in range(KC)]
                self.mm_group(bb3, mms, [sbuf] + [self.b_hT[kc][1] for kc in range(KC)])
                o, ob = self.tmp.next()
                T.op("dve", "tensor_copy", dict(out=o[0:15, :], in_=bk3[0:15, :]), reads=[bb3], writes=[ob])
                T.dma("sp", d["poolp"][l, :, :], o[0:15, :], buf=ob, reads=[ob], store=True)
        if sup == 1:
            c0 = SUPTOK
            bk, bb = self.banks.next()
            for g in range(4):
                mms = [(bk[:, g * NS:(g + 1) * NS], slot[:, (kc * 4 + g) * 128:(kc * 4 + g + 1) * 128], self.hT[:, kc, c0:c0 + NS],
                        kc == 0, kc == KC - 1, None) for kc in range(KC)]
                self.mm_group(bb, mms, [sbuf] + [self.b_hT[kc][2] for kc in range(KC)])
            xc, xcb = self.tmp.next()
            T.op("act", "activation", dict(out=xc[:, 0:4 * NS], in_=bk[:, 0:4 * NS], func=AF.Copy), reads=[bb], writes=[xcb])
            for g, w in enumerate(POOL_WIN):
                T.op("dve", "scalar_tensor_tensor", dict(
                    out=self.pmS[:, g, :], in0=xc[:, g * NS:(g + 1) * NS], scalar=1.0 / w - 1.0, in1=self.SpT[:, l, g, :], op0=ALU.mult, op1=ALU.add),
                    reads=[xcb, self.b_SpT[l]], writes=[self.b_pmS])
            bk2, bb2 = self.banks.next()
            self.tr_group(bb2, [(bk2[0:NS, g * 128:(g + 1) * 128], xc[:, g * NS:(g + 1) * NS], self.ident[:, :]) for g in range(4)], [xcb])
            o, ob = self.tmp.next()
            T.op("act", "activation", dict(out=o[0:NS, :], in_=bk2[0:NS, :], func=AF.Copy), reads=[bb2], writes=[ob])
            T.dma("sp", d["pools"][l, :, 14, :], o[0:NS, :], buf=ob, reads=[ob], store=True)

    def phase_c_pool(self, sup, l):
        T = self.T
        d = self.d
        slot, sbuf = self.use_tile(sup, l, "m", T_CG)
        for g in range(4):
            for ti, (c0, wd, samp) in enumerate(self.tiles):
                pm, pmb = self.tbf.next()
                if not samp:
                    bk, bb = self.banks.next()
                    mms = []
                    rd = [self.b_ptb]
                    for bi in range(4):
                        blk = ti * 4 + bi
                        o = bk[:, bi * 128:(bi + 1) * 128]
                        cur = self.cxb[:, blk, g * 128:(g + 1) * 128]
                        rd.append(self.b_cxb[blk])
                        if sup == 0 and blk == 0:
                            mms.append((o, cur, self.ptb[:, g * 4 + 2, :], True, False, None))
                            mms.append((o, cur, self.ptb[:, g * 4 + 3, :], False, True, None))
                        else:
                            if blk == 0:
                                prev = self.cxcarry[64:128, l, g * 128:(g + 1) * 128]
                                rd.append(self.b_cxcarry[l])
                            else:
                                prev = self.cxb[64:128, blk - 1, g * 128:(g + 1) * 128]
                                rd.append(self.b_cxb[blk - 1])
                            mms.append((o, cur, self.ptb[:, g * 4 + 0, :], True, False, None))
                            mms.append((o, prev, self.ptb[64:128, g * 4 + 1, :], False, True, None))
                    self.mm_group(bb, mms, rd)
                    T.op("act", "activation", dict(out=pm[:, 0:wd], in_=bk[:, 0:wd], func=AF.Copy), reads=[bb], writes=[pmb])
                    rhs = pm[:, 0:wd]
                    rhs_b = pmb
                else:
                    rhs = self.pmS[:, g, :]
                    rhs_b = self.b_pmS
                b_g, bb_g = self.proj_group(slot, sbuf, (lambda kc, g=g: g * 8 + kc), ti, c0, wd)
                byc, bbyc = self.banks.next()
                self.mm_group(bbyc, [(byc[:, 0:wd], self.pwb[:, l * 4 + g, :], rhs, True, True, None)], [self.b_pwb, rhs_b])
                sg, sgb = self.silu2(b_g, bb_g, wd)
                t, tb = self.tmp.next()
                T.op("dve", "tensor_scalar", dict(
                    out=t[:, 0:wd], in0=byc[:, 0:wd], scalar1=self.col(O_PB + l * 4 + g), scalar2=self.col(O_PS + l * 4 + g), op0=ALU.add, op1=ALU.mult),
                    reads=[bbyc, self.b_cols], writes=[tb])
                T.op("dve", "scalar_tensor_tensor", dict(out=self.yb[2][:, g, c0:c0 + wd], in0=t[:, 0:wd], scalar=0.5, in1=sg[:, 0:wd], op0=ALU.mult, op1=ALU.mult),
                     reads=[tb, sgb], writes=[self.b_yb[2][g][ti]])
        if sup == 0:
            T.op("dve", "tensor_copy", dict(out=self.cxcarry[:, l, :], in_=self.cxb[:, 7, :]), reads=[self.b_cxb[7]], writes=[self.b_cxcarry[l]])

    def merge(self, sup, l):
        T = self.T
        for j in range(KC):
            for ti, (c0, wd, samp) in enumerate(self.tiles):
                acc, accb = self.tmp.next()
                for br in range(3):
                    base = j * 36 + br * 12
                    bk, bb = self.banks.next()
                    mms = []
                    rd = []
                    for kc in range(KC):
                        idx = base + kc
                        slot, sbuf = self.use_tile(sup, l, "m", T_MRG + idx // 32)
                        p = idx % 32
                        mms.append((bk[:, 0:wd], slot[:, p * 128:(p + 1) * 128], self.hT[:, kc, c0:c0 + wd], kc == 0, kc == KC - 1, None))
                        if sbuf not in rd:
                            rd.append(sbuf)
                    self.mm_group(bb, mms, rd + [self.b_hT[kc][ti] for kc in range(KC)])
                    sgt, sgtb = self.tmp.next()
                    T.op("act", "activation", dict(out=sgt[:, 0:wd], in_=bk[:, 0:wd], func=AF.Tanh, scale=0.5), reads=[bb], writes=[sgtb])
                    bp, bbp = self.banks.next()
                    mms = []
                    rd = []
                    for k4 in range(4):
                        idx = base + 8 + k4
                        slot, sbuf = self.use_tile(sup, l, "m", T_MRG + idx // 32)
                        p = idx % 32
                        mms.append((bp[:, 0:wd], slot[:, p * 128:(p + 1) * 128], self.yb[br][:, k4, c0:c0 + wd], k4 == 0, k4 == 3, None))
                        if sbuf not in rd:
                            rd.append(sbuf)
                    self.mm_group(bbp, mms, rd + [self.b_yb[br][k4][ti] for k4 in range(4)])
                    if br == 0:
                        T.op("dve", "scalar_tensor_tensor", dict(out=acc[:, 0:wd], in0=sgt[:, 0:wd], scalar=1.0, in1=bp[:, 0:wd], op0=ALU.add, op1=ALU.mult),
                             reads=[bbp, sgtb], writes=[accb])
                    else:
                        T.op("dve", "scalar_tensor_tensor", dict(out=sgt[:, 0:wd], in0=sgt[:, 0:wd], scalar=1.0, in1=bp[:, 0:wd], op0=ALU.add, op1=ALU.mult),
                             reads=[bbp, sgtb], writes=[sgtb])
                        if br == 1:
                            T.op("pool" if "merge" in OFFLOAD else "dve", "tensor_tensor", dict(out=acc[:, 0:wd], in0=acc[:, 0:wd], in1=sgt[:, 0:wd], op=ALU.add),
                                 reads=[accb, sgtb], writes=[accb])
                        else:
                            T.op("pool" if "merge" in OFFLOAD else "dve", "tensor_tensor", dict(out=self.mg[:, j, c0:c0 + wd], in0=acc[:, 0:wd], in1=sgt[:, 0:wd], op=ALU.add),
                                 reads=[accb, sgtb], writes=[self.b_mg[j][ti]])

    def wout(self, sup, l):
        T = self.T
        for j in range(KC):
            slot, sbuf = self.use_tile(sup, l, "m", T_WO + j // 4)
            jj = j % 4
            for ti, (c0, wd, samp) in enumerate(self.tiles):
                bk, bb = self.banks.next()
                mms = [(bk[:, 0:wd], slot[:, (jj * 8 + kc) * 128:(jj * 8 + kc + 1) * 128], self.mg[:, kc, c0:c0 + wd], kc == 0, kc == KC - 1, None)
                       for kc in range(KC)]
                self.mm_group(bb, mms, [sbuf] + [self.b_mg[kc][ti] for kc in range(KC)])
                if not samp:
                    T.op("dve", "scalar_tensor_tensor", dict(
                        out=self.xT[:, j, c0:c0 + wd], in0=bk[:, 0:wd], scalar=self.modT[:, l, 16 + j, 0:1], in1=self.xT[:, j, c0:c0 + wd],
                        op0=ALU.mult, op1=ALU.add), reads=[bb, self.b_modT[l], self.b_xT[j][ti]], writes=[self.b_xT[j][ti]])
                else:
                    t, tb = self.tmp.next()
                    T.op("dve", "tensor_tensor", dict(out=t[:, 0:wd], in0=bk[:, 0:wd], in1=self.modT[:, l, 16 + j, 1:NS + 1], op=ALU.mult),
                         reads=[bb, self.b_modT[l]], writes=[tb])
                    T.op("dve", "tensor_tensor", dict(out=self.xT[:, j, c0:c0 + wd], in0=self.xT[:, j, c0:c0 + wd], in1=t[:, 0:wd], op=ALU.add),
                         reads=[tb, self.b_xT[j][ti]], writes=[self.b_xT[j][ti]])

    def final(self, sup):
        T = self.T
        d = self.d
        for ti, (c0, wd, samp) in enumerate(self.tiles):
            rt, rtb = self.rms_rstd(ti, c0, wd)
            for kc in range(KC):
                T.op("dve", "scalar_tensor_tensor", dict(
                    out=self.xT[:, kc, c0:c0 + wd], in0=self.xT[:, kc, c0:c0 + wd], scalar=self.col(O_FG + kc), in1=rt[:, 0:wd],
                    op0=ALU.mult, op1=ALU.mult), reads=[self.b_xT[kc][ti], self.b_cols, rtb], writes=[self.b_xT[kc][ti]])
            if not samp:
                for bi in range(4):
                    blk = ti * 4 + bi
                    r0 = (sup * 8 + blk) * 128
                    for half in range(2):
                        bk, bb = self.banks.next()
                        self.tr_group(bb, [(bk[:, i * 128:(i + 1) * 128], self.xT[:, half * 4 + i, blk * 128:(blk + 1) * 128], self.ident[:, :])
                                           for i in range(4)], [self.b_xT[half * 4 + i][ti] for i in range(4)])
                        o, ob = self.tmp.next()
                        if half == 0:
                            T.op("act", "activation", dict(out=o[:, :], in_=bk[:, :], func=AF.Copy), reads=[bb], writes=[ob])
                        else:
                            T.op("dve", "tensor_copy", dict(out=o[:, :], in_=bk[:, :]), reads=[bb], writes=[ob])
                        T.dma("sp", d["yp"][r0:r0 + 128, half * 512:(half + 1) * 512], o[:, :], buf=ob, reads=[ob], store=True)
            else:
                st = self.st32
                for half in range(2):
                    bk, bb = self.banks.next()
                    self.tr_group(bb, [(bk[0:NS, i * 128:(i + 1) * 128], self.xT[:, half * 4 + i, c0:c0 + NS], self.ident[:, :]) for i in range(4)],
                                  [self.b_xT[half * 4 + i][ti] for i in range(4)])
                    T.op("dve", "tensor_copy", dict(out=st[0:NS, half * 512:(half + 1) * 512], in_=bk[0:NS, :]), reads=[bb], writes=[self.b_st32])
                T.dma("sp", d["ys"][:, :], st[0:NS, 0:D], buf=self.b_st32, reads=[self.b_st32], store=True)


def _tiles_from_blocks(blocks):
    n = blocks.shape[0] // 32
    return np.ascontiguousarray(blocks.reshape(n, 32, 128, 128).transpose(0, 2, 1, 3).reshape(n, 128, 4096))


def _build_streams(w_ada, w_in, w_branch, w_out):
    wa = np.empty((L, NT_ADA, 128, 4096), np.float32)
    wm = np.empty((L, NT_MAIN, 128, 4096), np.float32)
    CO = dict(ab=0, ac=4, ah=8, ag=12, bu=16, bv=20, bg=24, cx=28, cg=32, ma=36, mb=44, mc=52)
    for l in range(L):
        a4 = w_ada[l].reshape(8, 128, 24, 128).transpose(0, 2, 1, 3)
        bl = [a4[kc, 4 * t + c] for t in range(NT_ADA) for kc in range(8) for c in range(4)]
        wa[l] = _tiles_from_blocks(np.stack(bl))
        i4 = w_in[l].reshape(8, 128, 60, 128).transpose(0, 2, 1, 3)
        b4 = w_branch[l].reshape(3, 4, 128, 8, 128).transpose(0, 1, 3, 2, 4)
        o4 = w_out[l].reshape(8, 128, 8, 128).transpose(0, 2, 1, 3)
        bl = []
        for m in range(4):
            bl += [i4[kc, CO[c] + m] for kc in range(8) for c in ("ab", "ac", "ah", "ag")]
        bl += [i4[kc, CO["bv"] + c] for kc in range(8) for c in range(4)]
        bl += [i4[kc, CO["cx"] + c] for kc in range(8) for c in range(4)]
        for q in range(2):
            for mm_ in range(2):
                for c in ("bu", "bg"):
                    bl += [i4[kc, CO[c] + 2 * q + mm_] for kc in range(8)]
        bl += [i4[kc, CO["cg"] + g] for g in range(4) for kc in range(8)]
        for j in range(8):
            for br, c in enumerate(("ma", "mb", "mc")):
                bl += [i4[kc, CO[c] + j] for kc in range(8)]
                bl += [b4[br, k4, j] for k4 in range(4)]
        for j in range(8):
            bl += [o4[kc, j] for kc in range(8)]
        assert len(bl) == NT_MAIN * 32
        wm[l] = _tiles_from_blocks(np.stack(bl))
    return wa, wm


def _chunk_cols(v, n):
    return np.ascontiguousarray(np.asarray(v, np.float32).reshape(n, 128).T)


def _pool_tables():
    import ml_dtypes
    pt = np.zeros((128, 16, 128), np.float64)
    s = np.arange(128)[:, None]
    t = np.arange(128)[None, :]
    for g, w in enumerate(POOL_WIN):
        diag = ((s <= t) & (s > t - w)) / float(w) - (s == t)
        off = ((s - 128) > (t - w)) / float(w)
        cnt = np.minimum(w, t + 1).astype(np.float64)
        first = ((s <= t) & (s > t - w)) / cnt - (s == t)
        hi = first.astype(np.float32).astype(ml_dtypes.bfloat16).astype(np.float64)
        lo = (first - hi).astype(np.float32).astype(ml_dtypes.bfloat16).astype(np.float64)
        pt[:, g * 4 + 0] = diag
        pt[:, g * 4 + 1] = off
        pt[:, g * 4 + 2] = hi
        pt[:, g * 4 + 3] = lo
    return pt.astype(np.float32)


_NC_CACHE = {}


def _get_nc(nl):
    if nl not in _NC_CACHE:
        _NC_CACHE[nl] = Prog(nl).build()
    return _NC_CACHE[nl]


def kernel(x_prompt, x_sample, c_prompt, c_sample, state_conv, state_pool, w_ada, b_ada, norm_g,
           w_in, conv_w, conv_b, lnv_g, lnv_b, sgu_w, sgu_b, pool_w, pool_b, pool_scale,
           w_branch, w_out, final_g):
    f = lambda a: np.ascontiguousarray(np.asarray(a, dtype=np.float32))
    x_prompt, x_sample, c_prompt, c_sample = f(x_prompt), f(x_sample), f(c_prompt), f(c_sample)
    state_conv, state_pool = f(state_conv), f(state_pool)
    w_ada, w_in, w_branch, w_out = f(w_ada), f(w_in), f(w_branch), f(w_out)
    sgu_w, sgu_b, pool_w = f(sgu_w), f(sgu_b), f(pool_w)
    nl = int(os.environ.get("MK_NL", L))
    wa, wm = _build_streams(w_ada, w_in, w_branch, w_out)
    cols = np.zeros((128, NCOLS), np.float32)
    for l in range(L):
        cols[:, O_BA + l * 24:O_BA + (l + 1) * 24] = _chunk_cols(b_ada[l], 24)
        cols[:, O_NG + l * 8:O_NG + (l + 1) * 8] = _chunk_cols(norm_g[l], 8)
        for k in range(3):
            cols[:, O_CW + (l * 3 + k) * 4:O_CW + (l * 3 + k + 1) * 4] = _chunk_cols(conv_w[l, k], 4)
        cols[:, O_CB + l * 4:O_CB + (l + 1) * 4] = _chunk_cols(conv_b[l], 4)
        cols[:, O_PB + l * 4:O_PB + (l + 1) * 4] = _chunk_cols(pool_b[l], 4)
        cols[:, O_PS + l * 4:O_PS + (l + 1) * 4] = _chunk_cols(pool_scale[l], 4)
        cols[:, O_SA + l * 4:O_SA + (l + 1) * 4] = _chunk_cols(np.repeat(sgu_w[l, :, 0, 0], 64), 4)
        cols[:, O_SB + l * 4:O_SB + (l + 1) * 4] = _chunk_cols(np.repeat(sgu_b[l, :, 0], 64), 4)
    cols[:, O_FG:O_FG + 8] = _chunk_cols(final_g, 8)
    cols[:, O_EPS] = EPS
    lnv = np.ascontiguousarray(np.stack([f(lnv_g), f(lnv_b)], axis=1))
    biasT = np.ascontiguousarray(np.repeat(sgu_b.reshape(L, 4, 2, 1, 128), 64, axis=3).reshape(L, 4, 128, 128).transpose(2, 0, 1, 3))
    poolw = np.ascontiguousarray(pool_w.reshape(L * 4, 128, 128).transpose(1, 0, 2))
    ident = np.eye(128, dtype=np.float32)
    mask = np.triu(np.ones((128, 128), np.float32))
    pt = _pool_tables()
    shared = dict(wa=wa, wm=wm, cols=cols, lnv=lnv, sguw=sgu_w, biasT=biasT, poolw=poolw, ident=ident, mask=mask, pt=pt)
    in_maps = []
    for c in range(NCORE):
        s0, s1 = c * NS, (c + 1) * NS
        m = dict(shared)
        m["xp"] = x_prompt[c]
        m["xs"] = np.ascontiguousarray(x_sample[s0:s1, 0, :])
        m["cc"] = np.ascontiguousarray(np.concatenate([c_prompt[c:c + 1], c_sample[s0:s1]], axis=0))
        m["sconv"] = np.ascontiguousarray(state_conv[:, s0:s1])
        m["spool"] = np.ascontiguousarray(state_pool[:, s0:s1])
        in_maps.append(m)
    nc = _get_nc(nl)
    res = run_bass_kernel_spmd(nc, in_maps, core_ids=list(range(NCORE)))
    R = res.results
    y_prompt = np.stack([R[c]["yp"] for c in range(NCORE)], axis=0)
    y_sample = np.concatenate([R[c]["ys"] for c in range(NCORE)], axis=0)[:, None, :]
    conv_p = np.stack([R[c]["convp"] for c in range(NCORE)], axis=1)
    conv_s = np.concatenate([R[c]["convs"] for c in range(NCORE)], axis=1)
    pool_p = np.stack([R[c]["poolp"] for c in range(NCORE)], axis=1)
    pool_s = np.concatenate([R[c]["pools"] for c in range(NCORE)], axis=1)
    v_p = np.stack([R[c]["vp"] for c in range(NCORE)], axis=1)
    v_s = np.concatenate([R[c]["vs"] for c in range(NCORE)], axis=1)[:, :, None, :]
    outs = (y_prompt, y_sample, conv_p, conv_s, pool_p, pool_s, v_p, v_s)
    return tuple(np.ascontiguousarray(o.astype(np.float32)) for o in outs)
```

```python
import os
import numpy as np
from contextlib import ExitStack
import concourse.bass as bass
import concourse.mybir as mybir
from concourse.bass_utils import run_bass_kernel_spmd

F32 = mybir.dt.float32
BF16 = mybir.dt.bfloat16
AF = mybir.ActivationFunctionType
ALU = mybir.AluOpType
AX = mybir.AxisListType

D = 1024
W = 512
L = 4
KC = 8
NCORE = 8
SEQ = 2048
NS = 16
SUPTOK = 1024
NTS = SUPTOK + NS
EPS = 1e-6
NSLOT = 3
NBANK = 8
NTMP = 6
POOL_WIN = (2, 4, 8, 16)
NEWTON = int(os.environ.get("MK_NEWTON", 2))
RMS_NEWTON = int(os.environ.get("MK_RMS_NEWTON", 2))
OFFLOAD = os.environ.get("MK_POOL", "").split(",")

O_BA = 0
O_NG = 96
O_FG = 128
O_CW = 136
O_CB = 184
O_PB = 200
O_PS = 216
O_SA = 232
O_SB = 248
O_EPS = 264
NCOLS = 272

T_A = 0
T_BV = 4
T_BUG = 6
T_CX = 5
T_CG = 8
T_MRG = 9
T_WO = 18
NT_MAIN = 20
NT_ADA = 6


class Buf:
    __slots__ = ("name", "w", "r", "sem", "total")

    def __init__(self, name):
        self.name = name
        self.w = {}
        self.r = {}
        self.sem = None
        self.total = 0


class Eng:
    def __init__(self, name, sem):
        self.name = name
        self.sem = sem
        self.count = 0
        self.waited = {}
        self.prog = []


class Tracker:
    def __init__(self, nc, es):
        self.nc = nc
        self.es = es
        self.sems = {}
        self.eng = {}
        for n in ("pe", "act", "dve", "pool", "sp"):
            key = "s_" + n
            self.sems[key] = es.enter_context(nc.semaphore(key))
            self.eng[n] = Eng(n, key)
        self.store_bufs = []

    def _need(self, reads, writes):
        need = {}
        for b in reads:
            for k, v in b.w.items():
                if need.get(k, 0) < v:
                    need[k] = v
        for b in writes:
            for k, v in b.w.items():
                if need.get(k, 0) < v:
                    need[k] = v
            for k, v in b.r.items():
                if need.get(k, 0) < v:
                    need[k] = v
        return need

    def _waits(self, E, need, skip=None):
        for key, val in need.items():
            if key == skip:
                continue
            if E.name == "pe" and key == E.sem:
                continue
            if E.waited.get(key, 0) < val:
                E.waited[key] = val
                h = self.sems[key]
                E.prog.append(("wait_ge", dict(sem=h, val=val), None))

    def op(self, eng, name, kw=None, reads=(), writes=(), inc=True):
        E = self.eng[eng]
        self._waits(E, self._need(reads, writes))
        if inc:
            E.count += 1
            val = E.count
            h = self.sems[E.sem]
        else:
            val = E.count + 1
            h = None
        E.prog.append((name, kw, h))
        for b in reads:
            if b.r.get(E.sem, 0) < val:
                b.r[E.sem] = val
        for b in writes:
            b.w = {E.sem: val}
            b.r = {}

    def dma(self, q, out_ap, in_ap, buf, reads=(), writes=(), store=False, multi=False):
        E = self.eng[q]
        if buf.sem is None:
            key = "d_" + buf.name
            self.sems[key] = self.es.enter_context(self.nc.semaphore(key))
            buf.sem = key
        key = buf.sem
        self._waits(E, self._need(reads, writes), skip=key if multi else None)
        buf.total += 16
        val = buf.total
        h = self.sems[key]
        E.prog.append(("dma_start", dict(out=out_ap, in_=in_ap), (h, 16)))
        for b in reads:
            if b.r.get(key, 0) < val:
                b.r[key] = val
        for b in writes:
            b.w = {key: val}
            b.r = {}
        if store and buf not in self.store_bufs:
            self.store_bufs.append(buf)

    def final_wait(self, q):
        E = self.eng[q]
        for b in self.store_bufs:
            h = self.sems[b.sem]
            E.prog.append(("wait_ge", dict(sem=h, val=b.total), None))


def replay(e, prog):
    for name, kw, h in prog:
        if name == "wait_ge":
            e.wait_ge(kw["sem"], kw["val"])
            continue
        if callable(name):
            ins = name(e)
        else:
            ins = getattr(e, name)(**kw)
        if h is not None:
            if isinstance(h, tuple):
                ins.then_inc(h[0], h[1])
            else:
                ins.then_inc(h, 1)


class Ring:
    def __init__(self, items):
        self.items = items
        self.i = 0

    def next(self):
        it = self.items[self.i % len(self.items)]
        self.i += 1
        return it


class Prog:
    def __init__(self, nlayers=L):
        self.nl = nlayers
        self.nc = bass.Bass("TRN2", target_bir_lowering=False)

    def dram_in(self, name, shape):
        return self.nc.dram_tensor(name, list(shape), F32, kind="ExternalInput").ap()

    def dram_out(self, name, shape):
        return self.nc.dram_tensor(name, list(shape), F32, kind="ExternalOutput").ap()

    def sb(self, name, shape, dt=F32):
        return self.es.enter_context(self.nc.sbuf_tensor("sb_" + name, list(shape), dt))

    def build(self):
        nc = self.nc
        d = {}
        d["xp"] = self.dram_in("xp", [SEQ, D])
        d["xs"] = self.dram_in("xs", [NS, D])
        d["cc"] = self.dram_in("cc", [NS + 1, D])
        d["sconv"] = self.dram_in("sconv", [L, NS, 2, W])
        d["spool"] = self.dram_in("spool", [L, NS, 15, W])
        d["wa"] = self.dram_in("wa", [L, NT_ADA, 128, 4096])
        d["wm"] = self.dram_in("wm", [L, NT_MAIN, 128, 4096])
        d["cols"] = self.dram_in("cols", [128, NCOLS])
        d["lnv"] = self.dram_in("lnv", [L, 2, W])
        d["sguw"] = self.dram_in("sguw", [L, 8, 128, 128])
        d["biasT"] = self.dram_in("biasT", [128, L, 4, 128])
        d["poolw"] = self.dram_in("poolw", [128, L * 4, 128])
        d["ident"] = self.dram_in("ident", [128, 128])
        d["mask"] = self.dram_in("mask", [128, 128])
        d["pt"] = self.dram_in("pt", [128, 16, 128])
        d["yp"] = self.dram_out("yp", [SEQ, D])
        d["ys"] = self.dram_out("ys", [NS, D])
        d["convp"] = self.dram_out("convp", [L, 2, W])
        d["convs"] = self.dram_out("convs", [L, NS, 2, W])
        d["poolp"] = self.dram_out("poolp", [L, 15, W])
        d["pools"] = self.dram_out("pools", [L, NS, 15, W])
        d["vp"] = self.dram_out("vp", [L, 128, W])
        d["vs"] = self.dram_out("vs", [L, NS, W])
        self.d = d
        with ExitStack() as es:
            self.es = es
            self.T = Tracker(nc, es)
            self.alloc()
            self.emit()
            block = es.enter_context(nc.Block())

            @block.tensor
            def _(e):
                replay(e, self.T.eng["pe"].prog)

            @block.scalar
            def _(e):
                replay(e, self.T.eng["act"].prog)

            @block.vector
            def _(e):
                replay(e, self.T.eng["dve"].prog)

            @block.gpsimd
            def _(e):
                replay(e, self.T.eng["pool"].prog)

            @block.sync
            def _(e):
                replay(e, self.T.eng["sp"].prog)
        return nc

    def alloc(self):
        sb = self.sb
        self.colsT = sb("colsT", [128, NCOLS]); self.b_cols = Buf("cols")
        self.ident = sb("ident", [128, 128]); self.b_ident = Buf("ident")
        self.mask = sb("mask", [128, 128]); self.b_mask = Buf("mask")
        self.ones = sb("ones", [128, 128], BF16); self.b_ones = Buf("ones")
        self.ptb = sb("ptb", [128, 16, 128], BF16); self.b_ptb = Buf("ptb")
        self.pwb = sb("pwb", [128, L * 4, 128], BF16); self.b_pwb = Buf("pwb")
        self.wsT = sb("wsT", [128, 2, 8, 128], BF16); self.b_wsT = [Buf("wsT0"), Buf("wsT1")]
        self.biasT = sb("biasT", [128, 2, 4, 128]); self.b_biasT = [Buf("biasT0"), Buf("biasT1")]
        self.lnv = sb("lnv", [128, 2, 2, W]); self.b_lnv = [Buf("lnv0"), Buf("lnv1")]
        self.modT = sb("modT", [128, L, 24, NS + 1]); self.b_modT = [Buf("modT%d" % l) for l in range(L)]
        self.gsP = sb("gsP", [128, L, 8]); self.b_gsP = [Buf("gsP%d" % l) for l in range(L)]
        self.gsS = sb("gsS", [128, L, 8, NS]); self.b_gsS = [Buf("gsS%d" % l) for l in range(L)]
        self.scb = sb("scb", [128, 8, NS + 1], BF16); self.b_scb = Buf("scb")
        self.zcarry = sb("zcarry", [128, L, 4, 2]); self.b_zcarry = [[Buf("zc%d_%d" % (l, m)) for m in range(4)] for l in range(L)]
        self.cxcarry = sb("cxcarry", [128, L, W], BF16); self.b_cxcarry = [Buf("cxc%d" % l) for l in range(L)]
        self.CP = sb("CP", [128, L, 4, NS]); self.b_CP = [Buf("CP%d" % l) for l in range(L)]
        self.SpT = sb("SpT", [128, L, 4, NS]); self.b_SpT = [Buf("SpT%d" % l) for l in range(L)]
        self.ztail = sb("ztail", [128, 4, 2]); self.b_ztail = Buf("ztail")
        self.zsall = sb("zsall", [128, 4, NS]); self.b_zsall = Buf("zsall")
        small = sb("small", [128, 4, 16]); self.small_ring = Ring([(small[:, i, :], Buf("small%d" % i)) for i in range(4)])
        rstd = [sb("rstd%d" % i, [128, W]) for i in range(2)]
        self.rstd = Ring([(rstd[i], Buf("rstd%d" % i)) for i in range(2)])
        self.xT = sb("xT", [128, KC, NTS]); self.b_xT = [[Buf("xT%d_%d" % (k, t)) for t in range(3)] for k in range(KC)]
        self.hT = sb("hT", [128, KC, NTS], BF16); self.b_hT = [[Buf("hT%d_%d" % (k, t)) for t in range(3)] for k in range(KC)]
        self.yb = [sb("y%s" % n, [128, 4, NTS], BF16) for n in "ABC"]
        self.b_yb = [[[Buf("y%s%d_%d" % (n, m, t)) for t in range(3)] for m in range(4)] for n in "ABC"]
        self.mg = sb("mg", [128, KC, NTS], BF16); self.b_mg = [[Buf("mg%d_%d" % (k, t)) for t in range(3)] for k in range(KC)]
        self.slots = [sb("slot%d" % i, [128, 4096], BF16) for i in range(NSLOT)]
        self.b_slots = [Buf("slot%d" % i) for i in range(NSLOT)]
        self.zbuf = sb("zbuf", [128, 2 + SUPTOK]); self.b_zbuf = [Buf("zb_c"), Buf("zb_0"), Buf("zb_1")]
        self.vb = sb("vb", [128, 9, W], BF16); self.b_vb = [Buf("vb%d" % i) for i in range(9)]
        self.cxb = sb("cxb", [128, 8, W], BF16); self.b_cxb = [Buf("cxb%d" % i) for i in range(8)]
        tmps = [sb("tmp%d" % i, [128, W]) for i in range(NTMP)]
        self.tmp = Ring([(tmps[i], Buf("tmp%d" % i)) for i in range(NTMP)])
        tbs = [sb("tbf%d" % i, [128, W], BF16) for i in range(3)]
        self.tbf = Ring([(tbs[i], Buf("tbf%d" % i)) for i in range(3)])
        self.xstage = sb("xstage", [128, D]); self.b_xstage = Buf("xstage")
        self.st32 = sb("st32", [32, 2048]); self.b_st32 = Buf("st32")
        self.pmS = sb("pmS", [128, 4, NS], BF16); self.b_pmS = Buf("pmS")
        self.mixS = sb("mixS", [128, 4, NS]); self.b_mixS = Buf("mixS")
        self.b_d2d = Buf("d2d")
        banks = [self.es.enter_context(self.nc.psum_tensor("bk%d" % i, [128, 512], F32)) for i in range(NBANK)]
        self.banks = Ring([(banks[i], Buf("bk%d" % i)) for i in range(NBANK)])
        if os.environ.get("MK_VERBOSE"):
            print("SBUF bytes remaining/partition:", self.nc.sbuf_bytes_remaining)

    def col(self, off):
        return self.colsT[:, off:off + 1]

    def rsqrt_dve(self, x, xb, y, yb, t, tb, iters=None):
        T = self.T
        I32 = mybir.dt.int32
        if iters is None:
            iters = NEWTON
        T.op("dve", "tensor_single_scalar", dict(out=t.bitcast(I32), in_=x.bitcast(I32), scalar=1, op=ALU.arith_shift_right), reads=[xb], writes=[tb])
        T.op("dve", "tensor_scalar", dict(out=y.bitcast(I32), in0=t.bitcast(I32), scalar1=-1, scalar2=0x5f3759df, op0=ALU.mult, op1=ALU.add),
             reads=[tb], writes=[yb])
        for _ in range(iters):
            T.op("dve", "tensor_tensor", dict(out=t, in0=y, in1=y, op=ALU.mult), reads=[yb], writes=[tb])
            T.op("dve", "scalar_tensor_tensor", dict(out=t, in0=t, scalar=-0.5, in1=x, op0=ALU.mult, op1=ALU.mult), reads=[tb, xb], writes=[tb])
            T.op("dve", "scalar_tensor_tensor", dict(out=y, in0=t, scalar=1.5, in1=y, op0=ALU.add, op1=ALU.mult), reads=[tb, yb], writes=[yb])

    def silu2(self, bank, bank_buf, wd):
        T = self.T
        sg, sgb = self.tmp.next()
        T.op("act", "activation", dict(out=sg[:, 0:wd], in_=bank[:, 0:wd], func=AF.Tanh, scale=0.5), reads=[bank_buf], writes=[sgb])
        T.op("dve", "scalar_tensor_tensor", dict(out=sg[:, 0:wd], in0=sg[:, 0:wd], scalar=1.0, in1=bank[:, 0:wd], op0=ALU.add, op1=ALU.mult),
             reads=[sgb, bank_buf], writes=[sgb])
        return sg, sgb

    def mm_group(self, bank_buf, mms, reads, inc=True):
        def fn(e, mms=mms):
            ins = None
            for (o, lt, r, st, sp, tp) in mms:
                if tp is None:
                    ins = e.matmul(o, lt, r, start=st, stop=sp)
                else:
                    ins = e.matmul(o, lt, r, start=st, stop=sp, tile_position=tp)
            return ins
        self.T.op("pe", fn, None, reads=reads, writes=[bank_buf], inc=inc)

    def tr_group(self, bank_buf, trs, reads):
        def fn(e, trs=trs):
            ins = None
            for (o, i, idn) in trs:
                ins = e.transpose(o, i, idn)
            return ins
        self.T.op("pe", fn, None, reads=list(reads) + [self.b_ident], writes=[bank_buf])

    ADA_AFTER = (0, 1, 2, 3, 4, 7)

    def plan_stream(self):
        order = []
        for sup in range(2):
            for l in range(self.nl):
                if sup == 0 and l == 0:
                    for t in range(NT_ADA):
                        order.append((sup, l, "a", t))
                ta = 0
                for t in range(NT_MAIN):
                    order.append((sup, l, "m", t))
                    if sup == 0 and l + 1 < self.nl and t in self.ADA_AFTER:
                        order.append((0, l + 1, "a", ta))
                        ta += 1
        self.fills = []
        self.fill_pos = {}
        for k, (sup, l, kind, t) in enumerate(order):
            self.fill_pos[(sup, l, kind, t)] = k
            self.fills.append(self.d["wa"][l, t, :, :] if kind == "a" else self.d["wm"][l, t, :, :])
        self.fill_emitted = 0
        self.pending_mod = []

    def use_tile(self, sup, l, kind, t):
        k = self.fill_pos[(sup, l, kind, t)]
        while self.fill_emitted < min(len(self.fills), k + NSLOT - 1):
            j = self.fill_emitted
            s = j % NSLOT
            self.T.dma("pool", self.slots[s][:, :], self.fills[j], buf=self.b_slots[s], writes=[self.b_slots[s]])
            self.fill_emitted += 1
        s = k % NSLOT
        return self.slots[s], self.b_slots[s]

    def emit(self):
        T = self.T
        d = self.d
        self.plan_stream()
        stop = int(os.environ.get("MK_STOP", 10 ** 9))

        class _Stop(Exception):
            pass

        def stage(n):
            if n + 100 * self.cur_sup >= stop:
                raise _Stop()
        self.cur_sup = 0
        try:
            self.startup()
            stage(1)
            for sup in range(2):
                self.cur_sup = sup
                self.tiles = [(0, 512, False), (512, 512, False)] + ([(1024, NS, True)] if sup == 1 else [])
                self.load_x(sup)
                stage(2)
                for l in range(self.nl):
                    self.layer_params(sup, l)
                    stage(3)
                    if sup == 0 and l == 0:
                        self.mod(l)
                    if sup == 0 and l + 1 < self.nl:
                        self.pending_mod = [(l + 1, t) for t in range(NT_ADA)]
                    stage(4)
                    self.rmsnorm(sup, l)
                    stage(5)
                    self.phase_a(sup, l)
                    stage(6)
                    self.phase_b(sup, l)
                    self.phase_c(sup, l)
                    self.phase_b_bug(sup, l)
                    self.phase_b_end(sup)
                    stage(7)
                    self.phase_c_pool(sup, l)
                    stage(8)
                    self.merge(sup, l)
                    stage(9)
                    self.wout(sup, l)
                    stage(10)
                self.final(sup)
                stage(11)
        except _Stop:
            for sb_ in self.b_slots:
                if sb_.sem is not None:
                    T.eng["sp"].prog.append(("wait_ge", dict(sem=T.sems[sb_.sem], val=sb_.total), None))
        T.final_wait("sp")

    def startup(self):
        T = self.T
        d = self.d
        T.dma("sp", self.colsT[:, :], d["cols"][:, :], buf=self.b_cols, writes=[self.b_cols])
        T.dma("sp", self.ident[:, :], d["ident"][:, :], buf=self.b_ident, writes=[self.b_ident])
        T.dma("sp", self.mask[:, :], d["mask"][:, :], buf=self.b_mask, writes=[self.b_mask])
        T.dma("pool", self.ptb[:, :, :], d["pt"][:, :, :], buf=self.b_ptb, writes=[self.b_ptb])
        T.dma("pool", self.pwb[:, :, :], d["poolw"][:, :, :], buf=self.b_pwb, writes=[self.b_pwb])
        T.op("dve", "memset", dict(ap=self.ones[:, :], constant=1.0), writes=[self.b_ones])
        st = self.st32
        T.dma("sp", st[0:NS + 1, 0:D], d["cc"][:, :], buf=self.b_st32, writes=[self.b_st32])
        T.op("act", "activation", dict(out=st[0:NS + 1, 0:D], in_=st[0:NS + 1, 0:D], func=AF.Silu),
             reads=[self.b_st32], writes=[self.b_st32])
        bk, bb = self.banks.next()
        n1 = NS + 1
        self.tr_group(bb, [(bk[:, kc * n1:(kc + 1) * n1], st[0:n1, kc * 128:(kc + 1) * 128], self.ident[0:n1, 0:n1])
                           for kc in range(KC)], [self.b_st32])
        T.op("dve", "tensor_copy", dict(out=self.scb[:, :, :], in_=bk[:, 0:KC * n1].rearrange("p (a b) -> p a b", a=KC)),
             reads=[bb], writes=[self.b_scb])
        for l in range(self.nl):
            T.dma("sp", d["convs"][l, :, 0, :], d["sconv"][l, :, 1, :], buf=self.b_d2d, store=True)
            T.dma("sp", d["pools"][l, :, 0:14, :], d["spool"][l, :, 1:15, :], buf=self.b_d2d, store=True)
        for l in range(self.nl):
            T.dma("sp", st[0:NS, 0:2 * W], d["sconv"][l].rearrange("n r c -> n (r c)"), buf=self.b_st32, writes=[self.b_st32])
            bk, bb = self.banks.next()
            self.tr_group(bb, [(bk[:, (r * 4 + m) * NS:(r * 4 + m + 1) * NS], st[0:NS, r * W + m * 128:r * W + (m + 1) * 128],
                                self.ident[0:NS, 0:NS]) for r in range(2) for m in range(4)], [self.b_st32])
            for m in range(4):
                w0 = self.col(O_CW + (l * 3 + 0) * 4 + m)
                w1 = self.col(O_CW + (l * 3 + 1) * 4 + m)
                cb = self.col(O_CB + l * 4 + m)
                T.op("dve", "tensor_scalar", dict(
                    out=self.CP[:, l, m, :], in0=bk[:, m * NS:(m + 1) * NS], scalar1=w0, scalar2=cb, op0=ALU.mult, op1=ALU.add),
                    reads=[bb, self.b_cols], writes=[self.b_CP[l]])
                T.op("dve", "scalar_tensor_tensor", dict(
                    out=self.CP[:, l, m, :], in0=bk[:, (4 + m) * NS:(5 + m) * NS], scalar=w1, in1=self.CP[:, l, m, :],
                    op0=ALU.mult, op1=ALU.add), reads=[bb, self.b_cols, self.b_CP[l]], writes=[self.b_CP[l]])
        for l in range(self.nl):
            sp_t, sp_b = self.tmp.next()
            for g, w in enumerate(POOL_WIN):
                nr = w - 1
                T.dma("sp", st[0:NS, 0:nr * 128].rearrange("p (r c) -> p r c", r=nr),
                      d["spool"][l, :, 15 - nr:15, g * 128:(g + 1) * 128], buf=self.b_st32, writes=[self.b_st32])
                T.op("dve", "tensor_reduce", dict(
                    out=sp_t[0:NS, g * 128:(g + 1) * 128], in_=st[0:NS, 0:nr * 128].rearrange("p (r c) -> p c r", r=nr),
                    axis=AX.X, op=ALU.add), reads=[self.b_st32], writes=[sp_b])
            bk, bb = self.banks.next()
            self.tr_group(bb, [(bk[:, g * NS:(g + 1) * NS], sp_t[0:NS, g * 128:(g + 1) * 128], self.ident[0:NS, 0:NS])
                               for g in range(4)], [sp_b])
            for g, w in enumerate(POOL_WIN):
                T.op("dve", "tensor_scalar", dict(
                    out=self.SpT[:, l, g, :], in0=bk[:, g * NS:(g + 1) * NS], scalar1=1.0 / w, scalar2=None, op0=ALU.mult),
                    reads=[bb], writes=[self.b_SpT[l]])
        T.op("dve", "memset", dict(ap=self.zcarry[:, :, :, :], constant=0.0), writes=[b for r in self.b_zcarry for b in r])

    def load_x(self, sup):
        T = self.T
        d = self.d
        for blk in range(8):
            r0 = (sup * 8 + blk) * 128
            ti = blk // 4
            for half in range(2):
                xs_, xsb_ = self.tmp.next()
                T.dma("sp", xs_[:, :], d["xp"][r0:r0 + 128, half * 512:(half + 1) * 512], buf=xsb_, writes=[xsb_])
                bk, bb = self.banks.next()
                self.tr_group(bb, [(bk[:, i * 128:(i + 1) * 128], xs_[:, i * 128:(i + 1) * 128], self.ident[:, :]) for i in range(4)], [xsb_])
                out = self.xT[:, half * 4:(half + 1) * 4, blk * 128:(blk + 1) * 128]
                src = bk[:, :].rearrange("p (a b) -> p a b", a=4)
                wr = [self.b_xT[half * 4 + i][ti] for i in range(4)]
                if half == 0:
                    T.op("act", "activation", dict(out=out, in_=src, func=AF.Copy), reads=[bb], writes=wr)
                else:
                    T.op("dve", "tensor_copy", dict(out=out, in_=src), reads=[bb], writes=wr)
        if sup == 1:
            st = self.st32
            T.dma("sp", st[0:NS, 0:D], d["xs"][:, :], buf=self.b_st32, writes=[self.b_st32])
            bk, bb = self.banks.next()
            self.tr_group(bb, [(bk[:, kc * NS:(kc + 1) * NS], st[0:NS, kc * 128:(kc + 1) * 128], self.ident[0:NS, 0:NS])
                               for kc in range(KC)], [self.b_st32])
            T.op("dve", "tensor_copy", dict(out=self.xT[:, :, SUPTOK:NTS], in_=bk[:, 0:KC * NS].rearrange("p (a b) -> p a b", a=KC)),
                 reads=[bb], writes=[self.b_xT[k][2] for k in range(KC)])

    def layer_params(self, sup, l):
        T = self.T
        d = self.d
        r = (sup * self.nl + l) % 2
        self.pr = r
        for i in range(2):
            T.dma("sp", self.lnv[:, r, i, :], d["lnv"][l, i:i + 1, :].partition_broadcast(128), buf=self.b_lnv[r], writes=[self.b_lnv[r]], multi=(i == 1))
        T.dma("sp", self.biasT[:, r, :, :], d["biasT"][:, l, :, :], buf=self.b_biasT[r], writes=[self.b_biasT[r]])
        T.dma("sp", self.xstage[:, :].rearrange("p (h s) -> p h s", h=8), d["sguw"][l].rearrange("h t s -> t h s"),
              buf=self.b_xstage, writes=[self.b_xstage])
        for half in range(2):
            bk, bb = self.banks.next()
            self.tr_group(bb, [(bk[:, i * 128:(i + 1) * 128], self.xstage[:, (half * 4 + i) * 128:(half * 4 + i + 1) * 128],
                                self.ident[:, :]) for i in range(4)], [self.b_xstage])
            T.op("dve", "tensor_tensor", dict(
                out=self.wsT[:, r, half * 4:(half + 1) * 4, :], in0=bk[:, :].rearrange("p (a b) -> p a b", a=4),
                in1=self.mask[:, :].unsqueeze(1).broadcast_to([128, 4, 128]), op=ALU.mult),
                reads=[bb, self.b_mask], writes=[self.b_wsT[r]])

    def mod_tile(self, l, t):
        T = self.T
        n1 = NS + 1
        bk, bb = self.banks.next()
        slot, sbuf = self.use_tile(0, l, "a", t)
        for c in range(4):
            mms = [(bk[:, c * n1:(c + 1) * n1], slot[:, (kc * 4 + c) * 128:(kc * 4 + c + 1) * 128], self.scb[:, kc, :],
                    kc == 0, kc == KC - 1, None) for kc in range(KC)]
            self.mm_group(bb, mms, [sbuf, self.b_scb])
        T.op("dve", "tensor_tensor", dict(
            out=self.modT[:, l, 4 * t:4 * t + 4, :], in0=bk[:, 0:4 * n1].rearrange("p (a b) -> p a b", a=4),
            in1=self.colsT[:, O_BA + l * 24 + 4 * t:O_BA + l * 24 + 4 * t + 4].unsqueeze(2).broadcast_to([128, 4, n1]), op=ALU.add),
            reads=[bb, self.b_cols], writes=[self.b_modT[l]])
        if t == NT_ADA - 1:
            self.mod_finish(l)

    def mod_step(self):
        if self.pending_mod:
            l, t = self.pending_mod.pop(0)
            self.mod_tile(l, t)

    def mod(self, l):
        for t in range(NT_ADA):
            self.mod_tile(l, t)

    def mod_finish(self, l):
        T = self.T
        n1 = NS + 1
        T.op("dve", "tensor_scalar", dict(out=self.modT[:, l, 16:24, :], in0=self.modT[:, l, 16:24, :], scalar1=0.5, scalar2=None, op0=ALU.mult),
             reads=[self.b_modT[l]], writes=[self.b_modT[l]])
        T.op("dve", "scalar_tensor_tensor", dict(
            out=self.gsP[:, l, :], in0=self.modT[:, l, 8:16, 0], scalar=1.0, in1=self.colsT[:, O_NG + l * 8:O_NG + (l + 1) * 8],
            op0=ALU.add, op1=ALU.mult), reads=[self.b_modT[l], self.b_cols], writes=[self.b_gsP[l]])
        T.op("dve", "scalar_tensor_tensor", dict(
            out=self.gsS[:, l, :, :], in0=self.modT[:, l, 8:16, 1:n1], scalar=1.0,
            in1=self.colsT[:, O_NG + l * 8:O_NG + (l + 1) * 8].unsqueeze(2).broadcast_to([128, 8, NS]),
            op0=ALU.add, op1=ALU.mult), reads=[self.b_modT[l], self.b_cols], writes=[self.b_gsS[l]])

    def rms_rstd(self, ti, c0, wd):
        T = self.T
        bk, bb = self.banks.next()
        for kc in range(KC):
            sq, sqb = self.tbf.next()
            T.op("act", "activation", dict(out=sq[:, 0:wd], in_=self.xT[:, kc, c0:c0 + wd], func=AF.Square),
                 reads=[self.b_xT[kc][ti]], writes=[sqb])
            self.mm_group(bb, [(bk[:, 0:wd], self.ones[:, :], sq[:, 0:wd], kc == 0, kc == KC - 1, None)], [sqb, self.b_ones])
        rt, rtb = self.rstd.next()
        xs, xsb = self.tmp.next()
        ts, tsb = self.tmp.next()
        T.op("dve", "tensor_scalar", dict(out=xs[:, 0:wd], in0=bk[:, 0:wd], scalar1=1.0 / D, scalar2=EPS, op0=ALU.mult, op1=ALU.add),
             reads=[bb], writes=[xsb])
        self.rsqrt_dve(xs[:, 0:wd], xsb, rt[:, 0:wd], rtb, ts[:, 0:wd], tsb, iters=RMS_NEWTON)
        return rt, rtb

    def rmsnorm(self, sup, l):
        T = self.T
        for ti, (c0, wd, samp) in enumerate(self.tiles):
            rt, rtb = self.rms_rstd(ti, c0, wd)
            if not samp:
                for kc in range(KC):
                    t, tb = self.tmp.next()
                    T.op("dve", "scalar_tensor_tensor", dict(
                        out=t[:, 0:wd], in0=self.xT[:, kc, c0:c0 + wd], scalar=self.gsP[:, l, kc:kc + 1], in1=rt[:, 0:wd],
                        op0=ALU.mult, op1=ALU.mult), reads=[self.b_xT[kc][ti], self.b_gsP[l], rtb], writes=[tb])
                    T.op("act", "activation", dict(
                        out=self.hT[:, kc, c0:c0 + wd], in_=t[:, 0:wd], func=AF.Identity, bias=self.modT[:, l, kc, 0:1], scale=1.0),
                        reads=[tb, self.b_modT[l]], writes=[self.b_hT[kc][ti]])
            else:
                t, tb = self.tmp.next()
                tv = t[:, 0:KC * NS].rearrange("p (a b) -> p a b", a=KC)
                T.op("dve", "tensor_tensor", dict(
                    out=tv, in0=self.xT[:, :, c0:c0 + wd], in1=rt[:, 0:wd].unsqueeze(1).broadcast_to([128, KC, NS]), op=ALU.mult),
                    reads=[self.b_xT[k][ti] for k in range(KC)] + [rtb], writes=[tb])
                T.op("dve", "tensor_tensor", dict(out=tv, in0=tv, in1=self.gsS[:, l, :, :], op=ALU.mult),
                     reads=[tb, self.b_gsS[l]], writes=[tb])
                T.op("dve", "tensor_tensor", dict(out=self.hT[:, :, c0:c0 + wd], in0=tv, in1=self.modT[:, l, 0:8, 1:NS + 1], op=ALU.add),
                     reads=[tb, self.b_modT[l]], writes=[self.b_hT[k][ti] for k in range(KC)])

    def proj_group(self, slot, sbuf, blk_of_kc, ti, c0, wd, bank=None):
        if bank is None:
            bank = self.banks.next()
        bk, bb = bank
        mms = [(bk[:, 0:wd], slot[:, blk_of_kc(kc) * 128:(blk_of_kc(kc) + 1) * 128], self.hT[:, kc, c0:c0 + wd],
                kc == 0, kc == KC - 1, None) for kc in range(KC)]
        self.mm_group(bb, mms, [sbuf] + [self.b_hT[kc][ti] for kc in range(KC)])
        return bk, bb

    def phase_a(self, sup, l):
        T = self.T
        d = self.d
        zb = self.zbuf
        for m in range(4):
            slot, sbuf = self.use_tile(sup, l, "m", T_A + m)
            w0 = self.col(O_CW + (l * 3 + 0) * 4 + m)
            w1 = self.col(O_CW + (l * 3 + 1) * 4 + m)
            w2 = self.col(O_CW + (l * 3 + 2) * 4 + m)
            cb = self.col(O_CB + l * 4 + m)
            T.op("dve", "tensor_copy", dict(out=zb[:, 0:2], in_=self.zcarry[:, l, m, :]),
                 reads=[self.b_zcarry[l][m]], writes=[self.b_zbuf[0]])
            for ti, (c0, wd, samp) in enumerate(self.tiles):
                if samp:
                    bk, bb = self.banks.next()
                    for c in range(4):
                        mms = [(bk[:, c * NS:(c + 1) * NS], slot[:, (kc * 4 + c) * 128:(kc * 4 + c + 1) * 128], self.hT[:, kc, c0:c0 + wd],
                                kc == 0, kc == KC - 1, None) for kc in range(KC)]
                        self.mm_group(bb, mms, [sbuf] + [self.b_hT[kc][ti] for kc in range(KC)])
                    s_, sb_ = self.tmp.next()
                    T.op("act", "activation", dict(out=s_[:, 0:4 * NS], in_=bk[:, 0:4 * NS], func=AF.Copy), reads=[bb], writes=[sb_])
                    T.op("act", "activation", dict(out=s_[:, 4 * NS:5 * NS], in_=s_[:, 3 * NS:4 * NS], func=AF.Tanh, scale=0.5), reads=[sb_], writes=[sb_])
                    T.op("dve", "scalar_tensor_tensor", dict(out=s_[:, 4 * NS:5 * NS], in0=s_[:, 4 * NS:5 * NS], scalar=1.0, in1=s_[:, 3 * NS:4 * NS],
                                                             op0=ALU.add, op1=ALU.mult), reads=[sb_], writes=[sb_])
                    T.op("dve", "tensor_tensor", dict(out=self.zsall[:, m, :], in0=s_[:, NS:2 * NS], in1=s_[:, 2 * NS:3 * NS], op=ALU.mult),
                         reads=[sb_], writes=[self.b_zsall])
                    T.op("dve", "scalar_tensor_tensor", dict(out=s_[:, 5 * NS:6 * NS], in0=self.zsall[:, m, :], scalar=w2, in1=self.CP[:, l, m, :],
                                                             op0=ALU.mult, op1=ALU.add), reads=[self.b_zsall, self.b_CP[l], self.b_cols], writes=[sb_])
                    T.op("dve", "tensor_tensor", dict(out=s_[:, 5 * NS:6 * NS], in0=s_[:, 0:NS], in1=s_[:, 5 * NS:6 * NS], op=ALU.mult), reads=[sb_], writes=[sb_])
                    T.op("dve", "scalar_tensor_tensor", dict(out=self.yb[0][:, m, c0:c0 + wd], in0=s_[:, 5 * NS:6 * NS], scalar=0.5, in1=s_[:, 4 * NS:5 * NS],
                                                             op0=ALU.mult, op1=ALU.mult), reads=[sb_], writes=[self.b_yb[0][m][ti]])
                    continue
                pb = [self.proj_group(slot, sbuf, (lambda kc, c=c: kc * 4 + c), ti, c0, wd) for c in range(4)]
                (b_ab, bb_ab), (b_ac, bb_ac), (b_ah, bb_ah), (b_ag, bb_ag) = pb
                sg, sgb = self.silu2(b_ag, bb_ag, wd)
                ah, ahb = self.tmp.next()
                T.op("act", "activation", dict(out=ah[:, 0:wd], in_=b_ah[:, 0:wd], func=AF.Copy),
                     reads=[bb_ah], writes=[ahb])
                acc, accb = self.tmp.next()
                if not samp:
                    zcur = self.b_zbuf[1 + ti]
                    zprev = self.b_zbuf[ti]
                    T.op("dve", "tensor_tensor", dict(out=zb[:, 2 + c0:2 + c0 + wd], in0=b_ac[:, 0:wd], in1=ah[:, 0:wd], op=ALU.mult),
                         reads=[bb_ac, ahb], writes=[zcur])
                    T.op("act", "activation", dict(out=acc[:, 0:wd], in_=zb[:, 2 + c0:2 + c0 + wd], func=AF.Identity, bias=cb, scale=w2),
                         reads=[zcur, self.b_cols], writes=[accb])
                    T.op("pool" if "conv" in OFFLOAD else "dve", "scalar_tensor_tensor", dict(out=acc[:, 0:wd], in0=zb[:, 1 + c0:1 + c0 + wd], scalar=w1, in1=acc[:, 0:wd],
                                                                          op0=ALU.mult, op1=ALU.add),
                         reads=[zcur, zprev, accb, self.b_cols], writes=[accb])
                    T.op("pool" if "conv" in OFFLOAD else "dve", "scalar_tensor_tensor", dict(out=acc[:, 0:wd], in0=zb[:, c0:c0 + wd], scalar=w0, in1=acc[:, 0:wd],
                                                                          op0=ALU.mult, op1=ALU.add),
                         reads=[zcur, zprev, accb, self.b_cols], writes=[accb])
                else:
                    T.op("dve", "tensor_tensor", dict(out=self.zsall[:, m, :], in0=b_ac[:, 0:wd], in1=ah[:, 0:wd], op=ALU.mult),
                         reads=[bb_ac, ahb], writes=[self.b_zsall])
                    T.op("dve", "scalar_tensor_tensor", dict(out=acc[:, 0:wd], in0=self.zsall[:, m, :], scalar=w2, in1=self.CP[:, l, m, :],
                                                                               op0=ALU.mult, op1=ALU.add),
                         reads=[self.b_zsall, self.b_CP[l], self.b_cols], writes=[accb])
                T.op("dve", "tensor_tensor", dict(out=acc[:, 0:wd], in0=b_ab[:, 0:wd], in1=acc[:, 0:wd], op=ALU.mult),
                     reads=[bb_ab, accb], writes=[accb])
                T.op("dve", "scalar_tensor_tensor", dict(out=self.yb[0][:, m, c0:c0 + wd], in0=acc[:, 0:wd], scalar=0.5, in1=sg[:, 0:wd], op0=ALU.mult, op1=ALU.mult),
                     reads=[accb, sgb], writes=[self.b_yb[0][m][ti]])
            if sup == 0:
                self.mod_step()
                T.op("dve", "tensor_copy", dict(out=self.zcarry[:, l, m, :], in_=zb[:, SUPTOK:SUPTOK + 2]),
                     reads=[self.b_zbuf[2]], writes=[self.b_zcarry[l][m]])
            else:
                T.op("dve", "tensor_copy", dict(out=self.ztail[:, m, :], in_=zb[:, SUPTOK:SUPTOK + 2]),
                     reads=[self.b_zbuf[2]], writes=[self.b_ztail])
        if sup == 1 and "a_out" not in os.environ.get("MK_SKIP", ""):
            bk, bb = self.banks.next()
            self.tr_group(bb, [(bk[0:2, m * 128:(m + 1) * 128], self.ztail[:, m, :], self.ident[:, :]) for m in range(4)], [self.b_ztail])
            o, ob = self.tmp.next()
            T.op("act", "activation", dict(out=o[0:2, :], in_=bk[0:2, :], func=AF.Copy), reads=[bb], writes=[ob])
            T.dma("sp", d["convp"][l, :, :], o[0:2, :], buf=ob, reads=[ob], store=True)
            bk, bb = self.banks.next()
            self.tr_group(bb, [(bk[0:NS, m * 128:(m + 1) * 128], self.zsall[:, m, :], self.ident[:, :]) for m in range(4)], [self.b_zsall])
            o, ob = self.tmp.next()
            T.op("act", "activation", dict(out=o[0:NS, :], in_=bk[0:NS, :], func=AF.Copy), reads=[bb], writes=[ob])
            T.dma("sp", d["convs"][l, :, 1, :], o[0:NS, :], buf=ob, reads=[ob], store=True)

    def phase_b(self, sup, l):
        T = self.T
        d = self.d
        r = self.pr
        slot, sbuf = self.use_tile(sup, l, "m", T_BV)
        SKIP = os.environ.get("MK_SKIP", "").split(",")
        nblk = 9 if (sup == 1 and "b_samp" not in SKIP) else 8
        st_ = {}

        def part1(blk):
            samp = blk == 8
            np_ = NS if samp else 128
            c0 = SUPTOK if samp else blk * 128
            ti = 2 if samp else blk // 4
            bk, bb = self.banks.next()
            mms = [(bk[0:np_, :], self.hT[:, kc, c0:c0 + np_], slot[:, kc * 512:(kc + 1) * 512], kc == 0, kc == KC - 1, None) for kc in range(KC)]
            self.mm_group(bb, mms, [sbuf] + [self.b_hT[kc][ti] for kc in range(KC)])
            g, gb = self.tmp.next()
            T.op("act", "activation", dict(out=g[0:np_, :], in_=bk[0:np_, :], func=AF.Gelu_apprx_tanh), reads=[bb], writes=[gb])
            sm, smb = self.small_ring.next()
            T.op("dve", "bn_stats", dict(out=sm[0:np_, 0:6], in_=g[0:np_, :]), reads=[gb], writes=[smb])
            T.op("dve", "bn_aggr", dict(out=sm[0:np_, 8:10], in_=sm[0:np_, 0:6]), reads=[smb], writes=[smb])
            T.op("dve", "tensor_scalar", dict(out=sm[0:np_, 10:11], in0=sm[0:np_, 9:10], scalar1=EPS, scalar2=None, op0=ALU.add), reads=[smb], writes=[smb])
            self.rsqrt_dve(sm[0:np_, 10:11], smb, sm[0:np_, 11:12], smb, sm[0:np_, 12:13], smb)
            T.op("dve", "scalar_tensor_tensor", dict(out=sm[0:np_, 13:14], in0=sm[0:np_, 8:9], scalar=-1.0, in1=sm[0:np_, 11:12], op0=ALU.mult, op1=ALU.mult),
                 reads=[smb], writes=[smb])
            T.op("act", "activation", dict(out=g[0:np_, :], in_=g[0:np_, :], func=AF.Identity, bias=sm[0:np_, 13:14], scale=sm[0:np_, 11:12]),
                 reads=[gb, smb], writes=[gb])
            st_[blk] = (samp, np_, g, gb)

        def part2(blk):
            samp, np_, g, gb = st_.pop(blk)
            T.op("pool" if "ln" in OFFLOAD else "dve", "tensor_tensor", dict(out=g[0:np_, :], in0=g[0:np_, :], in1=self.lnv[0:np_, r, 0, :], op=ALU.mult),
                 reads=[gb, self.b_lnv[r]], writes=[gb])
            T.op("pool" if "ln" in OFFLOAD else "dve", "tensor_tensor", dict(out=g[0:np_, :], in0=g[0:np_, :], in1=self.lnv[0:np_, r, 1, :], op=ALU.add),
                 reads=[gb, self.b_lnv[r]], writes=[gb])
            T.op("act", "activation", dict(out=self.vb[0:np_, blk, :], in_=g[0:np_, :], func=AF.Copy), reads=[gb], writes=[self.b_vb[blk]])
            if sup == 1 and blk == 7 and "b_vp" not in SKIP:
                T.dma("sp", d["vp"][l, :, :], g[:, :], buf=gb, reads=[gb], store=True)
            if samp and "b_vs" not in SKIP:
                T.dma("sp", d["vs"][l, :, :], g[0:NS, :], buf=gb, reads=[gb], store=True)
            if samp:
                self.samp_v = (g, gb)

        for blk in range(nblk + 1):
            if blk < nblk:
                part1(blk)
            if blk >= 1:
                part2(blk - 1)
        if sup == 0:
            self.mod_step()

    def phase_b_bug(self, sup, l):
        T = self.T
        d = self.d
        r = self.pr
        if sup == 1:
            g, gb = self.samp_v
            bk2, bb2 = self.banks.next()
            self.tr_group(bb2, [(bk2[:, m * NS:(m + 1) * NS], g[0:NS, m * 128:(m + 1) * 128], self.ident[0:NS, 0:NS]) for m in range(4)], [gb])
            for m in range(4):
                T.op("dve", "tensor_scalar", dict(
                    out=self.mixS[:, m, :], in0=bk2[:, m * NS:(m + 1) * NS], scalar1=self.col(O_SA + l * 4 + m), scalar2=self.col(O_SB + l * 4 + m),
                    op0=ALU.mult, op1=ALU.add), reads=[bb2, self.b_cols], writes=[self.b_mixS])
        for m in range(4):
            slot, sbuf = self.use_tile(sup, l, "m", T_BUG + m // 2)
            mm_ = m % 2
            for ti, (c0, wd, samp) in enumerate(self.tiles):
                if samp:
                    bk, bb = self.banks.next()
                    for c in range(2):
                        mms = [(bk[:, c * NS:(c + 1) * NS], slot[:, ((mm_ * 2 + c) * 8 + kc) * 128:((mm_ * 2 + c) * 8 + kc + 1) * 128],
                                self.hT[:, kc, c0:c0 + wd], kc == 0, kc == KC - 1, None) for kc in range(KC)]
                        self.mm_group(bb, mms, [sbuf] + [self.b_hT[kc][ti] for kc in range(KC)])
                    s_, sb_ = self.tmp.next()
                    T.op("act", "activation", dict(out=s_[:, 0:NS], in_=bk[:, 0:NS], func=AF.Gelu_apprx_tanh), reads=[bb], writes=[sb_])
                    T.op("act", "activation", dict(out=s_[:, NS:2 * NS], in_=bk[:, NS:2 * NS], func=AF.Copy), reads=[bb], writes=[sb_])
                    T.op("act", "activation", dict(out=s_[:, 2 * NS:3 * NS], in_=s_[:, NS:2 * NS], func=AF.Tanh, scale=0.5), reads=[sb_], writes=[sb_])
                    T.op("dve", "scalar_tensor_tensor", dict(out=s_[:, 2 * NS:3 * NS], in0=s_[:, 2 * NS:3 * NS], scalar=1.0, in1=s_[:, NS:2 * NS],
                                                             op0=ALU.add, op1=ALU.mult), reads=[sb_], writes=[sb_])
                    T.op("dve", "tensor_tensor", dict(out=s_[:, 3 * NS:4 * NS], in0=self.mixS[:, m, :], in1=s_[:, 0:NS], op=ALU.mult),
                         reads=[self.b_mixS, sb_], writes=[sb_])
                    T.op("dve", "scalar_tensor_tensor", dict(out=self.yb[1][:, m, c0:c0 + wd], in0=s_[:, 3 * NS:4 * NS], scalar=0.5, in1=s_[:, 2 * NS:3 * NS],
                                                             op0=ALU.mult, op1=ALU.mult), reads=[sb_], writes=[self.b_yb[1][m][ti]])
                    continue
                b_u, bb_u = self.proj_group(slot, sbuf, (lambda kc: (mm_ * 2 + 0) * 8 + kc), ti, c0, wd)
                b_g, bb_g = self.proj_group(slot, sbuf, (lambda kc: (mm_ * 2 + 1) * 8 + kc), ti, c0, wd)
                u, ub = self.tmp.next()
                T.op("act", "activation", dict(out=u[:, 0:wd], in_=b_u[:, 0:wd], func=AF.Gelu_apprx_tanh), reads=[bb_u], writes=[ub])
                sg, sgb = self.silu2(b_g, bb_g, wd)
                if not samp:
                    bk, bb = self.banks.next()
                    mms = []
                    for bi in range(4):
                        blk = ti * 4 + bi
                        for hh in range(2):
                            h = 2 * m + hh
                            mms.append((bk[hh * 64:(hh + 1) * 64, bi * 128:(bi + 1) * 128], self.vb[:, blk, h * 64:(h + 1) * 64],
                                        self.wsT[:, r, h, :], True, True, (0, hh * 64)))
                    self.mm_group(bb, mms, [self.b_vb[ti * 4 + bi] for bi in range(4)] + [self.b_wsT[r]])
                    t, tb = self.tmp.next()
                    T.op("dve", "tensor_tensor", dict(
                        out=t[:, :].rearrange("p (a b) -> p a b", a=4), in0=bk[:, :].rearrange("p (a b) -> p a b", a=4),
                        in1=self.biasT[:, r, m, :].unsqueeze(1).broadcast_to([128, 4, 128]), op=ALU.add),
                        reads=[bb, self.b_biasT[r]], writes=[tb])
                    T.op("pool" if "bug" in OFFLOAD else "dve", "tensor_tensor", dict(out=t[:, 0:wd], in0=t[:, 0:wd], in1=u[:, 0:wd], op=ALU.mult), reads=[tb, ub], writes=[tb])
                else:
                    t, tb = self.tmp.next()
                    T.op("dve", "tensor_tensor", dict(out=t[:, 0:wd], in0=self.mixS[:, m, :], in1=u[:, 0:wd], op=ALU.mult),
                         reads=[self.b_mixS, ub], writes=[tb])
                T.op("dve", "scalar_tensor_tensor", dict(out=self.yb[1][:, m, c0:c0 + wd], in0=t[:, 0:wd], scalar=0.5, in1=sg[:, 0:wd], op0=ALU.mult, op1=ALU.mult),
                     reads=[tb, sgb], writes=[self.b_yb[1][m][ti]])

    def phase_b_end(self, sup):
        if sup == 0:
            self.mod_step()

    def phase_c(self, sup, l):
        T = self.T
        d = self.d
        slot, sbuf = self.use_tile(sup, l, "m", T_CX)
        if sup == 1:
            c0 = SUPTOK
            bk, bb = self.banks.next()
            for g in range(4):
                mms = [(bk[:, g * NS:(g + 1) * NS], slot[:, (kc * 4 + g) * 128:(kc * 4 + g + 1) * 128], self.hT[:, kc, c0:c0 + NS],
                        kc == 0, kc == KC - 1, None) for kc in range(KC)]
                self.mm_group(bb, mms, [sbuf] + [self.b_hT[kc][2] for kc in range(KC)])
            xc, xcb = self.tmp.next()
            T.op("act", "activation", dict(out=xc[:, 0:4 * NS], in_=bk[:, 0:4 * NS], func=AF.Copy), reads=[bb], writes=[xcb])
            for g, w in enumerate(POOL_WIN):
                T.op("dve", "scalar_tensor_tensor", dict(
                    out=self.pmS[:, g, :], in0=xc[:, g * NS:(g + 1) * NS], scalar=1.0 / w - 1.0, in1=self.SpT[:, l, g, :], op0=ALU.mult, op1=ALU.add),
                    reads=[xcb, self.b_SpT[l]], writes=[self.b_pmS])
        for blk in range(8):
            ti = blk // 4
            c0 = blk * 128
            bk, bb = self.banks.next()
            mms = [(bk[:, :], self.hT[:, kc, c0:c0 + 128], slot[:, kc * 512:(kc + 1) * 512], kc == 0, kc == KC - 1, None) for kc in range(KC)]
            self.mm_group(bb, mms, [sbuf] + [self.b_hT[kc][ti] for kc in range(KC)])
            T.op("act", "activation", dict(out=self.cxb[:, blk, :], in_=bk[:, :], func=AF.Copy), reads=[bb], writes=[self.b_cxb[blk]])
            if sup == 1 and blk == 7:
                bk3, bb3 = self.banks.next()
                mms = [(bk3[0:15, :], self.hT[:, kc, SUPTOK - 15:SUPTOK], slot[:, kc * 512:(kc + 1) * 512], kc == 0, kc == KC - 1, None) for kc in range(KC)]
                self.mm_group(bb3, mms, [sbuf] + [self.b_hT[kc][1] for kc in range(KC)])
                o, ob = self.tmp.next()
                T.op("dve", "tensor_copy", dict(out=o[0:15, :], in_=bk3[0:15, :]), reads=[bb3], writes=[ob])
                T.dma("sp", d["poolp"][l, :, :], o[0:15, :], buf=ob, reads=[ob], store=True)
        if sup == 1:
            bk2, bb2 = self.banks.next()
            self.tr_group(bb2, [(bk2[0:NS, g * 128:(g + 1) * 128], xc[:, g * NS:(g + 1) * NS], self.ident[:, :]) for g in range(4)], [xcb])
            o, ob = self.tmp.next()
            T.op("act", "activation", dict(out=o[0:NS, :], in_=bk2[0:NS, :], func=AF.Copy), reads=[bb2], writes=[ob])
            T.dma("sp", d["pools"][l, :, 14, :], o[0:NS, :], buf=ob, reads=[ob], store=True)

    def phase_c_pool(self, sup, l):
        T = self.T
        d = self.d
        slot, sbuf = self.use_tile(sup, l, "m", T_CG)
        for g in range(4):
            for ti, (c0, wd, samp) in enumerate(self.tiles):
                pm, pmb = self.tbf.next()
                if not samp:
                    bk, bb = self.banks.next()
                    mms = []
                    rd = [self.b_ptb]
                    for bi in range(4):
                        blk = ti * 4 + bi
                        o = bk[:, bi * 128:(bi + 1) * 128]
                        cur = self.cxb[:, blk, g * 128:(g + 1) * 128]
                        rd.append(self.b_cxb[blk])
                        if sup == 0 and blk == 0:
                            mms.append((o, cur, self.ptb[:, g * 4 + 2, :], True, False, None))
                            mms.append((o, cur, self.ptb[:, g * 4 + 3, :], False, True, None))
                        else:
                            if blk == 0:
                                prev = self.cxcarry[64:128, l, g * 128:(g + 1) * 128]
                                rd.append(self.b_cxcarry[l])
                            else:
                                prev = self.cxb[64:128, blk - 1, g * 128:(g + 1) * 128]
                                rd.append(self.b_cxb[blk - 1])
                            mms.append((o, cur, self.ptb[:, g * 4 + 0, :], True, False, None))
                            mms.append((o, prev, self.ptb[64:128, g * 4 + 1, :], False, True, None))
                    self.mm_group(bb, mms, rd)
                    T.op("act", "activation", dict(out=pm[:, 0:wd], in_=bk[:, 0:wd], func=AF.Copy), reads=[bb], writes=[pmb])
                    rhs = pm[:, 0:wd]
                    rhs_b = pmb
                else:
                    rhs = self.pmS[:, g, :]
                    rhs_b = self.b_pmS
                b_g, bb_g = self.proj_group(slot, sbuf, (lambda kc, g=g: g * 8 + kc), ti, c0, wd)
                byc, bbyc = self.banks.next()
                self.mm_group(bbyc, [(byc[:, 0:wd], self.pwb[:, l * 4 + g, :], rhs, True, True, None)], [self.b_pwb, rhs_b])
                sg, sgb = self.silu2(b_g, bb_g, wd)
                t, tb = self.tmp.next()
                T.op("dve", "tensor_scalar", dict(
                    out=t[:, 0:wd], in0=byc[:, 0:wd], scalar1=self.col(O_PB + l * 4 + g), scalar2=self.col(O_PS + l * 4 + g), op0=ALU.add, op1=ALU.mult),
                    reads=[bbyc, self.b_cols], writes=[tb])
                T.op("dve", "scalar_tensor_tensor", dict(out=self.yb[2][:, g, c0:c0 + wd], in0=t[:, 0:wd], scalar=0.5, in1=sg[:, 0:wd], op0=ALU.mult, op1=ALU.mult),
                     reads=[tb, sgb], writes=[self.b_yb[2][g][ti]])
        if sup == 0:
            T.op("dve", "tensor_copy", dict(out=self.cxcarry[:, l, :], in_=self.cxb[:, 7, :]), reads=[self.b_cxb[7]], writes=[self.b_cxcarry[l]])

    def merge(self, sup, l):
        T = self.T
        for j in range(KC):
            for ti, (c0, wd, samp) in enumerate(self.tiles):
                acc, accb = self.tmp.next()
                for br in range(3):
                    base = j * 36 + br * 12
                    bk, bb = self.banks.next()
                    mms = []
                    rd = []
                    for kc in range(KC):
                        idx = base + kc
                        slot, sbuf = self.use_tile(sup, l, "m", T_MRG + idx // 32)
                        p = idx % 32
                        mms.append((bk[:, 0:wd], slot[:, p * 128:(p + 1) * 128], self.hT[:, kc, c0:c0 + wd], kc == 0, kc == KC - 1, None))
                        if sbuf not in rd:
                            rd.append(sbuf)
                    self.mm_group(bb, mms, rd + [self.b_hT[kc][ti] for kc in range(KC)])
                    sgt, sgtb = self.tmp.next()
                    T.op("act", "activation", dict(out=sgt[:, 0:wd], in_=bk[:, 0:wd], func=AF.Tanh, scale=0.5), reads=[bb], writes=[sgtb])
                    bp, bbp = self.banks.next()
                    mms = []
                    rd = []
                    for k4 in range(4):
                        idx = base + 8 + k4
                        slot, sbuf = self.use_tile(sup, l, "m", T_MRG + idx // 32)
                        p = idx % 32
                        mms.append((bp[:, 0:wd], slot[:, p * 128:(p + 1) * 128], self.yb[br][:, k4, c0:c0 + wd], k4 == 0, k4 == 3, None))
                        if sbuf not in rd:
                            rd.append(sbuf)
                    self.mm_group(bbp, mms, rd + [self.b_yb[br][k4][ti] for k4 in range(4)])
                    if br == 0:
                        T.op("dve", "scalar_tensor_tensor", dict(out=acc[:, 0:wd], in0=sgt[:, 0:wd], scalar=1.0, in1=bp[:, 0:wd], op0=ALU.add, op1=ALU.mult),
                             reads=[bbp, sgtb], writes=[accb])
                    else:
                        T.op("dve", "scalar_tensor_tensor", dict(out=sgt[:, 0:wd], in0=sgt[:, 0:wd], scalar=1.0, in1=bp[:, 0:wd], op0=ALU.add, op1=ALU.mult),
                             reads=[bbp, sgtb], writes=[sgtb])
                        if br == 1:
                            T.op("pool" if "merge" in OFFLOAD else "dve", "tensor_tensor", dict(out=acc[:, 0:wd], in0=acc[:, 0:wd], in1=sgt[:, 0:wd], op=ALU.add),
                                 reads=[accb, sgtb], writes=[accb])
                        else:
                            T.op("pool" if "merge" in OFFLOAD else "dve", "tensor_tensor", dict(out=self.mg[:, j, c0:c0 + wd], in0=acc[:, 0:wd], in1=sgt[:, 0:wd], op=ALU.add),
                                 reads=[accb, sgtb], writes=[self.b_mg[j][ti]])

    def wout(self, sup, l):
        T = self.T
        for j in range(KC):
            slot, sbuf = self.use_tile(sup, l, "m", T_WO + j // 4)
            jj = j % 4
            for ti, (c0, wd, samp) in enumerate(self.tiles):
                bk, bb = self.banks.next()
                mms = [(bk[:, 0:wd], slot[:, (jj * 8 + kc) * 128:(jj * 8 + kc + 1) * 128], self.mg[:, kc, c0:c0 + wd], kc == 0, kc == KC - 1, None)
                       for kc in range(KC)]
                self.mm_group(bb, mms, [sbuf] + [self.b_mg[kc][ti] for kc in range(KC)])
                if not samp:
                    T.op("dve", "scalar_tensor_tensor", dict(
                        out=self.xT[:, j, c0:c0 + wd], in0=bk[:, 0:wd], scalar=self.modT[:, l, 16 + j, 0:1], in1=self.xT[:, j, c0:c0 + wd],
                        op0=ALU.mult, op1=ALU.add), reads=[bb, self.b_modT[l], self.b_xT[j][ti]], writes=[self.b_xT[j][ti]])
                else:
                    t, tb = self.tmp.next()
                    T.op("dve", "tensor_tensor", dict(out=t[:, 0:wd], in0=bk[:, 0:wd], in1=self.modT[:, l, 16 + j, 1:NS + 1], op=ALU.mult),
                         reads=[bb, self.b_modT[l]], writes=[tb])
                    T.op("dve", "tensor_tensor", dict(out=self.xT[:, j, c0:c0 + wd], in0=self.xT[:, j, c0:c0 + wd], in1=t[:, 0:wd], op=ALU.add),
                         reads=[tb, self.b_xT[j][ti]], writes=[self.b_xT[j][ti]])

    def final(self, sup):
        T = self.T
        d = self.d
        for ti, (c0, wd, samp) in enumerate(self.tiles):
            rt, rtb = self.rms_rstd(ti, c0, wd)
            for kc in range(KC):
                T.op("dve", "scalar_tensor_tensor", dict(
                    out=self.xT[:, kc, c0:c0 + wd], in0=self.xT[:, kc, c0:c0 + wd], scalar=self.col(O_FG + kc), in1=rt[:, 0:wd],
                    op0=ALU.mult, op1=ALU.mult), reads=[self.b_xT[kc][ti], self.b_cols, rtb], writes=[self.b_xT[kc][ti]])
            if not samp:
                for bi in range(4):
                    blk = ti * 4 + bi
                    r0 = (sup * 8 + blk) * 128
                    for half in range(2):
                        bk, bb = self.banks.next()
                        self.tr_group(bb, [(bk[:, i * 128:(i + 1) * 128], self.xT[:, half * 4 + i, blk * 128:(blk + 1) * 128], self.ident[:, :])
                                           for i in range(4)], [self.b_xT[half * 4 + i][ti] for i in range(4)])
                        o, ob = self.tmp.next()
                        if half == 0:
                            T.op("act", "activation", dict(out=o[:, :], in_=bk[:, :], func=AF.Copy), reads=[bb], writes=[ob])
                        else:
                            T.op("dve", "tensor_copy", dict(out=o[:, :], in_=bk[:, :]), reads=[bb], writes=[ob])
                        T.dma("sp", d["yp"][r0:r0 + 128, half * 512:(half + 1) * 512], o[:, :], buf=ob, reads=[ob], store=True)
            else:
                st = self.st32
                for half in range(2):
                    bk, bb = self.banks.next()
                    self.tr_group(bb, [(bk[0:NS, i * 128:(i + 1) * 128], self.xT[:, half * 4 + i, c0:c0 + NS], self.ident[:, :]) for i in range(4)],
                                  [self.b_xT[half * 4 + i][ti] for i in range(4)])
                    T.op("dve", "tensor_copy", dict(out=st[0:NS, half * 512:(half + 1) * 512], in_=bk[0:NS, :]), reads=[bb], writes=[self.b_st32])
                T.dma("sp", d["ys"][:, :], st[0:NS, 0:D], buf=self.b_st32, reads=[self.b_st32], store=True)


def _tiles_from_blocks(blocks):
    n = blocks.shape[0] // 32
    return np.ascontiguousarray(blocks.reshape(n, 32, 128, 128).transpose(0, 2, 1, 3).reshape(n, 128, 4096))


def _build_streams(w_ada, w_in, w_branch, w_out):
    wa = np.empty((L, NT_ADA, 128, 4096), np.float32)
    wm = np.empty((L, NT_MAIN, 128, 4096), np.float32)
    CO = dict(ab=0, ac=4, ah=8, ag=12, bu=16, bv=20, bg=24, cx=28, cg=32, ma=36, mb=44, mc=52)
    for l in range(L):
        a4 = w_ada[l].reshape(8, 128, 24, 128).transpose(0, 2, 1, 3)
        bl = [a4[kc, 4 * t + c] for t in range(NT_ADA) for kc in range(8) for c in range(4)]
        wa[l] = _tiles_from_blocks(np.stack(bl))
        i4 = w_in[l].reshape(8, 128, 60, 128).transpose(0, 2, 1, 3)
        b4 = w_branch[l].reshape(3, 4, 128, 8, 128).transpose(0, 1, 3, 2, 4)
        o4 = w_out[l].reshape(8, 128, 8, 128).transpose(0, 2, 1, 3)
        bl = []
        for m in range(4):
            bl += [i4[kc, CO[c] + m] for kc in range(8) for c in ("ab", "ac", "ah", "ag")]
        bl += [i4[kc, CO["bv"] + c] for kc in range(8) for c in range(4)]
        bl += [i4[kc, CO["cx"] + c] for kc in range(8) for c in range(4)]
        for q in range(2):
            for mm_ in range(2):
                for c in ("bu", "bg"):
                    bl += [i4[kc, CO[c] + 2 * q + mm_] for kc in range(8)]
        bl += [i4[kc, CO["cg"] + g] for g in range(4) for kc in range(8)]
        for j in range(8):
            for br, c in enumerate(("ma", "mb", "mc")):
                bl += [i4[kc, CO[c] + j] for kc in range(8)]
                bl += [b4[br, k4, j] for k4 in range(4)]
        for j in range(8):
            bl += [o4[kc, j] for kc in range(8)]
        assert len(bl) == NT_MAIN * 32
        wm[l] = _tiles_from_blocks(np.stack(bl))
    return wa, wm


def _chunk_cols(v, n):
    return np.ascontiguousarray(np.asarray(v, np.float32).reshape(n, 128).T)


def _pool_tables():
    import ml_dtypes
    pt = np.zeros((128, 16, 128), np.float64)
    s = np.arange(128)[:, None]
    t = np.arange(128)[None, :]
    for g, w in enumerate(POOL_WIN):
        diag = ((s <= t) & (s > t - w)) / float(w) - (s == t)
        off = ((s - 128) > (t - w)) / float(w)
        cnt = np.minimum(w, t + 1).astype(np.float64)
        first = ((s <= t) & (s > t - w)) / cnt - (s == t)
        hi = first.astype(np.float32).astype(ml_dtypes.bfloat16).astype(np.float64)
        lo = (first - hi).astype(np.float32).astype(ml_dtypes.bfloat16).astype(np.float64)
        pt[:, g * 4 + 0] = diag
        pt[:, g * 4 + 1] = off
        pt[:, g * 4 + 2] = hi
        pt[:, g * 4 + 3] = lo
    return pt.astype(np.float32)


_NC_CACHE = {}


def _get_nc(nl):
    if nl not in _NC_CACHE:
        _NC_CACHE[nl] = Prog(nl).build()
    return _NC_CACHE[nl]


def kernel(x_prompt, x_sample, c_prompt, c_sample, state_conv, state_pool, w_ada, b_ada, norm_g,
           w_in, conv_w, conv_b, lnv_g, lnv_b, sgu_w, sgu_b, pool_w, pool_b, pool_scale,
           w_branch, w_out, final_g):
    f = lambda a: np.ascontiguousarray(np.asarray(a, dtype=np.float32))
    x_prompt, x_sample, c_prompt, c_sample = f(x_prompt), f(x_sample), f(c_prompt), f(c_sample)
    state_conv, state_pool = f(state_conv), f(state_pool)
    w_ada, w_in, w_branch, w_out = f(w_ada), f(w_in), f(w_branch), f(w_out)
    sgu_w, sgu_b, pool_w = f(sgu_w), f(sgu_b), f(pool_w)
    nl = int(os.environ.get("MK_NL", L))
    wa, wm = _build_streams(w_ada, w_in, w_branch, w_out)
    cols = np.zeros((128, NCOLS), np.float32)
    for l in range(L):
        cols[:, O_BA + l * 24:O_BA + (l + 1) * 24] = _chunk_cols(b_ada[l], 24)
        cols[:, O_NG + l * 8:O_NG + (l + 1) * 8] = _chunk_cols(norm_g[l], 8)
        for k in range(3):
            cols[:, O_CW + (l * 3 + k) * 4:O_CW + (l * 3 + k + 1) * 4] = _chunk_cols(conv_w[l, k], 4)
        cols[:, O_CB + l * 4:O_CB + (l + 1) * 4] = _chunk_cols(conv_b[l], 4)
        cols[:, O_PB + l * 4:O_PB + (l + 1) * 4] = _chunk_cols(pool_b[l], 4)
        cols[:, O_PS + l * 4:O_PS + (l + 1) * 4] = _chunk_cols(pool_scale[l], 4)
        cols[:, O_SA + l * 4:O_SA + (l + 1) * 4] = _chunk_cols(np.repeat(sgu_w[l, :, 0, 0], 64), 4)
        cols[:, O_SB + l * 4:O_SB + (l + 1) * 4] = _chunk_cols(np.repeat(sgu_b[l, :, 0], 64), 4)
    cols[:, O_FG:O_FG + 8] = _chunk_cols(final_g, 8)
    cols[:, O_EPS] = EPS
    lnv = np.ascontiguousarray(np.stack([f(lnv_g), f(lnv_b)], axis=1))
    biasT = np.ascontiguousarray(np.repeat(sgu_b.reshape(L, 4, 2, 1, 128), 64, axis=3).reshape(L, 4, 128, 128).transpose(2, 0, 1, 3))
    poolw = np.ascontiguousarray(pool_w.reshape(L * 4, 128, 128).transpose(1, 0, 2))
    ident = np.eye(128, dtype=np.float32)
    mask = np.triu(np.ones((128, 128), np.float32))
    pt = _pool_tables()
    shared = dict(wa=wa, wm=wm, cols=cols, lnv=lnv, sguw=sgu_w, biasT=biasT, poolw=poolw, ident=ident, mask=mask, pt=pt)
    in_maps = []
    for c in range(NCORE):
        s0, s1 = c * NS, (c + 1) * NS
        m = dict(shared)
        m["xp"] = x_prompt[c]
        m["xs"] = np.ascontiguousarray(x_sample[s0:s1, 0, :])
        m["cc"] = np.ascontiguousarray(np.concatenate([c_prompt[c:c + 1], c_sample[s0:s1]], axis=0))
        m["sconv"] = np.ascontiguousarray(state_conv[:, s0:s1])
        m["spool"] = np.ascontiguousarray(state_pool[:, s0:s1])
        in_maps.append(m)
    nc = _get_nc(nl)
    res = run_bass_kernel_spmd(nc, in_maps, core_ids=list(range(NCORE)))
    R = res.results
    y_prompt = np.stack([R[c]["yp"] for c in range(NCORE)], axis=0)
    y_sample = np.concatenate([R[c]["ys"] for c in range(NCORE)], axis=0)[:, None, :]
    conv_p = np.stack([R[c]["convp"] for c in range(NCORE)], axis=1)
    conv_s = np.concatenate([R[c]["convs"] for c in range(NCORE)], axis=1)
    pool_p = np.stack([R[c]["poolp"] for c in range(NCORE)], axis=1)
    pool_s = np.concatenate([R[c]["pools"] for c in range(NCORE)], axis=1)
    v_p = np.stack([R[c]["vp"] for c in range(NCORE)], axis=1)
    v_s = np.concatenate([R[c]["vs"] for c in range(NCORE)], axis=1)[:, :, None, :]
    outs = (y_prompt, y_sample, conv_p, conv_s, pool_p, pool_s, v_p, v_s)
    return tuple(np.ascontiguousarray(o.astype(np.float32)) for o in outs)
```
